# Optimizing a Trainium2 kernel written in Bass

```python
import math
import jax
import jax.numpy as jnp
from jax import lax
import numpy as np

D_MODEL = 1024
BATCH = 32
SEQ = 2048
DEPTH = 1
DEC_BATCH = 32
DEC_SEQ = 16
PAST_LEN = 2048

CHUNK = 64
QBLOCK = 128
ROPE_THETA = 10000.0
EPS = 1e-6
NEG_INF = -1e30

H_A = 4
DK_A = 64
DV_A = 2 * DK_A
W_A = H_A * DV_A
H_B = 4
D_NOPE = 128
D_ROPE = 64
DV_B = 128
Q_LORA = 512
KV_LORA = 256
W_B = H_B * DV_B

COLS = (H_A * 2 * DK_A,
        H_A * 2 * DK_A,
        H_A * DV_A,
        W_A,
        Q_LORA,
        KV_LORA,
        D_ROPE,
        W_B,
        D_MODEL,
        D_MODEL)
SPLIT_POINTS = tuple(int(c) for c in np.cumsum(COLS)[:-1])
D_IN = int(sum(COLS))
DIFF_SCALE = DK_A ** -0.5
MLA_SCALE = (D_NOPE + D_ROPE) ** -0.5

kernel_name = 'diff_mla_gated_streaming_encoder'


def rms_norm(x, g):
    xf = x.astype(jnp.float32)
    y = xf * lax.rsqrt(jnp.mean(xf * xf, axis=-1, keepdims=True) + EPS)
    return (y * g.astype(jnp.float32)).astype(x.dtype)


def rope(x, pos):
    d = x.shape[-1]
    half = d // 2
    inv = ROPE_THETA ** (-jnp.arange(half, dtype=jnp.float32) * 2.0 / d)
    ang = pos.astype(jnp.float32)[:, None] * inv[None, :]
    shape = (pos.shape[0],) + (1,) * (x.ndim - 3) + (half,)
    cos = jnp.cos(ang).reshape(shape)
    sin = jnp.sin(ang).reshape(shape)
    xf = x.astype(jnp.float32)
    x1, x2 = xf[..., :half], xf[..., half:]
    return jnp.concatenate([x1 * cos - x2 * sin, x2 * cos + x1 * sin], axis=-1).astype(x.dtype)


def chunk_mask(q_pos, k_pos):
    return (k_pos[None, :] // CHUNK) <= (q_pos[:, None] // CHUNK)


def masked_softmax(scores, mask):
    return jax.nn.softmax(jnp.where(mask[None, None], scores, NEG_INF), axis=-1)


def diff_attention(q1, q2, q_pos, k1, k2, v, k_pos, lam):
    mask = chunk_mask(q_pos, k_pos)
    s1 = jnp.einsum('bqhd,bkhd->bhqk', q1, k1).astype(jnp.float32) * DIFF_SCALE
    s2 = jnp.einsum('bqhd,bkhd->bhqk', q2, k2).astype(jnp.float32) * DIFF_SCALE
    attn = masked_softmax(s1, mask) - lam * masked_softmax(s2, mask)
    return jnp.einsum('bhqk,bkhd->bqhd', attn.astype(v.dtype), v)


def mla_attention(q_lat, q_pe, q_pos, ckv, kpe, k_pos):
    mask = chunk_mask(q_pos, k_pos)
    s = (jnp.einsum('bqhc,bkc->bhqk', q_lat, ckv)
         + jnp.einsum('bqhr,bkr->bhqk', q_pe, kpe)).astype(jnp.float32) * MLA_SCALE
    p = masked_softmax(s, mask)
    return jnp.einsum('bhqk,bkc->bqhc', p.astype(ckv.dtype), ckv)


def sweep_query_blocks(fn, qs, q_pos):
    b, t = qs[0].shape[:2]
    nb = t // QBLOCK
    blocks = tuple(jnp.moveaxis(a.reshape((b, nb, QBLOCK) + a.shape[2:]), 1, 0) for a in qs)
    out = lax.map(lambda args: fn(*args[0], args[1]), (blocks, q_pos.reshape(nb, QBLOCK)))
    return jnp.moveaxis(out, 0, 1).reshape((b, t) + out.shape[3:])


def mixer_layer(x, pos, past, lam_init, w_in, w_uq, w_uk, w_uv, w_oa, w_ob, w_out,
                lq1, lk1, lq2, lk2, g_in, g_qa, g_kva, g_sub):
    b, t, _ = x.shape
    h = rms_norm(x, g_in)
    proj = jnp.einsum('btd,de->bte', h, w_in)
    qa, ka, va, z_a, qd, ckv_in, kpe_in, z_b, m_a, m_b = jnp.split(proj, SPLIT_POINTS, axis=-1)

    qa = rope(qa.reshape(b, t, H_A, 2, DK_A), pos)
    k_rows = rope(ka.reshape(b, t, H_A, 2, DK_A), pos).reshape(b, t, H_A, 2 * DK_A)
    v_rows = va.reshape(b, t, H_A, DV_A)

    q = jnp.einsum('btc,che->bthe', rms_norm(qd, g_qa), w_uq)
    q_lat = jnp.einsum('bthd,chd->bthc', q[..., :D_NOPE], w_uk)
    q_pe = rope(q[..., D_NOPE:], pos)
    ckv_rows = rms_norm(ckv_in, g_kva)
    kpe_rows = rope(kpe_in, pos)

    lam = (jnp.exp(jnp.sum(lq1.astype(jnp.float32) * lk1.astype(jnp.float32)))
           - jnp.exp(jnp.sum(lq2.astype(jnp.float32) * lk2.astype(jnp.float32))) + lam_init)

    if past is None:
        k_all, v_all, ckv_all, kpe_all, k_pos = k_rows, v_rows, ckv_rows, kpe_rows, pos
    else:
        ck, cv, cckv, ckpe = past
        p_len = ck.shape[1]
        k_all = jnp.concatenate([ck.astype(k_rows.dtype), k_rows], axis=1)
        v_all = jnp.concatenate([cv.astype(v_rows.dtype), v_rows], axis=1)
        ckv_all = jnp.concatenate([cckv.astype(ckv_rows.dtype), ckv_rows], axis=1)
        kpe_all = jnp.concatenate([ckpe.astype(kpe_rows.dtype), kpe_rows], axis=1)
        k_pos = jnp.concatenate([jnp.arange(p_len, dtype=jnp.int32), pos])
    kk = k_all.reshape(k_all.shape[:3] + (2, DK_A))
    k1, k2 = kk[..., 0, :], kk[..., 1, :]
    q1, q2 = qa[..., 0, :], qa[..., 1, :]

    diff_fn = lambda a1, a2, qp: diff_attention(a1, a2, qp, k1, k2, v_all, k_pos, lam)
    mla_fn = lambda a1, a2, qp: mla_attention(a1, a2, qp, ckv_all, kpe_all, k_pos)
    if past is None:
        o_a = sweep_query_blocks(diff_fn, (q1, q2), pos)
        o_lat = sweep_query_blocks(mla_fn, (q_lat, q_pe), pos)
    else:
        o_a = diff_fn(q1, q2, pos)
        o_lat = mla_fn(q_lat, q_pe, pos)

    o_a = (rms_norm(o_a, g_sub) * (1.0 - lam_init)).reshape(b, t, W_A) * jax.nn.silu(z_a)
    y_a = jnp.einsum('btw,wd->btd', o_a, w_oa)
    o_b = jnp.einsum('bthc,che->bthe', o_lat, w_uv).reshape(b, t, W_B) * jax.nn.silu(z_b)
    y_b = jnp.einsum('btw,wd->btd', o_b, w_ob)

    merged = jax.nn.sigmoid(m_a) * y_a + jax.nn.sigmoid(m_b) * y_b
    out = x + jnp.einsum('btd,de->bte', merged, w_out)
    return out, (k_rows, v_rows, ckv_rows, kpe_rows)


def setup_inputs(seed: int = 0) -> dict:
    key = jax.random.key(seed)
    ks = jax.random.split(key, 24)

    def nrm(k, shape, scale):
        return jax.random.normal(k, shape, jnp.float32) * scale

    def gain(k, shape):
        return 1.0 + 0.02 * jax.random.normal(k, shape, jnp.float32)

    return {
        'x_prompt': nrm(ks[0], (BATCH, SEQ, D_MODEL), 1.0),
        'x_sample': nrm(ks[1], (DEC_BATCH, DEC_SEQ, D_MODEL), 1.0),
        'cache_diff_k': nrm(ks[2], (DEPTH, DEC_BATCH, PAST_LEN, H_A, 2 * DK_A), 1.0),
        'cache_diff_v': nrm(ks[3], (DEPTH, DEC_BATCH, PAST_LEN, H_A, DV_A), 1.0),
        'cache_mla_ckv': nrm(ks[4], (DEPTH, DEC_BATCH, PAST_LEN, KV_LORA), 1.0),
        'cache_mla_kpe': nrm(ks[5], (DEPTH, DEC_BATCH, PAST_LEN, D_ROPE), 1.0),
        'w_in': nrm(ks[6], (DEPTH, D_MODEL, D_IN), D_MODEL ** -0.5),
        'w_uq': nrm(ks[7], (DEPTH, Q_LORA, H_B, D_NOPE + D_ROPE), Q_LORA ** -0.5),
        'w_uk': nrm(ks[8], (DEPTH, KV_LORA, H_B, D_NOPE), KV_LORA ** -0.5),
        'w_uv': nrm(ks[9], (DEPTH, KV_LORA, H_B, DV_B), KV_LORA ** -0.5),
        'w_oa': nrm(ks[10], (DEPTH, W_A, D_MODEL), W_A ** -0.5),
        'w_ob': nrm(ks[11], (DEPTH, W_B, D_MODEL), W_B ** -0.5),
        'w_out': nrm(ks[12], (DEPTH, D_MODEL, D_MODEL), D_MODEL ** -0.5),
        'lambda_q1': nrm(ks[13], (DEPTH, DK_A), 0.1),
        'lambda_k1': nrm(ks[14], (DEPTH, DK_A), 0.1),
        'lambda_q2': nrm(ks[15], (DEPTH, DK_A), 0.1),
        'lambda_k2': nrm(ks[16], (DEPTH, DK_A), 0.1),
        'norm_in': gain(ks[17], (DEPTH, D_MODEL)),
        'norm_qa': gain(ks[18], (DEPTH, Q_LORA)),
        'norm_kva': gain(ks[19], (DEPTH, KV_LORA)),
        'norm_subln': gain(ks[20], (DEPTH, DV_A)),
        'norm_final': gain(ks[21], (D_MODEL,)),
    }


def reference(x_prompt, x_sample, cache_diff_k, cache_diff_v, cache_mla_ckv, cache_mla_kpe,
              w_in, w_uq, w_uk, w_uv, w_oa, w_ob, w_out,
              lambda_q1, lambda_k1, lambda_q2, lambda_k2,
              norm_in, norm_qa, norm_kva, norm_subln, norm_final):
    pos_p = jnp.arange(x_prompt.shape[1], dtype=jnp.int32)
    past_len = cache_diff_k.shape[2]
    pos_s = past_len + jnp.arange(x_sample.shape[1], dtype=jnp.int32)

    hp, hs = x_prompt, x_sample
    st_p = ([], [], [], [])
    st_s = ([], [], [], [])
    for l in range(DEPTH):
        lam_init = 0.8 - 0.6 * math.exp(-0.3 * l)
        w = (w_in[l], w_uq[l], w_uk[l], w_uv[l], w_oa[l], w_ob[l], w_out[l],
             lambda_q1[l], lambda_k1[l], lambda_q2[l], lambda_k2[l],
             norm_in[l], norm_qa[l], norm_kva[l], norm_subln[l])
        hp, new_p = mixer_layer(hp, pos_p, None, lam_init, *w)
        past = (cache_diff_k[l], cache_diff_v[l], cache_mla_ckv[l], cache_mla_kpe[l])
        hs, new_s = mixer_layer(hs, pos_s, past, lam_init, *w)
        for lst, a in zip(st_p, new_p):
            lst.append(a)
        for lst, a in zip(st_s, new_s):
            lst.append(a)

    y_prompt = rms_norm(hp, norm_final)
    y_sample = rms_norm(hs, norm_final)
    new_diff_k_prompt = jnp.stack(st_p[0])
    new_diff_v_prompt = jnp.stack(st_p[1])
    new_ckv_prompt = jnp.stack(st_p[2])
    new_kpe_prompt = jnp.stack(st_p[3])
    new_diff_k_sample = jnp.stack(st_s[0])
    new_diff_v_sample = jnp.stack(st_s[1])
    new_ckv_sample = jnp.stack(st_s[2])
    new_kpe_sample = jnp.stack(st_s[3])
    return (y_prompt, y_sample, new_diff_k_prompt, new_diff_v_prompt, new_ckv_prompt, new_kpe_prompt,
            new_diff_k_sample, new_diff_v_sample, new_ckv_sample, new_kpe_sample)
```

```python
import numpy as np
import concourse.bass as bass
import concourse.mybir as mybir
from concourse.bass_utils import run_bass_kernel_spmd

F32 = mybir.dt.float32
BF16 = mybir.dt.bfloat16
AF = mybir.ActivationFunctionType
ALU = mybir.AluOpType

ENGS = ("pe", "act", "dve", "pool", "sp")


class Op:
    __slots__ = ("idx", "eng", "fn", "deps", "dma", "semkey", "sig", "val", "eidx")

    def __init__(self, idx, eng, fn, dma, semkey):
        self.idx = idx
        self.eng = eng
        self.fn = fn
        self.deps = set()
        self.dma = dma
        self.semkey = semkey
        self.sig = False
        self.val = 0
        self.eidx = 0


class Prog:
    def __init__(self, nc):
        self.nc = nc
        self.ops = []
        self.eng_ops = {e: [] for e in ENGS}
        self.last_w = {}
        self.rd_eng = {}
        self.rd_dma = {}
        self.last_x = {}

    def op(self, eng, fn, reads=(), writes=(), dma=False, semkey=None):
        o = Op(len(self.ops), eng, fn, dma, semkey)
        xs = [r for r in list(reads) + list(writes) if isinstance(r, tuple) and r[0] == "ps"]
        if xs:
            reads = [r for r in reads if r not in xs]
            writes = [r for r in writes if r not in xs]
            for r in set(xs):
                la = self.last_x.get(r)
                if la is not None and self.ops[la].eng != eng:
                    o.deps.add(la)
                self.last_x[r] = o.idx
        for r in reads:
            w = self.last_w.get(r)
            if w is not None:
                o.deps.add(w)
        for w_ in writes:
            w = self.last_w.get(w_)
            rdrs = self.rd_eng.get(w_, {})
            covered = (w is not None and not dma and not self.ops[w].dma and self.ops[w].eng == eng
                       and any(e2 != eng for e2 in rdrs))
            if w is not None and not covered:
                o.deps.add(w)
            for i in rdrs.values():
                o.deps.add(i)
            for i in self.rd_dma.get(w_, ()):
                o.deps.add(i)
        for r in reads:
            if dma:
                self.rd_dma.setdefault(r, []).append(o.idx)
            else:
                self.rd_eng.setdefault(r, {})[eng] = o.idx
        for w_ in writes:
            self.last_w[w_] = o.idx
            self.rd_eng[w_] = {}
            self.rd_dma[w_] = []
        o.deps.discard(o.idx)
        o.eidx = len(self.eng_ops[eng])
        self.ops.append(o)
        self.eng_ops[eng].append(o)
        return o

    def barrier_all(self):
        lasts = []
        for e in ENGS:
            lst = [o for o in self.eng_ops[e] if not o.dma and o.fn is not None]
            if lst:
                lasts.append(lst[-1].idx)
        lastd = {}
        for o in self.ops:
            if o.dma:
                lastd[o.semkey] = o.idx
        self._pending_barrier = set(lasts) | set(lastd.values())
        for e in ENGS:
            eo = e
            o = self.op(eo, None, (), ())
            o.deps |= {d for d in self._pending_barrier if d != o.idx}

    def finalize(self):
        ops = self.ops
        for o in ops:
            for d in o.deps:
                od = ops[d]
                if od.dma:
                    od.sig = True
                elif od.eng == "pe" and o.eng == "pe" and not o.dma:
                    continue
                else:
                    od.sig = True
        cnt = {e: 0 for e in ENGS}
        dcnt = {}
        for o in ops:
            if o.dma:
                dcnt[o.semkey] = dcnt.get(o.semkey, 0) + 16
                o.val = dcnt[o.semkey]
            else:
                if o.sig:
                    cnt[o.eng] += 1
                o.val = cnt[o.eng]
        self.dma_keys = list(dcnt.keys())
        self.final_dma = dict(dcnt)

    def emit(self, sems, dsems):
        nc = self.nc
        ops = self.ops
        engobj = {"pe": nc.tensor, "act": nc.scalar, "dve": nc.vector, "pool": nc.gpsimd, "sp": nc.sync}

        def run(e, eng):
            seen = {}
            for o in self.eng_ops[e]:
                need = {}
                for d in o.deps:
                    od = ops[d]
                    if od.dma:
                        s = ("d", od.semkey)
                    else:
                        if od.eng == "pe" and e == "pe" and not o.dma:
                            continue
                        s = ("e", od.eng)
                    if od.val > need.get(s, 0):
                        need[s] = od.val
                todo = []
                for s, v in need.items():
                    if seen.get(s, 0) >= v:
                        continue
                    seen[s] = v
                    todo.append((dsems[s[1]] if s[0] == "d" else sems[s[1]], v))
                attach = None
                if todo and o.fn is not None and not o.dma:
                    attach = todo.pop()
                for sem, v in todo:
                    eng.wait_ge(sem, v)
                if o.fn is None:
                    continue
                ins = o.fn(eng)
                if attach is not None:
                    ins.wait_op(attach[0], attach[1], "sem-ge")
                if o.dma:
                    ins.then_inc(dsems[o.semkey], 16)
                elif o.sig:
                    ins.then_inc(sems[e], 1)
            if e == "sp":
                for k, v in self.final_dma.items():
                    eng.wait_ge(dsems[k], v)

        return run


D = 1024
DIN = 5440
NKT = 17
NKEY = NKT * 128
LAM_INIT = 0.2
OSC = 1.0 - LAM_INIT
EPS = 1e-6
DIFF_SCALE = 0.125
MLA_SCALE = 192.0 ** -0.5
PAST = 2048
DEC = 16


class Carver:
    def __init__(self, big, off, limit):
        self.big, self.off, self.limit, self.base = big, off, limit, off

    def take(self, cols, dt):
        nb = cols * (4 if dt == F32 else 2)
        nb = (nb + 63) // 64 * 64
        assert self.off + nb <= self.limit, ("SBUF carve overflow", self.off + nb, self.limit, self.base)
        a = self.big[:, self.off // 4:(self.off + nb) // 4]
        self.off += nb
        if dt != F32:
            a = a.bitcast(dt)
        return a[:, 0:cols]


def rope_tables():
    half = 32
    inv = (10000.0 ** (-np.arange(half, dtype=np.float32) * 2.0 / 64)).astype(np.float32)
    pos = np.arange(NKT * 128, dtype=np.float32)
    ang = (pos[:, None] * inv[None, :]).astype(np.float32)
    c = np.cos(ang).astype(np.float32).reshape(NKT, 128, half).transpose(1, 0, 2)
    s = np.sin(ang).astype(np.float32).reshape(NKT, 128, half).transpose(1, 0, 2)
    return np.ascontiguousarray(np.stack([c, s], axis=0))


def build_program(nseq=4, nblk=16, with_sample=True, sbuf_kb=223, stop=99):
    from contextlib import ExitStack
    T = nblk * 128
    nc = bass.Bass("TRN2", target_bir_lowering=False, dynamic_dma_scratch_size=256)

    def din(name, shape, dt=F32):
        return nc.dram_tensor(name, list(shape), dt, kind="ExternalInput").ap()

    def dout(name, shape, dt=F32):
        return nc.dram_tensor(name, list(shape), dt, kind="ExternalOutput").ap()

    xp = din("xp", [nseq, T, D])
    w_in = din("w_in", [D, DIN]); w_uq = din("w_uq", [512, 768]); w_uk = din("w_uk", [256, 512])
    w_uv = din("w_uv", [256, 512]); w_oa = din("w_oa", [512, D]); w_ob = din("w_ob", [512, D])
    w_out = din("w_out", [D, D]); lam4 = din("lam4", [4, 64])
    g_inT = din("g_inT", [128, 8]); g_qaT = din("g_qaT", [128, 4]); g_kva = din("g_kva", [1, 256])
    g_sub = din("g_sub", [1, 128]); g_fin = din("g_fin", [1, D])
    ident_d = din("ident", [128, 128], BF16); rope_d = din("rope", [2, 128, NKT, 32])
    yp = dout("yp", [nseq, T, D]); kp = dout("kp", [nseq, T, 512]); vp = dout("vp", [nseq, T, 512])
    cpo = dout("cpo", [nseq, T, 256]); ppo = dout("ppo", [nseq, T, 64])
    if with_sample:
        xs = din("xs", [nseq, DEC, D]); ck = din("ck", [nseq, PAST, 512]); cv = din("cv", [nseq, PAST, 512])
        cckv = din("cc", [nseq, PAST, 256]); cpe = din("cpe", [nseq, PAST, 64])
        ys = dout("ys", [nseq, DEC, D]); kso = dout("ks", [nseq, DEC, 512]); vso = dout("vs", [nseq, DEC, 512])
        cso = dout("cs", [nseq, DEC, 256]); pso = dout("ps", [nseq, DEC, 64])
    wkv_scr = nc.dram_tensor("wkv_scr", [8, 128, 1344], BF16, kind="Internal").ap()

    es = ExitStack()
    with es:
        big = es.enter_context(nc.sbuf_tensor("big", [128, sbuf_kb * 256], F32))
        psb = [es.enter_context(nc.psum_tensor(f"psb{i}", [128, 512], F32)) for i in range(8)]
        P = Prog(nc)
        R = Carver(big, 0, sbuf_kb * 1024)
        wB = R.take(8 * 4096, BF16).rearrange("p (c n) -> p c n", c=8)
        Wabs = R.take(4 * 1024, BF16).rearrange("p (c n) -> p c n", c=4)
        wpe = R.take(4 * 256, BF16).rearrange("p (c n) -> p c n", c=4)
        wuv = R.take(2 * 512, BF16).rearrange("p (c n) -> p c n", c=2)
        woa = R.take(4 * 1024, BF16).rearrange("p (c n) -> p c n", c=4)
        wob = R.take(4 * 1024, BF16).rearrange("p (c n) -> p c n", c=4)
        wout = R.take(8 * 1024, BF16).rearrange("p (c n) -> p c n", c=8)
        KT = R.take(4 * NKEY, BF16).rearrange("p (h k) -> p h k", h=4)
        Vaug = R.take(NKT * 4 * 130, BF16).rearrange("p (j h e) -> p j h e", j=NKT, h=4)
        ckvT = R.take(2 * NKEY, BF16).rearrange("p (c k) -> p c k", c=2)
        ckvaug = R.take(NKT * 258, BF16).rearrange("p (j e) -> p j e", j=NKT)
        kpeT = R.take(NKEY, BF16)
        ident = R.take(128, BF16)
        ropeT = R.take(2 * NKT * 32, F32).rearrange("p (a j e) -> p a j e", a=2, j=NKT)
        gkva = R.take(256, F32); gsub8 = R.take(128, F32); gfin = R.take(D, F32)
        ginT = R.take(8, F32); gqaT = R.take(4, F32)
        rstd_all = R.take(NKT, F32)
        sc_c = R.take(16, F32)
        mhalf = sc_c[:, 0:1]; nlam = sc_c[:, 1:2]; ld1 = sc_c[:, 2:3]; ld2 = sc_c[:, 3:4]
        U0 = R.off
        nc._u0 = U0
        ULIM = sbuf_kb * 1024

        cnt = {"ps": 0, "s": 0, "pt": 0, "acc": 0, "pa": 0, "ob": 0, "attn": 0}

        def proj_bank():
            if cnt["attn"]:
                return 0
            k = cnt["ps"] % 2
            cnt["ps"] += 1
            return k

        def s_bank():
            k = (2, 3, 1)[cnt["s"] % 3]
            cnt["s"] += 1
            return k

        def bfv(k):
            return psb[k][:, :].bitcast(BF16)

        evac_rr = [0]

        def copy(out, in_, reads, writes, eng=None):
            if eng is None:
                eng = ("dve", "act")[evac_rr[0] % 2]
                evac_rr[0] += 1
            if eng == "act":
                P.op("act", lambda e: e.activation(out=out, in_=in_, func=AF.Copy), reads, writes)
            elif eng == "dve":
                P.op("dve", lambda e: e.tensor_copy(out=out, in_=in_), reads, writes)
            else:
                P.op("pool", lambda e: e.tensor_copy(out=out, in_=in_), reads, writes)

        def scaled(out, in_, sc, reads, writes, eng):
            if eng == "act":
                P.op("act", lambda e: e.activation(out=out, in_=in_, func=AF.Copy, scale=sc), reads, writes)
            else:
                P.op("dve", lambda e: e.tensor_scalar(out=out, in0=in_, scalar1=sc, scalar2=None, op0=ALU.mult), reads, writes)

        def dma(out, in_, reads, writes, key, q="sp"):
            P.op(q, lambda e: e.dma_start(out=out, in_=in_), reads, writes, dma=True, semkey=key)

        def transposes(srcs, n, bank, rd):
            bv = bfv(bank)
            for k, src in enumerate(srcs):
                m = src.shape[1]
                P.op("pe", lambda e, k=k, src=src, m=m: e.transpose(out=bv[0:m, k * 128:k * 128 + n], in_=src, identity=ident[0:n, 0:n]),
                     reads=list(rd) + ["ident"], writes=[("ps", bank)])

        def mm_group(bank, col0, ncols, n, lhs_list, rhs_list, rd):
            nk = len(lhs_list)
            for k in range(nk):
                P.op("pe", lambda e, k=k: e.matmul(psb[bank][0:n, col0:col0 + ncols], lhsT=lhs_list[k], rhs=rhs_list[k],
                                                    start=(k == 0), stop=(k == nk - 1)),
                     reads=rd, writes=[("ps", bank)])

        def rsqrt_chain(ssq, mul, out, nm, n, extra=None):
            P.op("dve", lambda e: e.tensor_scalar(out=out[0:n], in0=ssq[0:n], scalar1=mul, scalar2=EPS, op0=ALU.mult, op1=ALU.add),
                 reads=[nm + "_ssq"], writes=[nm])
            P.op("pool", lambda e: e.tensor_tensor(out=out[0:n], in0=out[0:n], in1=mhalf[0:n], op=ALU.pow),
                 reads=[nm, "consts"], writes=[nm])

        dma(ident, ident_d, [], ["ident"], "c0")
        dma(ropeT, rope_d.rearrange("a p j e -> p a j e"), [], ["rope"], "c1")
        dma(gkva, g_kva.partition_broadcast(128), [], ["gkva"], "c2")
        dma(gsub8, g_sub.partition_broadcast(128), [], ["gsub8"], "c3")
        dma(gfin, g_fin.partition_broadcast(128), [], ["gfin"], "c4")
        dma(ginT, g_inT, [], ["ginT"], "c5")
        dma(gqaT, g_qaT, [], ["gqaT"], "c6")
        P.op("pool", lambda e: e.memset(sc_c, -0.5), [], ["consts"])
        P.op("dve", lambda e: e.tensor_scalar(out=gsub8, in0=gsub8, scalar1=OSC, scalar2=None, op0=ALU.mult), ["gsub8"], ["gsub8"])
        P.op("pool", lambda e: e.memset(Vaug, 1.0), [], ["Vaug"])
        P.op("pool", lambda e: e.memset(ckvaug, 1.0), [], ["ckvaug"])
        P.op("pool", lambda e: e.memset(kpeT, 0.0), [], ["kpeT"])

        U = Carver(big, U0, ULIM)
        stage = [U.take(2880, F32) for _ in range(2)]
        wkv16 = U.take(1344, BF16)
        lamj = U.take(64, F32)
        lamt = U.take(4 * 64, F32).rearrange("p (a e) -> p a e", a=4)
        for a in range(4):
            dma(lamt[:, a, :], lam4[a:a + 1, :].partition_broadcast(128), [], ["lamt"], "c7")
        P.op("dve", lambda e: e.scalar_tensor_tensor(out=lamj, in0=lamt[:, 0, :], scalar=1.0, in1=lamt[:, 1, :], op0=ALU.mult, op1=ALU.mult, accum_out=ld1),
             ["lamt", "consts"], ["lamj", "ld1"])
        P.op("dve", lambda e: e.scalar_tensor_tensor(out=lamj, in0=lamt[:, 2, :], scalar=1.0, in1=lamt[:, 3, :], op0=ALU.mult, op1=ALU.mult, accum_out=ld2),
             ["lamt", "ld1"], ["lamj", "ld2"])
        P.op("act", lambda e: e.activation(out=ld1, in_=ld1, func=AF.Exp), ["ld1"], ["ld1"])
        P.op("act", lambda e: e.activation(out=ld2, in_=ld2, func=AF.Exp), ["ld2"], ["ld2"])
        P.op("dve", lambda e: e.scalar_tensor_tensor(out=nlam, in0=ld2, scalar=-LAM_INIT, in1=ld1, op0=ALU.add, op1=ALU.subtract),
             ["ld1", "ld2"], ["nlam"])
        segs0 = [(0, 512, 0), (2048, 2560, 512), (1536, 2048, 1024)]
        segs1 = [(2880, 3392, 1536), (3392, 5440, 2048)]
        for c in range(8):
            gc = ginT[:, c:c + 1]
            dma(stage[0], w_in[c * 128:(c + 1) * 128, 0:2880], [], [("stage", 0)], ("stg", 0))
            for si, (a, b, o) in enumerate(segs0):
                scaled(wB[:, c, o:o + (b - a)], stage[0][:, a:b], gc, [("stage", 0), "ginT"], ["wB"], ("act", "dve")[si % 2])
            scaled(wkv16[:, 0:1024], stage[0][:, 512:1536], gc, [("stage", 0), "ginT"], ["wkv16"], "dve")
            scaled(wkv16[:, 1024:1344], stage[0][:, 2560:2880], gc, [("stage", 0), "ginT"], ["wkv16"], "act")
            dma(wkv_scr[c], wkv16, ["wkv16"], ["wkv_scr"], "wkvo")
            dma(stage[1][:, 0:2560], w_in[c * 128:(c + 1) * 128, 2880:5440], [], [("stage", 1)], ("stg", 1))
            for si, (a, b, o) in enumerate(segs1):
                scaled(wB[:, c, o:o + (b - a)], stage[1][:, a - 2880:b - 2880], gc, [("stage", 1), "ginT"], ["wB"], ("dve", "act")[si % 2])
        P.barrier_all()

        U = Carver(big, U0, ULIM)
        st2 = [U.take(1024, F32) for _ in range(2)]
        wuqn16 = U.take(4 * 512, BF16).rearrange("p (c n) -> p c n", c=4)
        wuk16 = U.take(2 * 512, BF16).rearrange("p (c n) -> p c n", c=2)
        wukT = U.take(4 * 256, BF16).rearrange("p (h n) -> p h n", h=4)
        wuqT = U.take(4 * 512, BF16).rearrange("p (h n) -> p h n", h=4)
        k2 = [0]

        def stage2(src, cols):
            sb = k2[0] % 2
            k2[0] += 1
            dma(st2[sb][:, 0:cols], src, [], [("st2", sb)], ("st2", sb))
            return st2[sb], ("st2", sb)

        for c in range(4):
            st, r = stage2(w_uq[c * 128:(c + 1) * 128, :], 768)
            s3 = st[:, 0:768].rearrange("p (h e) -> p h e", h=4)
            scaled(wuqn16[:, c, :].rearrange("p (h e) -> p h e", h=4), s3[:, :, 0:128], gqaT[:, c:c + 1], [r, "gqaT"], ["wuqn16"], "dve")
            scaled(wpe[:, c, :].rearrange("p (h e) -> p h e", h=4), s3[:, :, 128:192], gqaT[:, c:c + 1], [r, "gqaT"], ["wpe"], "act")
        for c in range(2):
            st, r = stage2(w_uk[c * 128:(c + 1) * 128, :], 512)
            copy(wuk16[:, c, :], st[:, 0:512], [r], ["wuk16"])
            st, r = stage2(w_uv[c * 128:(c + 1) * 128, :], 512)
            copy(wuv[:, c, :], st[:, 0:512], [r], ["wuv"])
        for (wsrc, wdst, nm, nch) in ((w_oa, woa, "woa", 4), (w_ob, wob, "wob", 4), (w_out, wout, "wout", 8)):
            for c in range(nch):
                st, r = stage2(wsrc[c * 128:(c + 1) * 128, :], 1024)
                scaled(wdst[:, c, :], st[:, 0:1024], 0.5, [r], [nm], ("act", "dve")[c % 2])
        for h in range(4):
            b = proj_bank()
            transposes([wuk16[:, cc, h * 128:(h + 1) * 128] for cc in range(2)], 128, b, ["wuk16"])
            copy(wukT[:, h, :], bfv(b)[:, 0:256], [("ps", b)], ["wukT"])
        for h in range(4):
            b = proj_bank()
            transposes([wuqn16[:, cc, h * 128:(h + 1) * 128] for cc in range(4)], 128, b, ["wuqn16"])
            copy(wuqT[:, h, :], bfv(b)[:, 0:512], [("ps", b)], ["wuqT"])
        for cc in range(4):
            for hp in range(2):
                b = proj_bank()
                for hh in range(2):
                    h = hp * 2 + hh
                    mm_group(b, hh * 256, 256, 128, [wuqT[:, h, cc * 128:(cc + 1) * 128]], [wukT[:, h, :]], ["wuqT", "wukT"])
                copy(Wabs[:, cc, hp * 512:(hp + 1) * 512], psb[b][:, 0:512], [("ps", b)], ["Wabs"])
        P.barrier_all()

        U = Carver(big, U0, ULIM)
        x32 = [U.take(D, F32) for _ in range(2)]
        ropeA = U.take(512, F32); ropeB = U.take(512, F32)
        scal = U.take(64, F32)
        xbf = U.take(D, BF16); hT = U.take(D, BF16); junk = U.take(D, BF16)
        UA0 = U.off
        oh32 = U.take(128, F32)
        qa16 = U.take(512, BF16); qd16 = U.take(512, BF16); qdT = U.take(512, BF16)
        zz = U.take(1024, BF16); tgs = [U.take(2048, BF16) for _ in range(2)]
        qlat16 = U.take(1024, BF16); qpe16 = U.take(512, BF16)
        QlT = U.take(1024, BF16)
        PT = [U.take(384, BF16) for _ in range(3)]
        ob16 = U.take(512, BF16); obT = U.take(512, BF16)
        m16 = U.take(1024, BF16); mT = U.take(1024, BF16)
        Q12z = U.take(1024, BF16).rearrange("p (h a q) -> p h a q", h=4, a=2); qpeT = U.take(512, BF16)

        def zero_kpe():
            P.op("pool", lambda e: e.memset(kpe16, 0.0), [], ["kpe16"])

        def zero_q():
            P.op("pool", lambda e: e.memset(qpe16, 0.0), [], ["qpe16"])
            P.op("pool", lambda e: e.memset(qpeT, 0.0), [], ["qpeT"])
            P.op("pool", lambda e: e.memset(Q12z, 0.0), [], ["Q12z"])
        oa16, oaT, olat16, olatT = qd16, qdT, qlat16, QlT
        rmall = scal[:, 32:36]
        UA = Carver(big, UA0, ULIM)
        wkv = UA.take(8 * 1344, BF16).rearrange("p (c n) -> p c n", c=8)
        kvout = UA.take(1344, F32)
        ka16 = UA.take(512, BF16); kpe16 = UA.take(128, BF16); ckv16 = UA.take(256, BF16)
        xcnt = [0]


        def S(i, n=128):
            return scal[0:n, i:i + 1]

        def rope(src, sc, ti, G, n, dst, rd, wr, dst_is_f32):
            W = G * 64
            cosb = ropeT[0:n, 0, ti, :].unsqueeze(1).broadcast_to([n, 2 * G, 32])
            sinb = ropeT[0:n, 1, ti, :].unsqueeze(1).broadcast_to([n, G, 32])
            nsc = S(23, n)
            P.op("dve", lambda e: e.tensor_scalar(out=nsc, in0=sc, scalar1=-1.0, scalar2=None, op0=ALU.mult), [r for r in rd if not (isinstance(r, tuple) and r[0] == "ps")], ["nsc"])
            g64 = lambda ap: ap.rearrange("p (g j) -> p g j", j=64)
            t1 = dst if dst_is_f32 else g64(ropeA[0:n, 0:W])
            t1n = wr if dst_is_f32 else ["ropeA"]
            v4 = lambda ap: ap.rearrange("p (g t j) -> p g t j", t=2, j=32)
            P.op("dve", lambda e: e.scalar_tensor_tensor(out=t1.rearrange("p g (t j) -> p (g t) j", j=32), in0=src.rearrange("p (g j) -> p g j", j=32),
                                                        scalar=sc, in1=cosb, op0=ALU.mult, op1=ALU.mult),
                 rd + ["rope"], t1n)
            t2 = ropeB[0:n, 0:W]
            P.op("dve", lambda e: e.scalar_tensor_tensor(out=v4(t2)[:, :, 0, :], in0=v4(src)[:, :, 1, :], scalar=nsc, in1=sinb, op0=ALU.mult, op1=ALU.mult),
                 rd + ["rope", "nsc"], ["ropeB"])
            P.op("dve", lambda e: e.scalar_tensor_tensor(out=v4(t2)[:, :, 1, :], in0=v4(src)[:, :, 0, :], scalar=sc, in1=sinb, op0=ALU.mult, op1=ALU.mult),
                 rd + ["rope", "ropeB"], ["ropeB"])
            P.op("pool", lambda e: e.tensor_tensor(out=dst, in0=t1, in1=g64(t2), op=ALU.add), t1n + ["ropeB"], wr)

        def load_x(src_ap, n):
            xb = xcnt[0] % 2
            xcnt[0] += 1
            dma(x32[xb][0:n, :], src_ap, [], [("x32", xb)], ("x", xb))
            return xb

        def norm_and_T(xb, n, want_rstd, ti, bank=None, evac="dve"):
            xa = x32[xb]
            if want_rstd:
                P.op("act", lambda e: e.activation(out=junk[0:n, :], in_=xa[0:n, :], func=AF.Square, accum_out=S(0, n)),
                     [("x32", xb)], ["junk", "rx_ssq"])
                rsqrt_chain(S(0), 1.0 / D, S(1), "rx", n)
                P.op("dve", lambda e: e.tensor_copy(out=rstd_all[0:n, ti:ti + 1], in_=S(1, n)), ["rx"], [("rstd", ti)])
            copy(xbf[0:n, :], xa[0:n, :], [("x32", xb)], ["xbf"], eng="act")
            b = proj_bank() if bank is None else bank
            transposes([xbf[0:n, c * 128:(c + 1) * 128] for c in range(8)], n, b, ["xbf"])
            copy(hT[:, :].rearrange("p (c t) -> p c t", c=8)[:, :, 0:n], bfv(b)[:, :].rearrange("p (c t) -> p c t", c=8)[:, :, 0:n],
                 [("ps", b)], ["hT"], eng=evac)

        def hT_c(c, n):
            return hT[:, c * 128:c * 128 + n]

        def bankA():
            k = cnt["pa"] % 8
            cnt["pa"] += 1
            return k

        def passA_part1(xb, n, ti):
            norm_and_T(xb, n, True, ti, bank=6)
            hl = [hT_c(c, n) for c in range(8)]
            k0 = 3 * (bankA() % 2)
            bc, bk_, bv = k0, k0 + 1, k0 + 2
            mm_group(bc, 0, 320, n, hl, [wkv[:, c, 1024:1344] for c in range(8)], ["hT", "wkv"])
            mm_group(bk_, 0, 512, n, hl, [wkv[:, c, 0:512] for c in range(8)], ["hT", "wkv"])
            mm_group(bv, 0, 512, n, hl, [wkv[:, c, 512:1024] for c in range(8)], ["hT", "wkv"])
            return (bc, bk_, bv)

        def passA_part2(banks, n, ti, kt, outs):
            ko, vo, co, po = outs
            bc, bk_, bv = banks
            rs = rstd_all[0:n, ti:ti + 1]
            rrs = [("rstd", ti)]
            g64 = lambda ap: ap.rearrange("p (g j) -> p g j", j=64)
            P.op("act", lambda e: e.activation(out=junk[0:n, 0:256], in_=psb[bc][0:n, 0:256], func=AF.Square, accum_out=S(2, n)),
                 [("ps", bc)], ["junk", "rc_ssq"])
            P.op("dve", lambda e: e.tensor_scalar(out=S(22, n), in0=rs, scalar1=rs, scalar2=1.0 / 256, op0=ALU.mult, op1=ALU.mult), rrs, ["tmp22"])
            P.op("dve", lambda e: e.tensor_scalar(out=S(3, n), in0=S(2, n), scalar1=S(22, n), scalar2=EPS, op0=ALU.mult, op1=ALU.add),
                 ["rc_ssq", "tmp22"], ["rc"])
            P.op("pool", lambda e: e.tensor_tensor(out=S(3, n), in0=S(3, n), in1=mhalf[0:n], op=ALU.pow), ["rc", "consts"], ["rc"])
            rope(psb[bk_][0:n, 0:512], rs, ti, 8, n, g64(kvout[0:n, 0:512]), [("ps", bk_)] + rrs, ["kv_k"], True)
            scaled(kvout[0:n, 512:1024], psb[bv][0:n, 0:512], rs, [("ps", bv)] + rrs, ["kv_v"], "act")
            copy(ka16[0:n, :], kvout[0:n, 0:512], ["kv_k"], ["ka16"], eng="act")
            dma(ko, kvout[0:n, 0:512], ["kv_k"], [], "oA0")
            dma(vo, kvout[0:n, 512:1024], ["kv_v"], [], "oA1")
            P.op("dve", lambda e: e.tensor_tensor(out=S(4, n), in0=S(3, n), in1=rs, op=ALU.mult), ["rc"] + rrs, ["sc_c"])
            P.op("dve", lambda e: e.scalar_tensor_tensor(out=kvout[0:n, 1024:1280], in0=psb[bc][0:n, 0:256], scalar=S(4, n), in1=gkva[0:n, :],
                                                        op0=ALU.mult, op1=ALU.mult), [("ps", bc), "sc_c", "gkva"], ["kv_c"])
            copy(ckv16[0:n, :], kvout[0:n, 1024:1280], ["kv_c"], ["ckv16"], eng="act")
            dma(co, kvout[0:n, 1024:1280], ["kv_c"], [], "oA2")
            rope(psb[bc][0:n, 256:320], rs, ti, 1, n, g64(kvout[0:n, 1280:1344]), [("ps", bc)] + rrs, ["kv_p"], True)
            copy(kpe16[0:n, 0:64], kvout[0:n, 1280:1344], ["kv_p"], ["kpe16"], eng="act")
            dma(po, kvout[0:n, 1280:1344], ["kv_p"], [], "oA3")
            P.op("pool", lambda e: e.tensor_copy(out=Vaug[0:n, kt, :, 0:128], in_=kvout[0:n, 512:1024].rearrange("p (h e) -> p h e", h=4)),
                 ["kv_v"], ["Vaug"])
            P.op("pool", lambda e: e.tensor_copy(out=ckvaug[0:n, kt, 0:256], in_=ckv16[0:n, :]), ["ckv16"], ["ckvaug"])
            b = 7
            transposes([ka16[0:n, h * 128:(h + 1) * 128] for h in range(4)], n, b, ["ka16"])
            copy(KT[:, :, kt * 128:kt * 128 + n], bfv(b)[:, 0:512].rearrange("p (h t) -> p h t", h=4)[:, :, 0:n], [("ps", b)], ["KT"], eng="dve")
            b = 6
            transposes([ckv16[0:n, c * 128:(c + 1) * 128] for c in range(2)] + [kpe16[0:n, :]], n, b, ["ckv16", "kpe16"])
            copy(ckvT[:, :, kt * 128:kt * 128 + n], bfv(b)[:, 0:256].rearrange("p (c t) -> p c t", c=2)[:, :, 0:n], [("ps", b)], ["ckvT"], eng="dve")
            copy(kpeT[:, kt * 128:kt * 128 + n], bfv(b)[:, 256:256 + n], [("ps", b)], ["kpeT"], eng="dve")

        def passA_seq(items, pre=None):
            xbs = dict(pre or {})
            for j in range(min(2, len(items))):
                if j not in xbs:
                    xbs[j] = load_x(items[j][0], items[j][1])
            banks = passA_part1(xbs[0], items[0][1], items[0][2])
            for i, (xsrc, n, ti, kt, outs) in enumerate(items):
                if i + 2 < len(items):
                    xbs[i + 2] = load_x(items[i + 2][0], items[i + 2][1])
                nb = None
                if i + 1 < len(items):
                    nb = passA_part1(xbs[i + 1], items[i + 1][1], items[i + 1][2])
                passA_part2(banks, n, ti, kt, outs)
                banks = nb

        def cache_seq(s):
            UC = Carver(big, UA0, ULIM)
            stg = [UC.take(1344, F32) for _ in range(3)]
            k16 = [UC.take(512, BF16) for _ in range(2)]
            c16 = [UC.take(256, BF16) for _ in range(2)]
            p16 = [UC.take(128, BF16) for _ in range(2)]
            for k in range(2):
                P.op("pool", lambda e, k=k: e.memset(p16[k], 0.0), [], [("p16", k)])

            def loads(kt):
                c0 = stg[kt % 3]
                r = slice(kt * 128, (kt + 1) * 128)
                q = kt % 3
                dma(c0[:, 0:512], ck[s, r, :], [], [("cstg", q, 0)], ("cA0", q))
                dma(c0[:, 512:1024], cv[s, r, :], [], [("cstg", q, 1)], ("cA1", q))
                dma(c0[:, 1024:1280], cckv[s, r, :], [], [("cstg", q, 2)], ("cA2", q))
                dma(c0[:, 1280:1344], cpe[s, r, :], [], [("cstg", q, 3)], ("cA3", q))

            loads(0)
            loads(1)
            for kt in range(16):
                if kt + 2 < 16:
                    loads(kt + 2)
                q, w = kt % 3, kt % 2
                c0 = stg[q]
                copy(k16[w][:, :], c0[:, 0:512], [("cstg", q, 0)], [("k16", w)], eng="act")
                copy(c16[w][:, :], c0[:, 1024:1280], [("cstg", q, 2)], [("c16", w)], eng="dve")
                copy(p16[w][:, 0:64], c0[:, 1280:1344], [("cstg", q, 3)], [("p16", w)], eng="act")
                v3 = c0[:, 512:1024].rearrange("p (h e) -> p h e", h=4)
                P.op("dve", lambda e, kt=kt, v3=v3: e.tensor_copy(out=Vaug[:, kt, 0:2, 0:128], in_=v3[:, 0:2, :]), [("cstg", q, 1)], [("Vaug", kt, 0)])
                P.op("pool", lambda e, kt=kt, v3=v3: e.tensor_copy(out=Vaug[:, kt, 2:4, 0:128], in_=v3[:, 2:4, :]), [("cstg", q, 1)], [("Vaug", kt, 1)])
                P.op("pool", lambda e, kt=kt, w=w: e.tensor_copy(out=ckvaug[:, kt, 0:256], in_=c16[w][:, :]), [("c16", w)], ["ckvaug"])
                b = bankA()
                transposes([k16[w][:, h * 128:(h + 1) * 128] for h in range(4)], 128, b, [("k16", w)])
                copy(KT[:, :, kt * 128:(kt + 1) * 128], bfv(b)[:, 0:512].rearrange("p (h t) -> p h t", h=4), [("ps", b)], ["KT"], eng="dve")
                b = bankA()
                transposes([c16[w][:, c * 128:(c + 1) * 128] for c in range(2)] + [p16[w][:, :]], 128, b, [("c16", w), ("p16", w)])
                copy(ckvT[:, :, kt * 128:(kt + 1) * 128], bfv(b)[:, 0:256].rearrange("p (c t) -> p c t", c=2), [("ps", b)], ["ckvT"], eng="act")
                copy(kpeT[:, kt * 128:(kt + 1) * 128], bfv(b)[:, 256:384], [("ps", b)], ["kpeT"], eng="act")

        def wcols(a, b_):
            return [wB[:, c, a:b_] for c in range(8)]

        class Blk:
            pass

        def pB_load(bk):
            bk.xb = load_x(bk.xsrc, bk.n)

        def pB_prologueA(bk):
            n, ti = bk.n, bk.ti
            norm_and_T(bk.xb, n, False, ti, evac="act")
            bk.rs = rstd_all[0:n, ti:ti + 1]
            bk.rrs = [("rstd", ti)]
            rs, rrs = bk.rs, bk.rrs
            P.op("dve", lambda e: e.tensor_scalar(out=S(7, n), in0=rs, scalar1=DIFF_SCALE, scalar2=None, op0=ALU.mult), rrs, ["sq8"])
            P.op("dve", lambda e: e.tensor_scalar(out=S(8, n), in0=rs, scalar1=0.5, scalar2=None, op0=ALU.mult), rrs, ["hrs"])
            bk.hl = [hT_c(c, n) for c in range(8)]

        def pB_prologueB(bk):
            n, ti = bk.n, bk.ti
            b = proj_bank()
            mm_group(b, 0, 512, n, bk.hl, wcols(0, 512), ["hT", "wB"])
            rope(psb[b][0:n, 0:512], S(7, n), ti, 8, n, qa16[0:n, :].rearrange("p (g j) -> p g j", j=64), [("ps", b), "sq8"], ["qa16"], False)

        def pB_gate(bk, gi, b):
            n = bk.n
            tg = tgs[bk.par]
            mm_group(b, 0, 512, n, bk.hl, wcols(2048 + gi * 512, 2560 + gi * 512), ["hT", "wB"])
            P.op("act", lambda e: e.activation(out=tg[0:n, gi * 512:(gi + 1) * 512], in_=psb[b][0:n, 0:512], func=AF.Tanh, scale=S(8, n)),
                 [("ps", b), "hrs"], [("tg", bk.par, gi)])

        def pB_stageP(bk):
            n, ti, rs, rrs, hl = bk.n, bk.ti, bk.rs, bk.rrs, bk.hl
            b = proj_bank()
            mm_group(b, 0, 512, n, hl, wcols(512, 1024), ["hT", "wB"])
            P.op("act", lambda e, b=b: e.activation(out=xbf[0:n, 0:512], in_=psb[b][0:n, 0:512], func=AF.Square, accum_out=S(5, n)),
                 [("ps", b)], ["xbf", "rq_ssq"])
            copy(qd16[0:n, :], psb[b][0:n, 0:512], [("ps", b)], ["qd16"], eng="dve")
            P.op("dve", lambda e: e.tensor_scalar(out=S(22, n), in0=rs, scalar1=rs, scalar2=1.0 / 512, op0=ALU.mult, op1=ALU.mult), rrs, ["tmp22"])
            P.op("dve", lambda e: e.tensor_scalar(out=S(6, n), in0=S(5, n), scalar1=S(22, n), scalar2=EPS, op0=ALU.mult, op1=ALU.add),
                 ["rq_ssq", "tmp22"], ["sq"])
            P.op("pool", lambda e: e.tensor_tensor(out=S(6, n), in0=S(6, n), in1=mhalf[0:n], op=ALU.pow), ["sq", "consts"], ["sq"])
            P.op("dve", lambda e: e.tensor_scalar(out=S(6, n), in0=S(6, n), scalar1=rs, scalar2=MLA_SCALE, op0=ALU.mult, op1=ALU.mult),
                 ["sq"] + rrs, ["sq"])

            yield
            if not bk.gates_done:
                for gi in range(4):
                    pB_gate(bk, gi, proj_bank())
            yield
            b = proj_bank()
            transposes([qd16[0:n, c * 128:(c + 1) * 128] for c in range(4)], n, b, ["qd16"])
            copy(qdT[:, :].rearrange("p (c t) -> p c t", c=4)[:, :, 0:n], bfv(b)[:, 0:512].rearrange("p (c t) -> p c t", c=4)[:, :, 0:n],
                 [("ps", b)], ["qdT"], eng="dve")
            yield
            ql = [qdT[:, c * 128:c * 128 + n] for c in range(4)]
            for half in range(2):
                b = proj_bank()
                mm_group(b, 0, 512, n, ql, [Wabs[:, c, half * 512:(half + 1) * 512] for c in range(4)], ["qdT", "Wabs"])
                scaled(qlat16[0:n, half * 512:(half + 1) * 512], psb[b][0:n, 0:512], S(6, n), [("ps", b), "sq"], ["qlat16"], ("act", "dve")[half])
            yield
            b = proj_bank()
            mm_group(b, 0, 256, n, ql, [wpe[:, c, :] for c in range(4)], ["qdT", "wpe"])
            rope(psb[b][0:n, 0:256], S(6, n), ti, 4, n, qpe16[0:n, :].rearrange("p (h e) -> p h e", h=4)[:, :, 0:64], [("ps", b), "sq"], ["qpe16"], False)
            yield
            v3 = lambda ap: ap.rearrange("p (h t) -> p h t", h=4)[:, :, 0:n]
            b = proj_bank()
            transposes([qa16[0:n, h * 128:(h + 1) * 128] for h in range(4)], n, b, ["qa16"])
            copy(Q12z[0:64, :, 0, 0:n], v3(bfv(b)[0:64, 0:512]), [("ps", b)], ["Q12z"], eng="dve")
            copy(Q12z[64:128, :, 1, 0:n], v3(bfv(b)[64:128, 0:512]), [("ps", b)], ["Q12z"], eng="dve")
            yield
            for zi in range(2):
                b = proj_bank()
                mm_group(b, 0, 512, n, hl, wcols(1024 + zi * 512, 1536 + zi * 512), ["hT", "wB"])
                P.op("act", lambda e, b=b: e.activation(out=xbf[0:n, 512:1024], in_=psb[b][0:n, 0:512], func=AF.Tanh, scale=S(8, n)),
                     [("ps", b), "hrs"], ["xbf"])
                P.op("dve", lambda e, b=b, zi=zi: e.scalar_tensor_tensor(out=zz[0:n, zi * 512:(zi + 1) * 512], in0=xbf[0:n, 512:1024], scalar=1.0,
                                                                        in1=psb[b][0:n, 0:512], op0=ALU.add, op1=ALU.mult),
                     [("ps", b), "xbf"], [("zz", zi)])
                if zi == 0:
                    P.op("dve", lambda e: e.tensor_tensor(out=zz[0:n, 0:512].rearrange("p (h e) -> p h e", h=4), in0=zz[0:n, 0:512].rearrange("p (h e) -> p h e", h=4),
                                                         in1=gsub8[0:n, :].unsqueeze(1).broadcast_to([n, 4, 128]), op=ALU.mult), [("zz", 0), "gsub8"], [("zz", 0)])
            yield
            b = proj_bank()
            transposes([qlat16[0:n, c * 128:(c + 1) * 128] for c in range(8)], n, b, ["qlat16"])
            copy(QlT[:, :].rearrange("p (c t) -> p c t", c=8)[:, :, 0:n], bfv(b)[:, :].rearrange("p (c t) -> p c t", c=8)[:, :, 0:n],
                 [("ps", b)], [("QlT", h_) for h_ in range(4)], eng="dve")
            yield
            b = proj_bank()
            transposes([qpe16[0:n, h * 128:(h + 1) * 128] for h in range(4)], n, b, ["qpe16"])
            copy(v3(qpeT[:, :]), v3(bfv(b)[:, 0:512]), [("ps", b)], ["qpeT"], eng="act")
            yield

        def pB_attention(bk, nxt):
            n, ti, rs, rrs, ktiles, diag = bk.n, bk.ti, bk.rs, bk.rrs, bk.ktiles, bk.diag
            nj = len(ktiles)
            tiles = [(h, jj, kt, nk) for h in range(4) for jj, (kt, nk) in enumerate(ktiles)]
            st = {}
            pending = []

            def emit_S(t):
                h, jj, kt, nk = tiles[t]
                sb = s_bank()
                pb = cnt["pt"] % 3
                cnt["pt"] += 1
                pt = PT[pb]
                ks = slice(kt * 128, kt * 128 + nk)
                qs = slice(h * 128, h * 128 + n)
                if n == 128:
                    P.op("pe", lambda e: e.matmul(psb[sb][0:nk, 0:256], lhsT=KT[:, h, ks], rhs=Q12z[:, h, :, :].rearrange("p a q -> p (a q)"),
                                                  start=True, stop=True), ["KT", "Q12z"], [("ps", sb)])
                else:
                    mm_group(sb, 0, n, nk, [KT[:, h, ks]], [Q12z[:, h, 0, 0:n]], ["KT", "Q12z"])
                    mm_group(sb, 128, n, nk, [KT[:, h, ks]], [Q12z[:, h, 1, 0:n]], ["KT", "Q12z"])
                mm_group(sb, 256, n, nk, [ckvT[:, 0, ks], ckvT[:, 1, ks], kpeT[:, ks]],
                         [QlT[:, (2 * h) * 128:(2 * h) * 128 + n], QlT[:, (2 * h + 1) * 128:(2 * h + 1) * 128 + n], qpeT[:, qs]],
                         ["ckvT", "kpeT", ("QlT", h), "qpeT"])
                P.op("act", lambda e: e.activation(
                    out=pt[0:nk, :].rearrange("p (a q) -> p a q", a=3)[:, :, 0:n],
                    in_=psb[sb][0:nk, 0:384].rearrange("p (a q) -> p a q", a=3)[:, :, 0:n], func=AF.Exp),
                    [("ps", sb)], [("PT", pb)])
                if diag and jj == nj - 1:
                    P.op("pool", lambda e: e.memset(pt[64:128, :].rearrange("p (a q) -> p a q", a=3)[:, :, 0:64], 0.0),
                         [("PT", pb)], [("PT", pb)])
                st[t] = (pt, pb)

            def emit_PV(t):
                h, jj, kt, nk = tiles[t]
                pt, pb = st.pop(t)
                if jj == 0:
                    cnt["acc"] += 1
                aset = cnt["acc"] % 2
                b12, bm = 4 + 2 * aset, 5 + 2 * aset
                first, last = (jj == 0), (jj == nj - 1)
                va = Vaug[0:nk, kt, h, 0:129]
                P.op("pe", lambda e: e.matmul(psb[b12][0:n, 0:129], lhsT=pt[0:nk, 0:n], rhs=va, start=first, stop=last),
                     [("PT", pb), "Vaug"], [("ps", b12)])
                P.op("pe", lambda e: e.matmul(psb[b12][0:n, 129:258], lhsT=pt[0:nk, 128:128 + n], rhs=va, start=False, stop=last, skip_group_check=True),
                     [("PT", pb), "Vaug"], [("ps", b12)])
                P.op("pe", lambda e: e.matmul(psb[bm][0:n, 0:257], lhsT=pt[0:nk, 256:256 + n], rhs=ckvaug[0:nk, kt, 0:257], start=first, stop=last),
                     [("PT", pb), "ckvaug"], [("ps", bm)])
                if last:
                    epilogue(h, b12, bm)
                    pending.append((t + 7, lambda h=h: head_post(h)))

            def epilogue(h, b12, bm):
                A = psb[b12]
                P.op("dve", lambda e: e.reciprocal(out=S(17, n), in_=A[0:n, 128:129]), [("ps", b12)], ["r1"])
                P.op("dve", lambda e: e.reciprocal(out=S(18, n), in_=A[0:n, 257:258]), [("ps", b12)], ["r2"])
                P.op("dve", lambda e: e.tensor_tensor(out=S(19, n), in0=S(18, n), in1=nlam[0:n], op=ALU.mult), ["r2", "nlam"], ["nl2"])
                P.op("dve", lambda e: e.tensor_scalar(out=oh32[0:n, :], in0=A[0:n, 0:128], scalar1=S(17, n), scalar2=None, op0=ALU.mult),
                     [("ps", b12), "r1"], ["oh32"])
                P.op("dve", lambda e: e.scalar_tensor_tensor(out=oh32[0:n, :], in0=A[0:n, 129:257], scalar=S(19, n), in1=oh32[0:n, :],
                                                            op0=ALU.mult, op1=ALU.add), [("ps", b12), "nl2", "oh32"], ["oh32"])
                P.op("dve", lambda e: e.reciprocal(out=rmall[0:n, h:h + 1], in_=psb[bm][0:n, 256:257]), [("ps", bm)], ["rm"])
                copy(olat16[0:n, h * 256:(h + 1) * 256], psb[bm][0:n, 0:256], [("ps", bm)], [("olat16", h), "qlat16"], eng="dve")
                P.op("dve", lambda e: e.tensor_tensor(out=rmall[0:n, h:h + 1], in0=rmall[0:n, h:h + 1], in1=rs, op=ALU.mult), ["rm"] + rrs, ["rm"])
                P.op("dve", lambda e: e.scalar_tensor_tensor(out=junk[0:n, 0:128], in0=oh32[0:n, :], scalar=1.0, in1=oh32[0:n, :],
                                                            op0=ALU.mult, op1=ALU.mult, accum_out=S(9, n)), ["oh32"], ["junk", "ro_ssq"])
                rsqrt_chain(S(9), 1.0 / 128, S(13), "ro", n)
                P.op("dve", lambda e: e.tensor_tensor(out=S(13, n), in0=S(13, n), in1=rs, op=ALU.mult), ["ro"] + rrs, ["ro"])
                P.op("dve", lambda e: e.scalar_tensor_tensor(out=oa16[0:n, h * 128:(h + 1) * 128], in0=oh32[0:n, :], scalar=S(13, n), in1=zz[0:n, h * 128:(h + 1) * 128],
                                                            op0=ALU.mult, op1=ALU.mult), ["oh32", "ro", ("zz", 0)], [("oa16", h), "qd16"])

            def head_post(h):
                b = proj_bank()
                transposes([oa16[0:n, h * 128:(h + 1) * 128], olat16[0:n, (2 * h) * 128:(2 * h + 1) * 128], olat16[0:n, (2 * h + 1) * 128:(2 * h + 2) * 128]],
                           n, b, [("oa16", h), ("olat16", h), "qd16", "qlat16"])
                copy(oaT[:, h * 128:h * 128 + n], bfv(b)[:, 0:n], [("ps", b)], [("oaT", h), "qdT"], eng="dve")
                copy(olatT[:, (2 * h) * 128:(2 * h + 2) * 128].rearrange("p (c t) -> p c t", c=2)[:, :, 0:n],
                     bfv(b)[:, 128:384].rearrange("p (c t) -> p c t", c=2)[:, :, 0:n], [("ps", b)], [("olatT", h), ("QlT", h)], eng="dve")
                b = proj_bank()
                mm_group(b, 0, 128, n, [olatT[:, (2 * h + c) * 128:(2 * h + c) * 128 + n] for c in range(2)],
                         [wuv[:, c, h * 128:(h + 1) * 128] for c in range(2)], [("olatT", h), ("QlT", h), "wuv"])
                P.op("dve", lambda e, b=b: e.scalar_tensor_tensor(out=ob16[0:n, h * 128:(h + 1) * 128], in0=psb[b][0:n, 0:128],
                                                                 scalar=rmall[0:n, h:h + 1], in1=zz[0:n, 512 + h * 128:512 + (h + 1) * 128],
                                                                 op0=ALU.mult, op1=ALU.mult), [("ps", b), "rm", ("zz", 1)], [("ob16", h)])

            cnt["attn"] = 1
            emit_S(0)
            if len(tiles) > 1:
                emit_S(1)
            tA = (len(tiles) * 5) // 8
            for t in range(len(tiles)):
                if t + 2 < len(tiles):
                    emit_S(t + 2)
                emit_PV(t)
                if t == tA and nxt is not None:
                    pB_prologueA(nxt)
                for (at, fn) in [p for p in pending if p[0] <= t]:
                    fn()
                pending[:] = [p for p in pending if p[0] > t]
            bk.pending = [fn for (_, fn) in pending]
            cnt["attn"] = 0

        def obank():
            k = cnt["ob"] % 4
            cnt["ob"] += 1
            return k

        def pB_stageO(bk, nxt):
            n, ti = bk.n, bk.ti
            xb = bk.xb
            if nxt is not None:
                pB_prologueB(nxt)
            for fn in bk.pending:
                fn()
            oaS = [("oaT", h) for h in range(4)]
            junk32 = junk[:, :].bitcast(F32)
            tg = tgs[bk.par]
            tmps = ((ropeA, ["ropeA"], ropeB, ["ropeB"]), (junk32, ["junk", "junk2"], ropeA, ["ropeA"]))

            def ngate(gi):
                if nxt is not None:
                    pB_gate(nxt, gi, obank())
                    nxt.gates_done = True
            for half in range(2):
                cs = slice(half * 512, (half + 1) * 512)
                b = obank()
                mm_group(b, 0, 512, n, [oaT[:, c * 128:c * 128 + n] for c in range(4)], [woa[:, c, cs] for c in range(4)], oaS + ["qdT", "woa"])
                ta, tak, tb, tbk = tmps[half]
                P.op("dve", lambda e, b=b, cs=cs, ta=ta: e.scalar_tensor_tensor(out=ta[0:n, :], in0=tg[0:n, cs], scalar=1.0, in1=psb[b][0:n, 0:512],
                                                                               op0=ALU.add, op1=ALU.mult), [("ps", b), ("tg", bk.par, half)], tak)
                ngate(half)
                if half == 0:
                    b = obank()
                    transposes([ob16[0:n, c * 128:(c + 1) * 128] for c in range(4)], n, b, [("ob16", h) for h in range(4)])
                    copy(obT[:, :].rearrange("p (c t) -> p c t", c=4)[:, :, 0:n], bfv(b)[:, 0:512].rearrange("p (c t) -> p c t", c=4)[:, :, 0:n],
                         [("ps", b)], ["obT"], eng="act")
            yield
            for half in range(2):
                cs = slice(half * 512, (half + 1) * 512)
                ta, tak, tb, tbk = tmps[half]
                b = obank()
                mm_group(b, 0, 512, n, [obT[:, c * 128:c * 128 + n] for c in range(4)], [wob[:, c, cs] for c in range(4)], ["obT", "wob"])
                P.op("dve", lambda e, b=b, half=half, tb=tb: e.scalar_tensor_tensor(out=tb[0:n, :], in0=tg[0:n, 1024 + half * 512:1536 + half * 512], scalar=1.0,
                                                                                   in1=psb[b][0:n, 0:512], op0=ALU.add, op1=ALU.mult),
                     [("ps", b), ("tg", bk.par, 2 + half)], tbk)
                P.op("dve", lambda e, cs=cs, ta=ta, tb=tb: e.tensor_tensor(out=m16[0:n, cs], in0=ta[0:n, :], in1=tb[0:n, :], op=ALU.add), tak + tbk, [("m16", half)])
                ngate(2 + half)
                b = obank()
                transposes([m16[0:n, c * 128:(c + 1) * 128] for c in range(4 * half, 4 * half + 4)], n, b, [("m16", half)])
                copy(mT[:, half * 512:(half + 1) * 512].rearrange("p (c t) -> p c t", c=4)[:, :, 0:n], bfv(b)[:, 0:512].rearrange("p (c t) -> p c t", c=4)[:, :, 0:n],
                     [("ps", b)], [("mT", half)], eng="act")
                yield
            xa = x32[xb]
            for half in range(2):
                cs = slice(half * 512, (half + 1) * 512)
                b = obank()
                mm_group(b, 0, 512, n, [mT[:, c * 128:c * 128 + n] for c in range(8)], [wout[:, c, cs] for c in range(8)], [("mT", 0), ("mT", 1), "wout"])
                P.op("dve", lambda e, b=b, cs=cs: e.tensor_tensor(out=xa[0:n, cs], in0=psb[b][0:n, 0:512], in1=xa[0:n, cs], op=ALU.add),
                     [("ps", b), ("x32", xb)], [("x32", xb)])
                yield
            P.op("act", lambda e: e.activation(out=junk[0:n, :], in_=xa[0:n, :], func=AF.Square, accum_out=S(20, n)),
                 [("x32", xb)], ["junk", "junk2", "rf_ssq"])
            rsqrt_chain(S(20), 1.0 / D, S(21), "rf", n)
            P.op("dve", lambda e: e.scalar_tensor_tensor(out=xa[0:n, :], in0=xa[0:n, :], scalar=S(21, n), in1=gfin[0:n, :], op0=ALU.mult, op1=ALU.mult),
                 [("x32", xb), "rf", "gfin"], [("x32", xb)])
            dma(bk.ydst, xa[0:n, :], [("x32", xb)], [], ("yo", xb))
            yield

        def passB_seq(blocks, pre=None):
            bks = []
            for (xsrc, n, ti, ktiles, diag, ydst) in blocks:
                bk = Blk()
                bk.xsrc, bk.n, bk.ti, bk.ktiles, bk.diag, bk.ydst = xsrc, n, ti, ktiles, diag, ydst
                bk.par = len(bks) % 2
                bk.gates_done = False
                bks.append(bk)
            if pre is None:
                pB_load(bks[0])
            else:
                bks[0].xb = pre
            pB_prologueA(bks[0])
            pB_prologueB(bks[0])
            for _ in pB_stageP(bks[0]):
                pass
            for i, bk in enumerate(bks):
                nxt = bks[i + 1] if i + 1 < len(bks) else None
                if nxt is not None:
                    pB_load(nxt)
                pB_attention(bk, nxt)
                gO = pB_stageO(bk, nxt)
                next(gO)
                gP = pB_stageP(nxt) if nxt is not None else iter(())
                doneO = doneP = False
                while not (doneO and doneP):
                    if not doneP:
                        try:
                            next(gP)
                        except StopIteration:
                            doneP = True
                    if not doneO:
                        try:
                            next(gO)
                        except StopIteration:
                            doneO = True

        def load_wkv():
            for c in range(8):
                dma(wkv[:, c, :], wkv_scr[c], ["wkv_scr"], ["wkv"], "wkvl")

        if stop <= 1:
            dma(x32[0], xp[0, 0:128, :], [], [("x32", 0)], ("x", 0))
        for s in range(nseq if stop >= 2 else 0):
            preA = {j: load_x(xp[s, j * 128:(j + 1) * 128, :], 128) for j in range(min(2, nblk))}
            load_wkv()
            zero_kpe()
            passA_seq(pre=preA, items=[(xp[s, i * 128:(i + 1) * 128, :], 128, i, i,
                        (kp[s, i * 128:(i + 1) * 128, :], vp[s, i * 128:(i + 1) * 128, :], cpo[s, i * 128:(i + 1) * 128, :], ppo[s, i * 128:(i + 1) * 128, :]))
                       for i in range(nblk)])
            P.barrier_all()
            zero_q()
            if stop >= 3:
                passB_seq([(xp[s, i * 128:(i + 1) * 128, :], 128, i, [(j, 128) for j in range(i + 1)], True, yp[s, i * 128:(i + 1) * 128, :])
                           for i in range(nblk)])
            P.barrier_all()
        if with_sample:
            for s in range(nseq):
                xa_ = load_x(xs[s], DEC)
                xb_ = load_x(xs[s], DEC)
                load_wkv()
                zero_kpe()
                passA_seq([(xs[s], DEC, 16, 16, (kso[s], vso[s], cso[s], pso[s]))], pre={0: xa_})
                P.barrier_all()
                cache_seq(s)
                P.barrier_all()
                zero_q()
                passB_seq([(xs[s], DEC, 16, [(j, 128) for j in range(16)] + [(16, DEC)], False, ys[s])], pre=xb_)
                P.barrier_all()

        P.finalize()
        sems = {e: es.enter_context(nc.semaphore(f"s_{e}")) for e in ENGS}
        dsems = {k: es.enter_context(nc.semaphore("d_" + "".join(ch for ch in str(k) if ch.isalnum()))) for k in P.dma_keys}
        run = P.emit(sems, dsems)
        with nc.Block() as block:
            @block.tensor
            def _(e):
                run("pe", e)

            @block.scalar
            def _(e):
                run("act", e)

            @block.vector
            def _(e):
                run("dve", e)

            @block.gpsimd
            def _(e):
                run("pool", e)

            @block.sync
            def _(e):
                run("sp", e)
    nc._prog_stats = {e: len(P.eng_ops[e]) for e in ENGS}
    return nc


def _consts():
    import ml_dtypes
    return {"ident": np.eye(128, dtype=np.float32).astype(ml_dtypes.bfloat16), "rope": rope_tables()}


def _weight_map(w_in, w_uq, w_uk, w_uv, w_oa, w_ob, w_out, lambda_q1, lambda_k1, lambda_q2, lambda_k2,
                norm_in, norm_qa, norm_kva, norm_subln, norm_final):
    f = lambda a: np.ascontiguousarray(np.asarray(a, dtype=np.float32))
    m = {
        "w_in": f(w_in[0]), "w_uq": f(w_uq[0]).reshape(512, 768), "w_uk": f(w_uk[0]).reshape(256, 512),
        "w_uv": f(w_uv[0]).reshape(256, 512), "w_oa": f(w_oa[0]), "w_ob": f(w_ob[0]), "w_out": f(w_out[0]),
        "lam4": f(np.stack([lambda_q1[0], lambda_k1[0], lambda_q2[0], lambda_k2[0]], axis=0)),
        "g_inT": f(np.asarray(norm_in[0]).reshape(8, 128).T), "g_qaT": f(np.asarray(norm_qa[0]).reshape(4, 128).T),
        "g_kva": f(norm_kva[0]).reshape(1, 256), "g_sub": f(norm_subln[0]).reshape(1, 128),
        "g_fin": f(norm_final).reshape(1, D),
    }
    m.update(_consts())
    return m


_NC_CACHE = {}


def kernel(x_prompt, x_sample, cache_diff_k, cache_diff_v, cache_mla_ckv, cache_mla_kpe,
           w_in, w_uq, w_uk, w_uv, w_oa, w_ob, w_out,
           lambda_q1, lambda_k1, lambda_q2, lambda_k2,
           norm_in, norm_qa, norm_kva, norm_subln, norm_final):
    NCORE = 8
    B, T, _ = x_prompt.shape
    nseq = B // NCORE
    nblk = T // 128
    key = (nseq, nblk)
    if key not in _NC_CACHE:
        _NC_CACHE[key] = build_program(nseq=nseq, nblk=nblk, with_sample=True)
    nc = _NC_CACHE[key]
    wm = _weight_map(w_in, w_uq, w_uk, w_uv, w_oa, w_ob, w_out, lambda_q1, lambda_k1, lambda_q2, lambda_k2,
                     norm_in, norm_qa, norm_kva, norm_subln, norm_final)
    f = lambda a: np.ascontiguousarray(np.asarray(a, dtype=np.float32))
    in_maps = []
    for c in range(NCORE):
        sl = slice(c * nseq, (c + 1) * nseq)
        m = dict(wm)
        m["xp"] = f(x_prompt[sl])
        m["xs"] = f(x_sample[sl])
        m["ck"] = f(cache_diff_k[0, sl]).reshape(nseq, PAST, 512)
        m["cv"] = f(cache_diff_v[0, sl]).reshape(nseq, PAST, 512)
        m["cc"] = f(cache_mla_ckv[0, sl])
        m["cpe"] = f(cache_mla_kpe[0, sl])
        in_maps.append(m)
    res = run_bass_kernel_spmd(nc, in_maps, core_ids=list(range(NCORE)))
    R = res.results
    cat = lambda k: np.concatenate([np.asarray(r[k], dtype=np.float32) for r in R], axis=0)
    y_p = cat("yp"); y_s = cat("ys")
    k_p = cat("kp").reshape(1, B, T, 4, 128); v_p = cat("vp").reshape(1, B, T, 4, 128)
    c_p = cat("cpo").reshape(1, B, T, 256); p_p = cat("ppo").reshape(1, B, T, 64)
    k_s = cat("ks").reshape(1, B, DEC, 4, 128); v_s = cat("vs").reshape(1, B, DEC, 4, 128)
    c_s = cat("cs").reshape(1, B, DEC, 256); p_s = cat("ps").reshape(1, B, DEC, 64)
    return (y_p, y_s, k_p, v_p, c_p, p_p, k_s, v_s, c_s, p_s)
```

```python
import numpy as np
import concourse.bass as bass
import concourse.mybir as mybir
from concourse.bass_utils import run_bass_kernel_spmd

F32 = mybir.dt.float32
BF16 = mybir.dt.bfloat16
AF = mybir.ActivationFunctionType
ALU = mybir.AluOpType

ENGS = ("pe", "act", "dve", "pool", "sp")


class Op:
    __slots__ = ("idx", "eng", "fn", "deps", "dma", "semkey", "sig", "val", "eidx")

    def __init__(self, idx, eng, fn, dma, semkey):
        self.idx = idx
        self.eng = eng
        self.fn = fn
        self.deps = set()
        self.dma = dma
        self.semkey = semkey
        self.sig = False
        self.val = 0
        self.eidx = 0


class Prog:
    def __init__(self, nc):
        self.nc = nc
        self.ops = []
        self.eng_ops = {e: [] for e in ENGS}
        self.last_w = {}
        self.rd_eng = {}
        self.rd_dma = {}
        self.last_x = {}

    def op(self, eng, fn, reads=(), writes=(), dma=False, semkey=None):
        o = Op(len(self.ops), eng, fn, dma, semkey)
        xs = [r for r in list(reads) + list(writes) if isinstance(r, tuple) and r[0] == "ps"]
        if xs:
            reads = [r for r in reads if r not in xs]
            writes = [r for r in writes if r not in xs]
            for r in set(xs):
                la = self.last_x.get(r)
                if la is not None and self.ops[la].eng != eng:
                    o.deps.add(la)
                self.last_x[r] = o.idx
        for r in reads:
            w = self.last_w.get(r)
            if w is not None:
                o.deps.add(w)
        for w_ in writes:
            w = self.last_w.get(w_)
            rdrs = self.rd_eng.get(w_, {})
            covered = (w is not None and not dma and not self.ops[w].dma and self.ops[w].eng == eng
                       and any(e2 != eng for e2 in rdrs))
            if w is not None and not covered:
                o.deps.add(w)
            for i in rdrs.values():
                o.deps.add(i)
            for i in self.rd_dma.get(w_, ()):
                o.deps.add(i)
        for r in reads:
            if dma:
                self.rd_dma.setdefault(r, []).append(o.idx)
            else:
                self.rd_eng.setdefault(r, {})[eng] = o.idx
        for w_ in writes:
            self.last_w[w_] = o.idx
            self.rd_eng[w_] = {}
            self.rd_dma[w_] = []
        o.deps.discard(o.idx)
        o.eidx = len(self.eng_ops[eng])
        self.ops.append(o)
        self.eng_ops[eng].append(o)
        return o

    def barrier_all(self):
        lasts = []
        for e in ENGS:
            lst = [o for o in self.eng_ops[e] if not o.dma and o.fn is not None]
            if lst:
                lasts.append(lst[-1].idx)
        lastd = {}
        for o in self.ops:
            if o.dma:
                lastd[o.semkey] = o.idx
        self._pending_barrier = set(lasts) | set(lastd.values())
        for e in ENGS:
            eo = e
            o = self.op(eo, None, (), ())
            o.deps |= {d for d in self._pending_barrier if d != o.idx}

    def finalize(self):
        ops = self.ops
        for o in ops:
            for d in o.deps:
                od = ops[d]
                if od.dma:
                    od.sig = True
                elif od.eng == "pe" and o.eng == "pe" and not o.dma:
                    continue
                else:
                    od.sig = True
        cnt = {e: 0 for e in ENGS}
        dcnt = {}
        for o in ops:
            if o.dma:
                dcnt[o.semkey] = dcnt.get(o.semkey, 0) + 16
                o.val = dcnt[o.semkey]
            else:
                if o.sig:
                    cnt[o.eng] += 1
                o.val = cnt[o.eng]
        self.dma_keys = list(dcnt.keys())
        self.final_dma = dict(dcnt)

    def plan_waits(self):
        ops = self.ops
        prod = {}
        for o in ops:
            if o.dma:
                prod[(("d", o.semkey), o.val)] = o.idx
            elif o.sig:
                prod[(("e", o.eng), o.val)] = o.idx
        k_eng = {e: {} for e in ENGS}
        k_done = {}
        plan = {}
        for o in ops:
            e = o.eng
            need = {}
            for d in o.deps:
                od = ops[d]
                if od.dma:
                    s = ("d", od.semkey)
                else:
                    if od.eng == "pe" and e == "pe" and not o.dma:
                        continue
                    s = ("e", od.eng)
                if od.val > need.get(s, 0):
                    need[s] = od.val
            K = k_eng[e]
            todo = []
            for s, v in sorted(need.items(), key=lambda kv: -prod.get((kv[0], kv[1]), -1)):
                if K.get(s, 0) >= v:
                    continue
                todo.append((s, v))
                K[s] = v
                p = prod.get((s, v))
                if p is not None and p in k_done:
                    for s2, v2 in k_done[p].items():
                        if v2 > K.get(s2, 0):
                            K[s2] = v2
            plan[o.idx] = todo
            if o.dma:
                kd = dict(K)
                kd[("d", o.semkey)] = o.val
                k_done[o.idx] = kd
            elif o.sig:
                kd = dict(K)
                kd[("e", e)] = o.val
                k_done[o.idx] = kd
        return plan

    def emit(self, sems, dsems):
        plan = self.plan_waits()

        def run(e, eng):
            for o in self.eng_ops[e]:
                todo = [(dsems[s[1]] if s[0] == "d" else sems[s[1]], v) for (s, v) in plan[o.idx]]
                attach = None
                if todo and o.fn is not None and not o.dma:
                    attach = todo.pop()
                for sem, v in todo:
                    eng.wait_ge(sem, v)
                if o.fn is None:
                    continue
                ins = o.fn(eng)
                if attach is not None:
                    ins.wait_op(attach[0], attach[1], "sem-ge")
                if o.dma:
                    ins.then_inc(dsems[o.semkey], 16)
                elif o.sig:
                    ins.then_inc(sems[e], 1)
            if e == "sp":
                for k, v in self.final_dma.items():
                    eng.wait_ge(dsems[k], v)

        return run


D = 1024
DIN = 5440
NKT = 17
NKEY = NKT * 128
LAM_INIT = 0.2
OSC = 1.0 - LAM_INIT
EPS = 1e-6
DIFF_SCALE = 0.125
MLA_SCALE = 192.0 ** -0.5
PAST = 2048
DEC = 16


class Carver:
    def __init__(self, big, off, limit):
        self.big, self.off, self.limit, self.base = big, off, limit, off

    def take(self, cols, dt):
        nb = cols * (4 if dt == F32 else 2)
        nb = (nb + 63) // 64 * 64
        assert self.off + nb <= self.limit, ("SBUF carve overflow", self.off + nb, self.limit, self.base)
        a = self.big[:, self.off // 4:(self.off + nb) // 4]
        self.off += nb
        if dt != F32:
            a = a.bitcast(dt)
        return a[:, 0:cols]


def rope_tables():
    half = 32
    inv = (10000.0 ** (-np.arange(half, dtype=np.float32) * 2.0 / 64)).astype(np.float32)
    pos = np.arange(NKT * 128, dtype=np.float32)
    ang = (pos[:, None] * inv[None, :]).astype(np.float32)
    c = np.cos(ang).astype(np.float32).reshape(NKT, 128, half).transpose(1, 0, 2)
    s = np.sin(ang).astype(np.float32).reshape(NKT, 128, half).transpose(1, 0, 2)
    return np.ascontiguousarray(np.stack([c, s], axis=0))


def build_program(nseq=4, nblk=16, with_sample=True, sbuf_kb=223, stop=99):
    from contextlib import ExitStack
    T = nblk * 128
    nc = bass.Bass("TRN2", target_bir_lowering=False, dynamic_dma_scratch_size=256)

    def din(name, shape, dt=F32):
        return nc.dram_tensor(name, list(shape), dt, kind="ExternalInput").ap()

    def dout(name, shape, dt=F32):
        return nc.dram_tensor(name, list(shape), dt, kind="ExternalOutput").ap()

    xp = din("xp", [nseq, T, D])
    w_in = din("w_in", [D, DIN]); w_uq = din("w_uq", [512, 768]); w_uk = din("w_uk", [256, 512])
    w_uv = din("w_uv", [256, 512]); w_oa = din("w_oa", [512, D]); w_ob = din("w_ob", [512, D])
    w_out = din("w_out", [D, D]); lam4 = din("lam4", [4, 64])
    g_inT = din("g_inT", [128, 8]); g_qaT = din("g_qaT", [128, 4]); g_kva = din("g_kva", [1, 256])
    g_sub = din("g_sub", [1, 128]); g_fin = din("g_fin", [1, D])
    ident_d = din("ident", [128, 128], BF16); rope_d = din("rope", [2, 128, NKT, 32])
    yp = dout("yp", [nseq, T, D]); kp = dout("kp", [nseq, T, 512]); vp = dout("vp", [nseq, T, 512])
    cpo = dout("cpo", [nseq, T, 256]); ppo = dout("ppo", [nseq, T, 64])
    if with_sample:
        xs = din("xs", [nseq, DEC, D]); ck = din("ck", [nseq, PAST, 512]); cv = din("cv", [nseq, PAST, 512])
        cckv = din("cc", [nseq, PAST, 256]); cpe = din("cpe", [nseq, PAST, 64])
        ys = dout("ys", [nseq, DEC, D]); kso = dout("ks", [nseq, DEC, 512]); vso = dout("vs", [nseq, DEC, 512])
        cso = dout("cs", [nseq, DEC, 256]); pso = dout("ps", [nseq, DEC, 64])
    wkv_scr = nc.dram_tensor("wkv_scr", [8, 128, 1344], BF16, kind="Internal").ap()

    es = ExitStack()
    with es:
        big = es.enter_context(nc.sbuf_tensor("big", [128, sbuf_kb * 256], F32))
        psb = [es.enter_context(nc.psum_tensor(f"psb{i}", [128, 512], F32)) for i in range(8)]
        P = Prog(nc)
        R = Carver(big, 0, sbuf_kb * 1024)
        wB = R.take(8 * 4096, BF16).rearrange("p (c n) -> p c n", c=8)
        Wabs = R.take(4 * 1024, BF16).rearrange("p (c n) -> p c n", c=4)
        wpe = R.take(4 * 256, BF16).rearrange("p (c n) -> p c n", c=4)
        wuv = R.take(2 * 512, BF16).rearrange("p (c n) -> p c n", c=2)
        woa = R.take(4 * 1024, BF16).rearrange("p (c n) -> p c n", c=4)
        wob = R.take(4 * 1024, BF16).rearrange("p (c n) -> p c n", c=4)
        wout = R.take(8 * 1024, BF16).rearrange("p (c n) -> p c n", c=8)
        KT = R.take(4 * NKEY, BF16).rearrange("p (h k) -> p h k", h=4)
        Vaug = R.take(NKT * 4 * 130, BF16).rearrange("p (j h e) -> p j h e", j=NKT, h=4)
        ckvT = R.take(2 * NKEY, BF16).rearrange("p (c k) -> p c k", c=2)
        ckvaug = R.take(NKT * 258, BF16).rearrange("p (j e) -> p j e", j=NKT)
        kpeT = R.take(NKEY, BF16)
        ident = R.take(128, BF16)
        ropeT = R.take(2 * NKT * 32, F32).rearrange("p (a j e) -> p a j e", a=2, j=NKT)
        gkva = R.take(256, F32); gsub8 = R.take(128, F32); gfin = R.take(D, F32)
        ginT = R.take(8, F32); gqaT = R.take(4, F32)
        rstd_all = R.take(NKT, F32)
        sc_c = R.take(16, F32)
        mhalf = sc_c[:, 0:1]; nlam = sc_c[:, 1:2]; ld1 = sc_c[:, 2:3]; ld2 = sc_c[:, 3:4]
        U0 = R.off
        nc._u0 = U0
        ULIM = sbuf_kb * 1024

        cnt = {"ps": 0, "s": 0, "pt": 0, "acc": 0, "pa": 0, "ob": 0, "attn": 0}

        def proj_bank():
            if cnt["attn"]:
                return 0
            k = cnt["ps"] % 2
            cnt["ps"] += 1
            return k

        def s_bank():
            k = (2, 3, 1)[cnt["s"] % 3]
            cnt["s"] += 1
            return k

        def bfv(k):
            return psb[k][:, :].bitcast(BF16)

        evac_rr = [0]

        def copy(out, in_, reads, writes, eng=None):
            if eng is None:
                eng = ("dve", "act")[evac_rr[0] % 2]
                evac_rr[0] += 1
            if eng == "act":
                P.op("act", lambda e: e.activation(out=out, in_=in_, func=AF.Copy), reads, writes)
            elif eng == "dve":
                P.op("dve", lambda e: e.tensor_copy(out=out, in_=in_), reads, writes)
            else:
                P.op("pool", lambda e: e.tensor_copy(out=out, in_=in_), reads, writes)

        def scaled(out, in_, sc, reads, writes, eng):
            if eng == "act":
                P.op("act", lambda e: e.activation(out=out, in_=in_, func=AF.Copy, scale=sc), reads, writes)
            else:
                P.op("dve", lambda e: e.tensor_scalar(out=out, in0=in_, scalar1=sc, scalar2=None, op0=ALU.mult), reads, writes)

        def dma(out, in_, reads, writes, key, q="sp"):
            P.op(q, lambda e: e.dma_start(out=out, in_=in_), reads, writes, dma=True, semkey=key)

        def transposes(srcs, n, bank, rd):
            bv = bfv(bank)
            for k, src in enumerate(srcs):
                m = src.shape[1]
                P.op("pe", lambda e, k=k, src=src, m=m: e.transpose(out=bv[0:m, k * 128:k * 128 + n], in_=src, identity=ident[0:n, 0:n]),
                     reads=list(rd) + ["ident"], writes=[("ps", bank)])

        def mm_group(bank, col0, ncols, n, lhs_list, rhs_list, rd):
            nk = len(lhs_list)
            for k in range(nk):
                P.op("pe", lambda e, k=k: e.matmul(psb[bank][0:n, col0:col0 + ncols], lhsT=lhs_list[k], rhs=rhs_list[k],
                                                    start=(k == 0), stop=(k == nk - 1)),
                     reads=rd, writes=[("ps", bank)])

        def rsqrt_chain(ssq, mul, out, nm, n, extra=None):
            P.op("dve", lambda e: e.tensor_scalar(out=out[0:n], in0=ssq[0:n], scalar1=mul, scalar2=EPS, op0=ALU.mult, op1=ALU.add),
                 reads=[nm + "_ssq"], writes=[nm])
            P.op("pool", lambda e: e.tensor_tensor(out=out[0:n], in0=out[0:n], in1=mhalf[0:n], op=ALU.pow),
                 reads=[nm, "consts"], writes=[nm])

        dma(ident, ident_d, [], ["ident"], "c0")
        dma(ropeT, rope_d.rearrange("a p j e -> p a j e"), [], ["rope"], "c1")
        dma(gkva, g_kva.partition_broadcast(128), [], ["gkva"], "c2")
        dma(gsub8, g_sub.partition_broadcast(128), [], ["gsub8"], "c3")
        dma(gfin, g_fin.partition_broadcast(128), [], ["gfin"], "c4")
        dma(ginT, g_inT, [], ["ginT"], "c5")
        dma(gqaT, g_qaT, [], ["gqaT"], "c6")
        P.op("pool", lambda e: e.memset(sc_c, -0.5), [], ["consts"])
        P.op("dve", lambda e: e.tensor_scalar(out=gsub8, in0=gsub8, scalar1=OSC, scalar2=None, op0=ALU.mult), ["gsub8"], ["gsub8"])
        P.op("pool", lambda e: e.memset(Vaug, 1.0), [], ["Vaug"])
        P.op("pool", lambda e: e.memset(ckvaug, 1.0), [], ["ckvaug"])
        P.op("pool", lambda e: e.memset(kpeT, 0.0), [], ["kpeT"])

        U = Carver(big, U0, ULIM)
        stage = [U.take(2880, F32) for _ in range(2)]
        wkv16 = U.take(1344, BF16)
        lamj = U.take(64, F32)
        lamt = U.take(4 * 64, F32).rearrange("p (a e) -> p a e", a=4)
        for a in range(4):
            dma(lamt[:, a, :], lam4[a:a + 1, :].partition_broadcast(128), [], ["lamt"], "c7")
        P.op("dve", lambda e: e.scalar_tensor_tensor(out=lamj, in0=lamt[:, 0, :], scalar=1.0, in1=lamt[:, 1, :], op0=ALU.mult, op1=ALU.mult, accum_out=ld1),
             ["lamt", "consts"], ["lamj", "ld1"])
        P.op("dve", lambda e: e.scalar_tensor_tensor(out=lamj, in0=lamt[:, 2, :], scalar=1.0, in1=lamt[:, 3, :], op0=ALU.mult, op1=ALU.mult, accum_out=ld2),
             ["lamt", "ld1"], ["lamj", "ld2"])
        P.op("act", lambda e: e.activation(out=ld1, in_=ld1, func=AF.Exp), ["ld1"], ["ld1"])
        P.op("act", lambda e: e.activation(out=ld2, in_=ld2, func=AF.Exp), ["ld2"], ["ld2"])
        P.op("dve", lambda e: e.scalar_tensor_tensor(out=nlam, in0=ld2, scalar=-LAM_INIT, in1=ld1, op0=ALU.add, op1=ALU.subtract),
             ["ld1", "ld2"], ["nlam"])
        segs0 = [(0, 512, 0), (2048, 2560, 512), (1536, 2048, 1024)]
        segs1 = [(2880, 3392, 1536), (3392, 5440, 2048)]
        for c in range(8):
            gc = ginT[:, c:c + 1]
            dma(stage[0], w_in[c * 128:(c + 1) * 128, 0:2880], [], [("stage", 0)], ("stg", 0))
            for si, (a, b, o) in enumerate(segs0):
                scaled(wB[:, c, o:o + (b - a)], stage[0][:, a:b], gc, [("stage", 0), "ginT"], ["wB"], ("act", "dve")[si % 2])
            scaled(wkv16[:, 0:1024], stage[0][:, 512:1536], gc, [("stage", 0), "ginT"], ["wkv16"], "dve")
            scaled(wkv16[:, 1024:1344], stage[0][:, 2560:2880], gc, [("stage", 0), "ginT"], ["wkv16"], "act")
            dma(wkv_scr[c], wkv16, ["wkv16"], ["wkv_scr"], "wkvo")
            dma(stage[1][:, 0:2560], w_in[c * 128:(c + 1) * 128, 2880:5440], [], [("stage", 1)], ("stg", 1))
            for si, (a, b, o) in enumerate(segs1):
                scaled(wB[:, c, o:o + (b - a)], stage[1][:, a - 2880:b - 2880], gc, [("stage", 1), "ginT"], ["wB"], ("dve", "act")[si % 2])
        P.barrier_all()

        U = Carver(big, U0, ULIM)
        st2 = [U.take(1024, F32) for _ in range(2)]
        wuqn16 = U.take(4 * 512, BF16).rearrange("p (c n) -> p c n", c=4)
        wuk16 = U.take(2 * 512, BF16).rearrange("p (c n) -> p c n", c=2)
        wukT = U.take(4 * 256, BF16).rearrange("p (h n) -> p h n", h=4)
        wuqT = U.take(4 * 512, BF16).rearrange("p (h n) -> p h n", h=4)
        k2 = [0]

        def stage2(src, cols):
            sb = k2[0] % 2
            k2[0] += 1
            dma(st2[sb][:, 0:cols], src, [], [("st2", sb)], ("st2", sb))
            return st2[sb], ("st2", sb)

        for c in range(4):
            st, r = stage2(w_uq[c * 128:(c + 1) * 128, :], 768)
            s3 = st[:, 0:768].rearrange("p (h e) -> p h e", h=4)
            scaled(wuqn16[:, c, :].rearrange("p (h e) -> p h e", h=4), s3[:, :, 0:128], gqaT[:, c:c + 1], [r, "gqaT"], ["wuqn16"], "dve")
            scaled(wpe[:, c, :].rearrange("p (h e) -> p h e", h=4), s3[:, :, 128:192], gqaT[:, c:c + 1], [r, "gqaT"], ["wpe"], "act")
        for c in range(2):
            st, r = stage2(w_uk[c * 128:(c + 1) * 128, :], 512)
            copy(wuk16[:, c, :], st[:, 0:512], [r], ["wuk16"])
            st, r = stage2(w_uv[c * 128:(c + 1) * 128, :], 512)
            copy(wuv[:, c, :], st[:, 0:512], [r], ["wuv"])
        for (wsrc, wdst, nm, nch) in ((w_oa, woa, "woa", 4), (w_ob, wob, "wob", 4), (w_out, wout, "wout", 8)):
            for c in range(nch):
                st, r = stage2(wsrc[c * 128:(c + 1) * 128, :], 1024)
                scaled(wdst[:, c, :], st[:, 0:1024], 0.5, [r], [nm], ("act", "dve")[c % 2])
        for h in range(4):
            b = proj_bank()
            transposes([wuk16[:, cc, h * 128:(h + 1) * 128] for cc in range(2)], 128, b, ["wuk16"])
            copy(wukT[:, h, :], bfv(b)[:, 0:256], [("ps", b)], ["wukT"])
        for h in range(4):
            b = proj_bank()
            transposes([wuqn16[:, cc, h * 128:(h + 1) * 128] for cc in range(4)], 128, b, ["wuqn16"])
            copy(wuqT[:, h, :], bfv(b)[:, 0:512], [("ps", b)], ["wuqT"])
        for cc in range(4):
            for hp in range(2):
                b = proj_bank()
                for hh in range(2):
                    h = hp * 2 + hh
                    mm_group(b, hh * 256, 256, 128, [wuqT[:, h, cc * 128:(cc + 1) * 128]], [wukT[:, h, :]], ["wuqT", "wukT"])
                copy(Wabs[:, cc, hp * 512:(hp + 1) * 512], psb[b][:, 0:512], [("ps", b)], ["Wabs"])
        P.barrier_all()

        U = Carver(big, U0, ULIM)
        x32 = [U.take(D, F32) for _ in range(2)]
        ropeA = U.take(512, F32); ropeB = U.take(512, F32)
        scal = U.take(64, F32)
        xbf = U.take(D, BF16); hT = U.take(D, BF16); junk = U.take(D, BF16)
        UA0 = U.off
        oh32 = U.take(128, F32)
        qa16 = U.take(512, BF16); qd16 = U.take(512, BF16); qdT = U.take(512, BF16)
        zz = U.take(1024, BF16); tgs = [U.take(2048, BF16) for _ in range(2)]
        qlat16 = U.take(1024, BF16); qpe16 = U.take(512, BF16)
        QlT = U.take(1024, BF16)
        PT = [U.take(384, BF16) for _ in range(3)]
        ob16 = U.take(512, BF16); obT = U.take(512, BF16)
        m16 = U.take(1024, BF16); mT = U.take(1024, BF16)
        Q12z = U.take(1024, BF16).rearrange("p (h a q) -> p h a q", h=4, a=2); qpeT = U.take(512, BF16)

        def zero_kpe():
            P.op("pool", lambda e: e.memset(kpe16, 0.0), [], ["kpe16"])

        def zero_q():
            P.op("pool", lambda e: e.memset(qpe16, 0.0), [], ["qpe16"])
            P.op("pool", lambda e: e.memset(qpeT, 0.0), [], ["qpeT"])
            P.op("pool", lambda e: e.memset(Q12z, 0.0), [], ["Q12z"])
        oa16, oaT, olat16, olatT = qd16, qdT, qlat16, QlT
        rmall = scal[:, 32:36]
        UA = Carver(big, UA0, ULIM)
        wkv = UA.take(8 * 1344, BF16).rearrange("p (c n) -> p c n", c=8)
        kvout = UA.take(1344, F32)
        ka16 = UA.take(512, BF16); kpe16 = UA.take(128, BF16); ckv16 = UA.take(256, BF16)
        xcnt = [0]


        def S(i, n=128):
            return scal[0:n, i:i + 1]

        def rope(src, sc, ti, G, n, dst, rd, wr, dst_is_f32):
            W = G * 64
            cosb = ropeT[0:n, 0, ti, :].unsqueeze(1).broadcast_to([n, 2 * G, 32])
            sinb = ropeT[0:n, 1, ti, :].unsqueeze(1).broadcast_to([n, G, 32])
            nsc = S(23, n)
            P.op("dve", lambda e: e.tensor_scalar(out=nsc, in0=sc, scalar1=-1.0, scalar2=None, op0=ALU.mult), [r for r in rd if not (isinstance(r, tuple) and r[0] == "ps")], ["nsc"])
            g64 = lambda ap: ap.rearrange("p (g j) -> p g j", j=64)
            t1 = dst if dst_is_f32 else g64(ropeA[0:n, 0:W])
            t1n = wr if dst_is_f32 else ["ropeA"]
            v4 = lambda ap: ap.rearrange("p (g t j) -> p g t j", t=2, j=32)
            P.op("dve", lambda e: e.scalar_tensor_tensor(out=t1.rearrange("p g (t j) -> p (g t) j", j=32), in0=src.rearrange("p (g j) -> p g j", j=32),
                                                        scalar=sc, in1=cosb, op0=ALU.mult, op1=ALU.mult),
                 rd + ["rope"], t1n)
            t2 = ropeB[0:n, 0:W]
            P.op("dve", lambda e: e.scalar_tensor_tensor(out=v4(t2)[:, :, 0, :], in0=v4(src)[:, :, 1, :], scalar=nsc, in1=sinb, op0=ALU.mult, op1=ALU.mult),
                 rd + ["rope", "nsc"], ["ropeB"])
            P.op("dve", lambda e: e.scalar_tensor_tensor(out=v4(t2)[:, :, 1, :], in0=v4(src)[:, :, 0, :], scalar=sc, in1=sinb, op0=ALU.mult, op1=ALU.mult),
                 rd + ["rope", "ropeB"], ["ropeB"])
            P.op("pool", lambda e: e.tensor_tensor(out=dst, in0=t1, in1=g64(t2), op=ALU.add), t1n + ["ropeB"], wr)

        def load_x(src_ap, n):
            xb = xcnt[0] % 2
            xcnt[0] += 1
            dma(x32[xb][0:n, :], src_ap, [], [("x32", xb)], ("x", xb))
            return xb

        def norm_and_T(xb, n, want_rstd, ti, bank=None, evac="dve"):
            xa = x32[xb]
            if want_rstd:
                P.op("act", lambda e: e.activation(out=junk[0:n, :], in_=xa[0:n, :], func=AF.Square, accum_out=S(0, n)),
                     [("x32", xb)], ["junk", "rx_ssq"])
                rsqrt_chain(S(0), 1.0 / D, S(1), "rx", n)
                P.op("dve", lambda e: e.tensor_copy(out=rstd_all[0:n, ti:ti + 1], in_=S(1, n)), ["rx"], [("rstd", ti)])
            copy(xbf[0:n, :], xa[0:n, :], [("x32", xb)], ["xbf"], eng="act")
            b = proj_bank() if bank is None else bank
            transposes([xbf[0:n, c * 128:(c + 1) * 128] for c in range(8)], n, b, ["xbf"])
            copy(hT[:, :].rearrange("p (c t) -> p c t", c=8)[:, :, 0:n], bfv(b)[:, :].rearrange("p (c t) -> p c t", c=8)[:, :, 0:n],
                 [("ps", b)], ["hT"], eng=evac)

        def hT_c(c, n):
            return hT[:, c * 128:c * 128 + n]

        def bankA():
            k = cnt["pa"] % 8
            cnt["pa"] += 1
            return k

        def passA_part1(xb, n, ti):
            norm_and_T(xb, n, True, ti, bank=6)
            hl = [hT_c(c, n) for c in range(8)]
            k0 = 3 * (bankA() % 2)
            bc, bk_, bv = k0, k0 + 1, k0 + 2
            mm_group(bc, 0, 320, n, hl, [wkv[:, c, 1024:1344] for c in range(8)], ["hT", "wkv"])
            mm_group(bk_, 0, 512, n, hl, [wkv[:, c, 0:512] for c in range(8)], ["hT", "wkv"])
            mm_group(bv, 0, 512, n, hl, [wkv[:, c, 512:1024] for c in range(8)], ["hT", "wkv"])
            return (bc, bk_, bv)

        def passA_part2(banks, n, ti, kt, outs):
            ko, vo, co, po = outs
            bc, bk_, bv = banks
            rs = rstd_all[0:n, ti:ti + 1]
            rrs = [("rstd", ti)]
            g64 = lambda ap: ap.rearrange("p (g j) -> p g j", j=64)
            P.op("act", lambda e: e.activation(out=junk[0:n, 0:256], in_=psb[bc][0:n, 0:256], func=AF.Square, accum_out=S(2, n)),
                 [("ps", bc)], ["junk", "rc_ssq"])
            P.op("dve", lambda e: e.tensor_scalar(out=S(22, n), in0=rs, scalar1=rs, scalar2=1.0 / 256, op0=ALU.mult, op1=ALU.mult), rrs, ["tmp22"])
            P.op("dve", lambda e: e.tensor_scalar(out=S(3, n), in0=S(2, n), scalar1=S(22, n), scalar2=EPS, op0=ALU.mult, op1=ALU.add),
                 ["rc_ssq", "tmp22"], ["rc"])
            P.op("pool", lambda e: e.tensor_tensor(out=S(3, n), in0=S(3, n), in1=mhalf[0:n], op=ALU.pow), ["rc", "consts"], ["rc"])
            rope(psb[bk_][0:n, 0:512], rs, ti, 8, n, g64(kvout[0:n, 0:512]), [("ps", bk_)] + rrs, ["kv_k"], True)
            scaled(kvout[0:n, 512:1024], psb[bv][0:n, 0:512], rs, [("ps", bv)] + rrs, ["kv_v"], "act")
            copy(ka16[0:n, :], kvout[0:n, 0:512], ["kv_k"], ["ka16"], eng="act")
            dma(ko, kvout[0:n, 0:512], ["kv_k"], [], "oA0")
            dma(vo, kvout[0:n, 512:1024], ["kv_v"], [], "oA1")
            P.op("dve", lambda e: e.tensor_tensor(out=S(4, n), in0=S(3, n), in1=rs, op=ALU.mult), ["rc"] + rrs, ["sc_c"])
            P.op("dve", lambda e: e.scalar_tensor_tensor(out=kvout[0:n, 1024:1280], in0=psb[bc][0:n, 0:256], scalar=S(4, n), in1=gkva[0:n, :],
                                                        op0=ALU.mult, op1=ALU.mult), [("ps", bc), "sc_c", "gkva"], ["kv_c"])
            copy(ckv16[0:n, :], kvout[0:n, 1024:1280], ["kv_c"], ["ckv16"], eng="act")
            dma(co, kvout[0:n, 1024:1280], ["kv_c"], [], "oA2")
            rope(psb[bc][0:n, 256:320], rs, ti, 1, n, g64(kvout[0:n, 1280:1344]), [("ps", bc)] + rrs, ["kv_p"], True)
            copy(kpe16[0:n, 0:64], kvout[0:n, 1280:1344], ["kv_p"], ["kpe16"], eng="act")
            dma(po, kvout[0:n, 1280:1344], ["kv_p"], [], "oA3")
            P.op("pool", lambda e: e.tensor_copy(out=Vaug[0:n, kt, :, 0:128], in_=kvout[0:n, 512:1024].rearrange("p (h e) -> p h e", h=4)),
                 ["kv_v"], ["Vaug"])
            P.op("pool", lambda e: e.tensor_copy(out=ckvaug[0:n, kt, 0:256], in_=ckv16[0:n, :]), ["ckv16"], ["ckvaug"])
            b = 7
            transposes([ka16[0:n, h * 128:(h + 1) * 128] for h in range(4)], n, b, ["ka16"])
            copy(KT[:, :, kt * 128:kt * 128 + n], bfv(b)[:, 0:512].rearrange("p (h t) -> p h t", h=4)[:, :, 0:n], [("ps", b)], ["KT"], eng="dve")
            b = 6
            transposes([ckv16[0:n, c * 128:(c + 1) * 128] for c in range(2)] + [kpe16[0:n, :]], n, b, ["ckv16", "kpe16"])
            copy(ckvT[:, :, kt * 128:kt * 128 + n], bfv(b)[:, 0:256].rearrange("p (c t) -> p c t", c=2)[:, :, 0:n], [("ps", b)], ["ckvT"], eng="dve")
            copy(kpeT[:, kt * 128:kt * 128 + n], bfv(b)[:, 256:256 + n], [("ps", b)], ["kpeT"], eng="dve")

        def passA_seq(items, pre=None):
            xbs = dict(pre or {})
            for j in range(min(2, len(items))):
                if j not in xbs:
                    xbs[j] = load_x(items[j][0], items[j][1])
            banks = passA_part1(xbs[0], items[0][1], items[0][2])
            for i, (xsrc, n, ti, kt, outs) in enumerate(items):
                if i + 2 < len(items):
                    xbs[i + 2] = load_x(items[i + 2][0], items[i + 2][1])
                nb = None
                if i + 1 < len(items):
                    nb = passA_part1(xbs[i + 1], items[i + 1][1], items[i + 1][2])
                passA_part2(banks, n, ti, kt, outs)
                banks = nb

        def cache_seq(s):
            UC = Carver(big, UA0, ULIM)
            stg = [UC.take(1344, F32) for _ in range(3)]
            k16 = [UC.take(512, BF16) for _ in range(2)]
            c16 = [UC.take(256, BF16) for _ in range(2)]
            p16 = [UC.take(128, BF16) for _ in range(2)]
            for k in range(2):
                P.op("pool", lambda e, k=k: e.memset(p16[k], 0.0), [], [("p16", k)])

            def loads(kt):
                c0 = stg[kt % 3]
                r = slice(kt * 128, (kt + 1) * 128)
                q = kt % 3
                dma(c0[:, 0:512], ck[s, r, :], [], [("cstg", q, 0)], ("cA0", q))
                dma(c0[:, 512:1024], cv[s, r, :], [], [("cstg", q, 1)], ("cA1", q))
                dma(c0[:, 1024:1280], cckv[s, r, :], [], [("cstg", q, 2)], ("cA2", q))
                dma(c0[:, 1280:1344], cpe[s, r, :], [], [("cstg", q, 3)], ("cA3", q))

            loads(0)
            loads(1)
            for kt in range(16):
                if kt + 2 < 16:
                    loads(kt + 2)
                q, w = kt % 3, kt % 2
                c0 = stg[q]
                copy(k16[w][:, :], c0[:, 0:512], [("cstg", q, 0)], [("k16", w)], eng="act")
                copy(c16[w][:, :], c0[:, 1024:1280], [("cstg", q, 2)], [("c16", w)], eng="dve")
                copy(p16[w][:, 0:64], c0[:, 1280:1344], [("cstg", q, 3)], [("p16", w)], eng="act")
                v3 = c0[:, 512:1024].rearrange("p (h e) -> p h e", h=4)
                P.op("dve", lambda e, kt=kt, v3=v3: e.tensor_copy(out=Vaug[:, kt, 0:2, 0:128], in_=v3[:, 0:2, :]), [("cstg", q, 1)], [("Vaug", kt, 0)])
                P.op("pool", lambda e, kt=kt, v3=v3: e.tensor_copy(out=Vaug[:, kt, 2:4, 0:128], in_=v3[:, 2:4, :]), [("cstg", q, 1)], [("Vaug", kt, 1)])
                P.op("pool", lambda e, kt=kt, w=w: e.tensor_copy(out=ckvaug[:, kt, 0:256], in_=c16[w][:, :]), [("c16", w)], ["ckvaug"])
                b = bankA()
                transposes([k16[w][:, h * 128:(h + 1) * 128] for h in range(4)], 128, b, [("k16", w)])
                copy(KT[:, :, kt * 128:(kt + 1) * 128], bfv(b)[:, 0:512].rearrange("p (h t) -> p h t", h=4), [("ps", b)], ["KT"], eng="dve")
                b = bankA()
                transposes([c16[w][:, c * 128:(c + 1) * 128] for c in range(2)] + [p16[w][:, :]], 128, b, [("c16", w), ("p16", w)])
                copy(ckvT[:, :, kt * 128:(kt + 1) * 128], bfv(b)[:, 0:256].rearrange("p (c t) -> p c t", c=2), [("ps", b)], ["ckvT"], eng="act")
                copy(kpeT[:, kt * 128:(kt + 1) * 128], bfv(b)[:, 256:384], [("ps", b)], ["kpeT"], eng="act")

        def wcols(a, b_):
            return [wB[:, c, a:b_] for c in range(8)]

        class Blk:
            pass

        def pB_load(bk):
            bk.xb = load_x(bk.xsrc, bk.n)

        def pB_prologueA(bk):
            n, ti = bk.n, bk.ti
            norm_and_T(bk.xb, n, False, ti, evac="act")
            bk.rs = rstd_all[0:n, ti:ti + 1]
            bk.rrs = [("rstd", ti)]
            rs, rrs = bk.rs, bk.rrs
            P.op("dve", lambda e: e.tensor_scalar(out=S(7, n), in0=rs, scalar1=DIFF_SCALE, scalar2=None, op0=ALU.mult), rrs, ["sq8"])
            P.op("dve", lambda e: e.tensor_scalar(out=S(8, n), in0=rs, scalar1=0.5, scalar2=None, op0=ALU.mult), rrs, ["hrs"])
            bk.hl = [hT_c(c, n) for c in range(8)]

        def pB_prologueB(bk):
            n, ti = bk.n, bk.ti
            b = proj_bank()
            mm_group(b, 0, 512, n, bk.hl, wcols(0, 512), ["hT", "wB"])
            rope(psb[b][0:n, 0:512], S(7, n), ti, 8, n, qa16[0:n, :].rearrange("p (g j) -> p g j", j=64), [("ps", b), "sq8"], ["qa16"], False)

        def pB_gate(bk, gi, b):
            n = bk.n
            tg = tgs[bk.par]
            mm_group(b, 0, 512, n, bk.hl, wcols(2048 + gi * 512, 2560 + gi * 512), ["hT", "wB"])
            P.op("act", lambda e: e.activation(out=tg[0:n, gi * 512:(gi + 1) * 512], in_=psb[b][0:n, 0:512], func=AF.Tanh, scale=S(8, n)),
                 [("ps", b), "hrs"], [("tg", bk.par, gi)])

        def pB_stageP(bk):
            n, ti, rs, rrs, hl = bk.n, bk.ti, bk.rs, bk.rrs, bk.hl
            b = proj_bank()
            mm_group(b, 0, 512, n, hl, wcols(512, 1024), ["hT", "wB"])
            P.op("act", lambda e, b=b: e.activation(out=xbf[0:n, 0:512], in_=psb[b][0:n, 0:512], func=AF.Square, accum_out=S(5, n)),
                 [("ps", b)], ["xbf", "rq_ssq"])
            copy(qd16[0:n, :], psb[b][0:n, 0:512], [("ps", b)], ["qd16"], eng="dve")
            P.op("dve", lambda e: e.tensor_scalar(out=S(22, n), in0=rs, scalar1=rs, scalar2=1.0 / 512, op0=ALU.mult, op1=ALU.mult), rrs, ["tmp22"])
            P.op("dve", lambda e: e.tensor_scalar(out=S(6, n), in0=S(5, n), scalar1=S(22, n), scalar2=EPS, op0=ALU.mult, op1=ALU.add),
                 ["rq_ssq", "tmp22"], ["sq"])
            P.op("pool", lambda e: e.tensor_tensor(out=S(6, n), in0=S(6, n), in1=mhalf[0:n], op=ALU.pow), ["sq", "consts"], ["sq"])
            P.op("dve", lambda e: e.tensor_scalar(out=S(6, n), in0=S(6, n), scalar1=rs, scalar2=MLA_SCALE, op0=ALU.mult, op1=ALU.mult),
                 ["sq"] + rrs, ["sq"])

            yield
            if not bk.gates_done:
                for gi in range(4):
                    pB_gate(bk, gi, proj_bank())
            yield
            b = proj_bank()
            transposes([qd16[0:n, c * 128:(c + 1) * 128] for c in range(4)], n, b, ["qd16"])
            copy(qdT[:, :].rearrange("p (c t) -> p c t", c=4)[:, :, 0:n], bfv(b)[:, 0:512].rearrange("p (c t) -> p c t", c=4)[:, :, 0:n],
                 [("ps", b)], ["qdT"], eng="dve")
            yield
            ql = [qdT[:, c * 128:c * 128 + n] for c in range(4)]
            for half in range(2):
                b = proj_bank()
                mm_group(b, 0, 512, n, ql, [Wabs[:, c, half * 512:(half + 1) * 512] for c in range(4)], ["qdT", "Wabs"])
                scaled(qlat16[0:n, half * 512:(half + 1) * 512], psb[b][0:n, 0:512], S(6, n), [("ps", b), "sq"], ["qlat16"], ("act", "dve")[half])
            yield
            b = proj_bank()
            mm_group(b, 0, 256, n, ql, [wpe[:, c, :] for c in range(4)], ["qdT", "wpe"])
            rope(psb[b][0:n, 0:256], S(6, n), ti, 4, n, qpe16[0:n, :].rearrange("p (h e) -> p h e", h=4)[:, :, 0:64], [("ps", b), "sq"], ["qpe16"], False)
            yield
            v3 = lambda ap: ap.rearrange("p (h t) -> p h t", h=4)[:, :, 0:n]
            b = proj_bank()
            transposes([qa16[0:n, h * 128:(h + 1) * 128] for h in range(4)], n, b, ["qa16"])
            copy(Q12z[0:64, :, 0, 0:n], v3(bfv(b)[0:64, 0:512]), [("ps", b)], ["Q12z"], eng="dve")
            copy(Q12z[64:128, :, 1, 0:n], v3(bfv(b)[64:128, 0:512]), [("ps", b)], ["Q12z"], eng="dve")
            yield
            for zi in range(2):
                b = proj_bank()
                mm_group(b, 0, 512, n, hl, wcols(1024 + zi * 512, 1536 + zi * 512), ["hT", "wB"])
                P.op("act", lambda e, b=b: e.activation(out=xbf[0:n, 512:1024], in_=psb[b][0:n, 0:512], func=AF.Tanh, scale=S(8, n)),
                     [("ps", b), "hrs"], ["xbf"])
                P.op("dve", lambda e, b=b, zi=zi: e.scalar_tensor_tensor(out=zz[0:n, zi * 512:(zi + 1) * 512], in0=xbf[0:n, 512:1024], scalar=1.0,
                                                                        in1=psb[b][0:n, 0:512], op0=ALU.add, op1=ALU.mult),
                     [("ps", b), "xbf"], [("zz", zi)])
                if zi == 0:
                    P.op("dve", lambda e: e.tensor_tensor(out=zz[0:n, 0:512].rearrange("p (h e) -> p h e", h=4), in0=zz[0:n, 0:512].rearrange("p (h e) -> p h e", h=4),
                                                         in1=gsub8[0:n, :].unsqueeze(1).broadcast_to([n, 4, 128]), op=ALU.mult), [("zz", 0), "gsub8"], [("zz", 0)])
            yield
            b = proj_bank()
            transposes([qlat16[0:n, c * 128:(c + 1) * 128] for c in range(8)], n, b, ["qlat16"])
            copy(QlT[:, :].rearrange("p (c t) -> p c t", c=8)[:, :, 0:n], bfv(b)[:, :].rearrange("p (c t) -> p c t", c=8)[:, :, 0:n],
                 [("ps", b)], [("QlT", h_) for h_ in range(4)], eng="dve")
            yield
            b = proj_bank()
            transposes([qpe16[0:n, h * 128:(h + 1) * 128] for h in range(4)], n, b, ["qpe16"])
            copy(v3(qpeT[:, :]), v3(bfv(b)[:, 0:512]), [("ps", b)], ["qpeT"], eng="act")
            yield

        def pB_attention(bk, nxt):
            n, ti, rs, rrs, ktiles, diag = bk.n, bk.ti, bk.rs, bk.rrs, bk.ktiles, bk.diag
            nj = len(ktiles)
            tiles = [(h, jj, kt, nk) for h in range(4) for jj, (kt, nk) in enumerate(ktiles)]
            st = {}
            pending = []

            def emit_S(t):
                h, jj, kt, nk = tiles[t]
                sb = s_bank()
                pb = cnt["pt"] % 3
                cnt["pt"] += 1
                pt = PT[pb]
                ks = slice(kt * 128, kt * 128 + nk)
                qs = slice(h * 128, h * 128 + n)
                if n == 128:
                    P.op("pe", lambda e: e.matmul(psb[sb][0:nk, 0:256], lhsT=KT[:, h, ks], rhs=Q12z[:, h, :, :].rearrange("p a q -> p (a q)"),
                                                  start=True, stop=True), ["KT", "Q12z"], [("ps", sb)])
                else:
                    mm_group(sb, 0, n, nk, [KT[:, h, ks]], [Q12z[:, h, 0, 0:n]], ["KT", "Q12z"])
                    mm_group(sb, 128, n, nk, [KT[:, h, ks]], [Q12z[:, h, 1, 0:n]], ["KT", "Q12z"])
                mm_group(sb, 256, n, nk, [ckvT[:, 0, ks], ckvT[:, 1, ks], kpeT[:, ks]],
                         [QlT[:, (2 * h) * 128:(2 * h) * 128 + n], QlT[:, (2 * h + 1) * 128:(2 * h + 1) * 128 + n], qpeT[:, qs]],
                         ["ckvT", "kpeT", ("QlT", h), "qpeT"])
                P.op("act", lambda e: e.activation(
                    out=pt[0:nk, :].rearrange("p (a q) -> p a q", a=3)[:, :, 0:n],
                    in_=psb[sb][0:nk, 0:384].rearrange("p (a q) -> p a q", a=3)[:, :, 0:n], func=AF.Exp),
                    [("ps", sb)], [("PT", pb)])
                if diag and jj == nj - 1:
                    P.op("pool", lambda e: e.memset(pt[64:128, :].rearrange("p (a q) -> p a q", a=3)[:, :, 0:64], 0.0),
                         [("PT", pb)], [("PT", pb)])
                st[t] = (pt, pb)

            def emit_PV(t):
                h, jj, kt, nk = tiles[t]
                pt, pb = st.pop(t)
                if jj == 0:
                    cnt["acc"] += 1
                aset = cnt["acc"] % 2
                b12, bm = 4 + 2 * aset, 5 + 2 * aset
                first, last = (jj == 0), (jj == nj - 1)
                va = Vaug[0:nk, kt, h, 0:129]
                P.op("pe", lambda e: e.matmul(psb[b12][0:n, 0:129], lhsT=pt[0:nk, 0:n], rhs=va, start=first, stop=last),
                     [("PT", pb), "Vaug"], [("ps", b12)])
                P.op("pe", lambda e: e.matmul(psb[b12][0:n, 129:258], lhsT=pt[0:nk, 128:128 + n], rhs=va, start=False, stop=last, skip_group_check=True),
                     [("PT", pb), "Vaug"], [("ps", b12)])
                P.op("pe", lambda e: e.matmul(psb[bm][0:n, 0:257], lhsT=pt[0:nk, 256:256 + n], rhs=ckvaug[0:nk, kt, 0:257], start=first, stop=last),
                     [("PT", pb), "ckvaug"], [("ps", bm)])
                if last:
                    epilogue(h, b12, bm)
                    pending.append((t + 7, lambda h=h: head_post(h)))

            def epilogue(h, b12, bm):
                A = psb[b12]
                P.op("dve", lambda e: e.reciprocal(out=S(17, n), in_=A[0:n, 128:129]), [("ps", b12)], ["r1"])
                P.op("dve", lambda e: e.reciprocal(out=S(18, n), in_=A[0:n, 257:258]), [("ps", b12)], ["r2"])
                P.op("dve", lambda e: e.tensor_tensor(out=S(19, n), in0=S(18, n), in1=nlam[0:n], op=ALU.mult), ["r2", "nlam"], ["nl2"])
                P.op("dve", lambda e: e.tensor_scalar(out=oh32[0:n, :], in0=A[0:n, 0:128], scalar1=S(17, n), scalar2=None, op0=ALU.mult),
                     [("ps", b12), "r1"], ["oh32"])
                P.op("dve", lambda e: e.scalar_tensor_tensor(out=oh32[0:n, :], in0=A[0:n, 129:257], scalar=S(19, n), in1=oh32[0:n, :],
                                                            op0=ALU.mult, op1=ALU.add), [("ps", b12), "nl2", "oh32"], ["oh32"])
                P.op("dve", lambda e: e.reciprocal(out=rmall[0:n, h:h + 1], in_=psb[bm][0:n, 256:257]), [("ps", bm)], ["rm"])
                copy(olat16[0:n, h * 256:(h + 1) * 256], psb[bm][0:n, 0:256], [("ps", bm)], [("olat16", h), "qlat16"], eng="dve")
                P.op("dve", lambda e: e.tensor_tensor(out=rmall[0:n, h:h + 1], in0=rmall[0:n, h:h + 1], in1=rs, op=ALU.mult), ["rm"] + rrs, ["rm"])
                P.op("dve", lambda e: e.scalar_tensor_tensor(out=junk[0:n, 0:128], in0=oh32[0:n, :], scalar=1.0, in1=oh32[0:n, :],
                                                            op0=ALU.mult, op1=ALU.mult, accum_out=S(9, n)), ["oh32"], ["junk", "ro_ssq"])
                rsqrt_chain(S(9), 1.0 / 128, S(13), "ro", n)
                P.op("dve", lambda e: e.tensor_tensor(out=S(13, n), in0=S(13, n), in1=rs, op=ALU.mult), ["ro"] + rrs, ["ro"])
                P.op("dve", lambda e: e.scalar_tensor_tensor(out=oa16[0:n, h * 128:(h + 1) * 128], in0=oh32[0:n, :], scalar=S(13, n), in1=zz[0:n, h * 128:(h + 1) * 128],
                                                            op0=ALU.mult, op1=ALU.mult), ["oh32", "ro", ("zz", 0)], [("oa16", h), "qd16"])

            def head_post(h):
                b = proj_bank()
                transposes([oa16[0:n, h * 128:(h + 1) * 128], olat16[0:n, (2 * h) * 128:(2 * h + 1) * 128], olat16[0:n, (2 * h + 1) * 128:(2 * h + 2) * 128]],
                           n, b, [("oa16", h), ("olat16", h), "qd16", "qlat16"])
                copy(oaT[:, h * 128:h * 128 + n], bfv(b)[:, 0:n], [("ps", b)], [("oaT", h), "qdT"], eng="dve")
                copy(olatT[:, (2 * h) * 128:(2 * h + 2) * 128].rearrange("p (c t) -> p c t", c=2)[:, :, 0:n],
                     bfv(b)[:, 128:384].rearrange("p (c t) -> p c t", c=2)[:, :, 0:n], [("ps", b)], [("olatT", h), ("QlT", h)], eng="dve")
                b = proj_bank()
                mm_group(b, 0, 128, n, [olatT[:, (2 * h + c) * 128:(2 * h + c) * 128 + n] for c in range(2)],
                         [wuv[:, c, h * 128:(h + 1) * 128] for c in range(2)], [("olatT", h), ("QlT", h), "wuv"])
                P.op("dve", lambda e, b=b: e.scalar_tensor_tensor(out=ob16[0:n, h * 128:(h + 1) * 128], in0=psb[b][0:n, 0:128],
                                                                 scalar=rmall[0:n, h:h + 1], in1=zz[0:n, 512 + h * 128:512 + (h + 1) * 128],
                                                                 op0=ALU.mult, op1=ALU.mult), [("ps", b), "rm", ("zz", 1)], [("ob16", h)])

            cnt["attn"] = 1
            emit_S(0)
            if len(tiles) > 1:
                emit_S(1)
            tA = (len(tiles) * 5) // 8
            for t in range(len(tiles)):
                if t + 2 < len(tiles):
                    emit_S(t + 2)
                emit_PV(t)
                if t == tA and nxt is not None:
                    pB_prologueA(nxt)
                for (at, fn) in [p for p in pending if p[0] <= t]:
                    fn()
                pending[:] = [p for p in pending if p[0] > t]
            bk.pending = [fn for (_, fn) in pending]
            cnt["attn"] = 0

        def obank():
            k = cnt["ob"] % 4
            cnt["ob"] += 1
            return k

        def pB_stageO(bk, nxt):
            n, ti = bk.n, bk.ti
            xb = bk.xb
            if nxt is not None:
                pB_prologueB(nxt)
            for fn in bk.pending:
                fn()
            oaS = [("oaT", h) for h in range(4)]
            junk32 = junk[:, :].bitcast(F32)
            tg = tgs[bk.par]
            tmps = ((ropeA, ["ropeA"], ropeB, ["ropeB"]), (junk32, ["junk", "junk2"], ropeA, ["ropeA"]))

            def ngate(gi):
                if nxt is not None:
                    pB_gate(nxt, gi, obank())
                    nxt.gates_done = True
            for half in range(2):
                cs = slice(half * 512, (half + 1) * 512)
                b = obank()
                mm_group(b, 0, 512, n, [oaT[:, c * 128:c * 128 + n] for c in range(4)], [woa[:, c, cs] for c in range(4)], oaS + ["qdT", "woa"])
                ta, tak, tb, tbk = tmps[half]
                P.op("dve", lambda e, b=b, cs=cs, ta=ta: e.scalar_tensor_tensor(out=ta[0:n, :], in0=tg[0:n, cs], scalar=1.0, in1=psb[b][0:n, 0:512],
                                                                               op0=ALU.add, op1=ALU.mult), [("ps", b), ("tg", bk.par, half)], tak)
                ngate(half)
                if half == 0:
                    b = obank()
                    transposes([ob16[0:n, c * 128:(c + 1) * 128] for c in range(4)], n, b, [("ob16", h) for h in range(4)])
                    copy(obT[:, :].rearrange("p (c t) -> p c t", c=4)[:, :, 0:n], bfv(b)[:, 0:512].rearrange("p (c t) -> p c t", c=4)[:, :, 0:n],
                         [("ps", b)], ["obT"], eng="act")
            yield
            for half in range(2):
                cs = slice(half * 512, (half + 1) * 512)
                ta, tak, tb, tbk = tmps[half]
                b = obank()
                mm_group(b, 0, 512, n, [obT[:, c * 128:c * 128 + n] for c in range(4)], [wob[:, c, cs] for c in range(4)], ["obT", "wob"])
                P.op("dve", lambda e, b=b, half=half, tb=tb: e.scalar_tensor_tensor(out=tb[0:n, :], in0=tg[0:n, 1024 + half * 512:1536 + half * 512], scalar=1.0,
                                                                                   in1=psb[b][0:n, 0:512], op0=ALU.add, op1=ALU.mult),
                     [("ps", b), ("tg", bk.par, 2 + half)], tbk)
                P.op("dve", lambda e, cs=cs, ta=ta, tb=tb: e.tensor_tensor(out=m16[0:n, cs], in0=ta[0:n, :], in1=tb[0:n, :], op=ALU.add), tak + tbk, [("m16", half)])
                ngate(2 + half)
                b = obank()
                transposes([m16[0:n, c * 128:(c + 1) * 128] for c in range(4 * half, 4 * half + 4)], n, b, [("m16", half)])
                copy(mT[:, half * 512:(half + 1) * 512].rearrange("p (c t) -> p c t", c=4)[:, :, 0:n], bfv(b)[:, 0:512].rearrange("p (c t) -> p c t", c=4)[:, :, 0:n],
                     [("ps", b)], [("mT", half)], eng="act")
                yield
            xa = x32[xb]
            for half in range(2):
                cs = slice(half * 512, (half + 1) * 512)
                b = obank()
                mm_group(b, 0, 512, n, [mT[:, c * 128:c * 128 + n] for c in range(8)], [wout[:, c, cs] for c in range(8)], [("mT", 0), ("mT", 1), "wout"])
                P.op("dve", lambda e, b=b, cs=cs: e.tensor_tensor(out=xa[0:n, cs], in0=psb[b][0:n, 0:512], in1=xa[0:n, cs], op=ALU.add),
                     [("ps", b), ("x32", xb)], [("x32", xb)])
                yield
            P.op("act", lambda e: e.activation(out=junk[0:n, :], in_=xa[0:n, :], func=AF.Square, accum_out=S(20, n)),
                 [("x32", xb)], ["junk", "junk2", "rf_ssq"])
            rsqrt_chain(S(20), 1.0 / D, S(21), "rf", n)
            P.op("dve", lambda e: e.scalar_tensor_tensor(out=xa[0:n, :], in0=xa[0:n, :], scalar=S(21, n), in1=gfin[0:n, :], op0=ALU.mult, op1=ALU.mult),
                 [("x32", xb), "rf", "gfin"], [("x32", xb)])
            dma(bk.ydst, xa[0:n, :], [("x32", xb)], [], ("yo", xb))
            yield

        def passB_seq(blocks, pre=None):
            bks = []
            for (xsrc, n, ti, ktiles, diag, ydst) in blocks:
                bk = Blk()
                bk.xsrc, bk.n, bk.ti, bk.ktiles, bk.diag, bk.ydst = xsrc, n, ti, ktiles, diag, ydst
                bk.par = len(bks) % 2
                bk.gates_done = False
                bks.append(bk)
            if pre is None:
                pB_load(bks[0])
            else:
                bks[0].xb = pre
            pB_prologueA(bks[0])
            pB_prologueB(bks[0])
            for _ in pB_stageP(bks[0]):
                pass
            for i, bk in enumerate(bks):
                nxt = bks[i + 1] if i + 1 < len(bks) else None
                if nxt is not None:
                    pB_load(nxt)
                pB_attention(bk, nxt)
                gO = pB_stageO(bk, nxt)
                next(gO)
                gP = pB_stageP(nxt) if nxt is not None else iter(())
                doneO = doneP = False
                while not (doneO and doneP):
                    if not doneP:
                        try:
                            next(gP)
                        except StopIteration:
                            doneP = True
                    if not doneO:
                        try:
                            next(gO)
                        except StopIteration:
                            doneO = True

        def load_wkv():
            for c in range(8):
                dma(wkv[:, c, :], wkv_scr[c], ["wkv_scr"], ["wkv"], "wkvl")

        if stop <= 1:
            dma(x32[0], xp[0, 0:128, :], [], [("x32", 0)], ("x", 0))
        for s in range(nseq if stop >= 2 else 0):
            preA = {j: load_x(xp[s, j * 128:(j + 1) * 128, :], 128) for j in range(min(2, nblk))}
            load_wkv()
            zero_kpe()
            passA_seq(pre=preA, items=[(xp[s, i * 128:(i + 1) * 128, :], 128, i, i,
                        (kp[s, i * 128:(i + 1) * 128, :], vp[s, i * 128:(i + 1) * 128, :], cpo[s, i * 128:(i + 1) * 128, :], ppo[s, i * 128:(i + 1) * 128, :]))
                       for i in range(nblk)])
            P.barrier_all()
            zero_q()
            if stop >= 3:
                passB_seq([(xp[s, i * 128:(i + 1) * 128, :], 128, i, [(j, 128) for j in range(i + 1)], True, yp[s, i * 128:(i + 1) * 128, :])
                           for i in range(nblk)])
            P.barrier_all()
        if with_sample:
            for s in range(nseq):
                xa_ = load_x(xs[s], DEC)
                xb_ = load_x(xs[s], DEC)
                load_wkv()
                zero_kpe()
                passA_seq([(xs[s], DEC, 16, 16, (kso[s], vso[s], cso[s], pso[s]))], pre={0: xa_})
                P.barrier_all()
                cache_seq(s)
                P.barrier_all()
                zero_q()
                passB_seq([(xs[s], DEC, 16, [(j, 128) for j in range(16)] + [(16, DEC)], False, ys[s])], pre=xb_)
                P.barrier_all()

        P.finalize()
        sems = {e: es.enter_context(nc.semaphore(f"s_{e}")) for e in ENGS}
        dsems = {k: es.enter_context(nc.semaphore("d_" + "".join(ch for ch in str(k) if ch.isalnum()))) for k in P.dma_keys}
        run = P.emit(sems, dsems)
        with nc.Block() as block:
            @block.tensor
            def _(e):
                run("pe", e)

            @block.scalar
            def _(e):
                run("act", e)

            @block.vector
            def _(e):
                run("dve", e)

            @block.gpsimd
            def _(e):
                run("pool", e)

            @block.sync
            def _(e):
                run("sp", e)
    nc._prog_stats = {e: len(P.eng_ops[e]) for e in ENGS}
    return nc


def _consts():
    import ml_dtypes
    return {"ident": np.eye(128, dtype=np.float32).astype(ml_dtypes.bfloat16), "rope": rope_tables()}


def _weight_map(w_in, w_uq, w_uk, w_uv, w_oa, w_ob, w_out, lambda_q1, lambda_k1, lambda_q2, lambda_k2,
                norm_in, norm_qa, norm_kva, norm_subln, norm_final):
    f = lambda a: np.ascontiguousarray(np.asarray(a, dtype=np.float32))
    m = {
        "w_in": f(w_in[0]), "w_uq": f(w_uq[0]).reshape(512, 768), "w_uk": f(w_uk[0]).reshape(256, 512),
        "w_uv": f(w_uv[0]).reshape(256, 512), "w_oa": f(w_oa[0]), "w_ob": f(w_ob[0]), "w_out": f(w_out[0]),
        "lam4": f(np.stack([lambda_q1[0], lambda_k1[0], lambda_q2[0], lambda_k2[0]], axis=0)),
        "g_inT": f(np.asarray(norm_in[0]).reshape(8, 128).T), "g_qaT": f(np.asarray(norm_qa[0]).reshape(4, 128).T),
        "g_kva": f(norm_kva[0]).reshape(1, 256), "g_sub": f(norm_subln[0]).reshape(1, 128),
        "g_fin": f(norm_final).reshape(1, D),
    }
    m.update(_consts())
    return m


_NC_CACHE = {}


def kernel(x_prompt, x_sample, cache_diff_k, cache_diff_v, cache_mla_ckv, cache_mla_kpe,
           w_in, w_uq, w_uk, w_uv, w_oa, w_ob, w_out,
           lambda_q1, lambda_k1, lambda_q2, lambda_k2,
           norm_in, norm_qa, norm_kva, norm_subln, norm_final):
    NCORE = 8
    B, T, _ = x_prompt.shape
    nseq = B // NCORE
    nblk = T // 128
    key = (nseq, nblk)
    if key not in _NC_CACHE:
        _NC_CACHE[key] = build_program(nseq=nseq, nblk=nblk, with_sample=True)
    nc = _NC_CACHE[key]
    wm = _weight_map(w_in, w_uq, w_uk, w_uv, w_oa, w_ob, w_out, lambda_q1, lambda_k1, lambda_q2, lambda_k2,
                     norm_in, norm_qa, norm_kva, norm_subln, norm_final)
    f = lambda a: np.ascontiguousarray(np.asarray(a, dtype=np.float32))
    in_maps = []
    for c in range(NCORE):
        sl = slice(c * nseq, (c + 1) * nseq)
        m = dict(wm)
        m["xp"] = f(x_prompt[sl])
        m["xs"] = f(x_sample[sl])
        m["ck"] = f(cache_diff_k[0, sl]).reshape(nseq, PAST, 512)
        m["cv"] = f(cache_diff_v[0, sl]).reshape(nseq, PAST, 512)
        m["cc"] = f(cache_mla_ckv[0, sl])
        m["cpe"] = f(cache_mla_kpe[0, sl])
        in_maps.append(m)
    res = run_bass_kernel_spmd(nc, in_maps, core_ids=list(range(NCORE)))
    R = res.results
    cat = lambda k: np.concatenate([np.asarray(r[k], dtype=np.float32) for r in R], axis=0)
    y_p = cat("yp"); y_s = cat("ys")
    k_p = cat("kp").reshape(1, B, T, 4, 128); v_p = cat("vp").reshape(1, B, T, 4, 128)
    c_p = cat("cpo").reshape(1, B, T, 256); p_p = cat("ppo").reshape(1, B, T, 64)
    k_s = cat("ks").reshape(1, B, DEC, 4, 128); v_s = cat("vs").reshape(1, B, DEC, 4, 128)
    c_s = cat("cs").reshape(1, B, DEC, 256); p_s = cat("ps").reshape(1, B, DEC, 64)
    return (y_p, y_s, k_p, v_p, c_p, p_p, k_s, v_s, c_s, p_s)
```

```python
import numpy as np
import concourse.bass as bass
import concourse.mybir as mybir
from concourse.bass_utils import run_bass_kernel_spmd

F32 = mybir.dt.float32
BF16 = mybir.dt.bfloat16
AF = mybir.ActivationFunctionType
ALU = mybir.AluOpType

ENGS = ("pe", "act", "dve", "pool", "sp")


class Op:
    __slots__ = ("idx", "eng", "fn", "deps", "dma", "semkey", "sig", "val", "eidx")

    def __init__(self, idx, eng, fn, dma, semkey):
        self.idx = idx
        self.eng = eng
        self.fn = fn
        self.deps = set()
        self.dma = dma
        self.semkey = semkey
        self.sig = False
        self.val = 0
        self.eidx = 0


class Prog:
    def __init__(self, nc):
        self.nc = nc
        self.ops = []
        self.eng_ops = {e: [] for e in ENGS}
        self.last_w = {}
        self.rd_eng = {}
        self.rd_dma = {}
        self.last_x = {}

    def op(self, eng, fn, reads=(), writes=(), dma=False, semkey=None):
        o = Op(len(self.ops), eng, fn, dma, semkey)
        xs = [r for r in list(reads) + list(writes) if isinstance(r, tuple) and r[0] == "ps"]
        if xs:
            reads = [r for r in reads if r not in xs]
            writes = [r for r in writes if r not in xs]
            for r in set(xs):
                la = self.last_x.get(r)
                if la is not None and self.ops[la].eng != eng:
                    o.deps.add(la)
                self.last_x[r] = o.idx
        for r in reads:
            w = self.last_w.get(r)
            if w is not None:
                o.deps.add(w)
        for w_ in writes:
            w = self.last_w.get(w_)
            rdrs = self.rd_eng.get(w_, {})
            covered = (w is not None and not dma and not self.ops[w].dma and self.ops[w].eng == eng
                       and any(e2 != eng for e2 in rdrs))
            if w is not None and not covered:
                o.deps.add(w)
            for i in rdrs.values():
                o.deps.add(i)
            for i in self.rd_dma.get(w_, ()):
                o.deps.add(i)
        for r in reads:
            if dma:
                self.rd_dma.setdefault(r, []).append(o.idx)
            else:
                self.rd_eng.setdefault(r, {})[eng] = o.idx
        for w_ in writes:
            self.last_w[w_] = o.idx
            self.rd_eng[w_] = {}
            self.rd_dma[w_] = []
        o.deps.discard(o.idx)
        o.eidx = len(self.eng_ops[eng])
        self.ops.append(o)
        self.eng_ops[eng].append(o)
        return o

    def barrier_all(self):
        lasts = []
        for e in ENGS:
            lst = [o for o in self.eng_ops[e] if not o.dma and o.fn is not None]
            if lst:
                lasts.append(lst[-1].idx)
        lastd = {}
        for o in self.ops:
            if o.dma:
                lastd[o.semkey] = o.idx
        self._pending_barrier = set(lasts) | set(lastd.values())
        for e in ENGS:
            eo = e
            o = self.op(eo, None, (), ())
            o.deps |= {d for d in self._pending_barrier if d != o.idx}

    def finalize(self):
        ops = self.ops
        for o in ops:
            for d in o.deps:
                od = ops[d]
                if od.dma:
                    od.sig = True
                elif od.eng == "pe" and o.eng == "pe" and not o.dma:
                    continue
                else:
                    od.sig = True
        cnt = {e: 0 for e in ENGS}
        dcnt = {}
        for o in ops:
            if o.dma:
                dcnt[o.semkey] = dcnt.get(o.semkey, 0) + 16
                o.val = dcnt[o.semkey]
            else:
                if o.sig:
                    cnt[o.eng] += 1
                o.val = cnt[o.eng]
        self.dma_keys = list(dcnt.keys())
        self.final_dma = dict(dcnt)

    def plan_waits(self):
        ops = self.ops
        prod = {}
        for o in ops:
            if o.dma:
                prod[(("d", o.semkey), o.val)] = o.idx
            elif o.sig:
                prod[(("e", o.eng), o.val)] = o.idx
        k_eng = {e: {} for e in ENGS}
        k_done = {}
        plan = {}
        for o in ops:
            e = o.eng
            need = {}
            for d in o.deps:
                od = ops[d]
                if od.dma:
                    s = ("d", od.semkey)
                else:
                    if od.eng == "pe" and e == "pe" and not o.dma:
                        continue
                    s = ("e", od.eng)
                if od.val > need.get(s, 0):
                    need[s] = od.val
            K = k_eng[e]
            todo = []
            for s, v in sorted(need.items(), key=lambda kv: -prod.get((kv[0], kv[1]), -1)):
                if K.get(s, 0) >= v:
                    continue
                p = prod.get((s, v))
                todo.append((s, v, p))
                K[s] = v
                if p is not None and p in k_done:
                    for s2, v2 in k_done[p].items():
                        if v2 > K.get(s2, 0):
                            K[s2] = v2
            plan[o.idx] = todo
            if o.dma:
                kd = dict(K)
                kd[("d", o.semkey)] = o.val
                k_done[o.idx] = kd
            elif o.sig:
                kd = dict(K)
                kd[("e", e)] = o.val
                k_done[o.idx] = kd
        return plan

    def emit(self, sems, dsems):
        plan = self.plan_waits()
        ops = self.ops
        used = set()
        for lst in plan.values():
            for (s, v, p) in lst:
                if p is not None:
                    used.add(p)
        cnt = {e: 0 for e in ENGS}
        for o in ops:
            if not o.dma:
                o.sig = o.idx in used
                if o.sig:
                    cnt[o.eng] += 1
                o.val = cnt[o.eng]

        def run(e, eng):
            for o in self.eng_ops[e]:
                todo = [(dsems[s[1]] if s[0] == "d" else sems[s[1]], (ops[p].val if p is not None else v)) for (s, v, p) in plan[o.idx]]
                attach = None
                if todo and o.fn is not None and not o.dma:
                    attach = todo.pop()
                for sem, v in todo:
                    eng.wait_ge(sem, v)
                if o.fn is None:
                    continue
                ins = o.fn(eng)
                if attach is not None:
                    ins.wait_op(attach[0], attach[1], "sem-ge")
                if o.dma:
                    ins.then_inc(dsems[o.semkey], 16)
                elif o.sig:
                    ins.then_inc(sems[e], 1)
            if e == "sp":
                for k, v in self.final_dma.items():
                    eng.wait_ge(dsems[k], v)

        return run


D = 1024
DIN = 5440
NKT = 17
NKEY = NKT * 128
LAM_INIT = 0.2
OSC = 1.0 - LAM_INIT
EPS = 1e-6
DIFF_SCALE = 0.125
MLA_SCALE = 192.0 ** -0.5
PAST = 2048
DEC = 16


class Carver:
    def __init__(self, big, off, limit):
        self.big, self.off, self.limit, self.base = big, off, limit, off

    def take(self, cols, dt):
        nb = cols * (4 if dt == F32 else 2)
        nb = (nb + 63) // 64 * 64
        assert self.off + nb <= self.limit, ("SBUF carve overflow", self.off + nb, self.limit, self.base)
        a = self.big[:, self.off // 4:(self.off + nb) // 4]
        self.off += nb
        if dt != F32:
            a = a.bitcast(dt)
        return a[:, 0:cols]


def rope_tables():
    half = 32
    inv = (10000.0 ** (-np.arange(half, dtype=np.float32) * 2.0 / 64)).astype(np.float32)
    pos = np.arange(NKT * 128, dtype=np.float32)
    ang = (pos[:, None] * inv[None, :]).astype(np.float32)
    c = np.cos(ang).astype(np.float32).reshape(NKT, 128, half).transpose(1, 0, 2)
    s = np.sin(ang).astype(np.float32).reshape(NKT, 128, half).transpose(1, 0, 2)
    return np.ascontiguousarray(np.stack([c, s], axis=0))


def build_program(nseq=4, nblk=16, with_sample=True, sbuf_kb=223, stop=99):
    from contextlib import ExitStack
    T = nblk * 128
    nc = bass.Bass("TRN2", target_bir_lowering=False, dynamic_dma_scratch_size=256)

    def din(name, shape, dt=F32):
        return nc.dram_tensor(name, list(shape), dt, kind="ExternalInput").ap()

    def dout(name, shape, dt=F32):
        return nc.dram_tensor(name, list(shape), dt, kind="ExternalOutput").ap()

    xp = din("xp", [nseq, T, D])
    w_in = din("w_in", [D, DIN]); w_uq = din("w_uq", [512, 768]); w_uk = din("w_uk", [256, 512])
    w_uv = din("w_uv", [256, 512]); w_oa = din("w_oa", [512, D]); w_ob = din("w_ob", [512, D])
    w_out = din("w_out", [D, D]); lam4 = din("lam4", [4, 64])
    g_inT = din("g_inT", [128, 8]); g_qaT = din("g_qaT", [128, 4]); g_kva = din("g_kva", [1, 256])
    g_sub = din("g_sub", [1, 128]); g_fin = din("g_fin", [1, D])
    ident_d = din("ident", [128, 128], BF16); rope_d = din("rope", [2, 128, NKT, 32])
    yp = dout("yp", [nseq, T, D]); kp = dout("kp", [nseq, T, 512]); vp = dout("vp", [nseq, T, 512])
    cpo = dout("cpo", [nseq, T, 256]); ppo = dout("ppo", [nseq, T, 64])
    if with_sample:
        xs = din("xs", [nseq, DEC, D]); ck = din("ck", [nseq, PAST, 512]); cv = din("cv", [nseq, PAST, 512])
        cckv = din("cc", [nseq, PAST, 256]); cpe = din("cpe", [nseq, PAST, 64])
        ys = dout("ys", [nseq, DEC, D]); kso = dout("ks", [nseq, DEC, 512]); vso = dout("vs", [nseq, DEC, 512])
        cso = dout("cs", [nseq, DEC, 256]); pso = dout("ps", [nseq, DEC, 64])
    wkv_scr = nc.dram_tensor("wkv_scr", [8, 128, 1344], BF16, kind="Internal").ap()

    es = ExitStack()
    with es:
        big = es.enter_context(nc.sbuf_tensor("big", [128, sbuf_kb * 256], F32))
        psb = [es.enter_context(nc.psum_tensor(f"psb{i}", [128, 512], F32)) for i in range(8)]
        P = Prog(nc)
        R = Carver(big, 0, sbuf_kb * 1024)
        wB = R.take(8 * 4096, BF16).rearrange("p (c n) -> p c n", c=8)
        Wabs = R.take(4 * 1024, BF16).rearrange("p (c n) -> p c n", c=4)
        wpe = R.take(4 * 256, BF16).rearrange("p (c n) -> p c n", c=4)
        wuv = R.take(2 * 512, BF16).rearrange("p (c n) -> p c n", c=2)
        woa = R.take(4 * 1024, BF16).rearrange("p (c n) -> p c n", c=4)
        wob = R.take(4 * 1024, BF16).rearrange("p (c n) -> p c n", c=4)
        wout = R.take(8 * 1024, BF16).rearrange("p (c n) -> p c n", c=8)
        KT = R.take(4 * NKEY, BF16).rearrange("p (h k) -> p h k", h=4)
        Vaug = R.take(NKT * 4 * 130, BF16).rearrange("p (j h e) -> p j h e", j=NKT, h=4)
        ckvT = R.take(2 * NKEY, BF16).rearrange("p (c k) -> p c k", c=2)
        ckvaug = R.take(NKT * 258, BF16).rearrange("p (j e) -> p j e", j=NKT)
        kpeT = R.take(NKEY, BF16)
        ident = R.take(128, BF16)
        ropeT = R.take(2 * NKT * 32, F32).rearrange("p (a j e) -> p a j e", a=2, j=NKT)
        gkva = R.take(256, F32); gsub8 = R.take(128, F32); gfin = R.take(D, F32)
        ginT = R.take(8, F32); gqaT = R.take(4, F32)
        rstd_all = R.take(NKT, F32)
        sc_c = R.take(16, F32)
        mhalf = sc_c[:, 0:1]; nlam = sc_c[:, 1:2]; ld1 = sc_c[:, 2:3]; ld2 = sc_c[:, 3:4]
        U0 = R.off
        nc._u0 = U0
        ULIM = sbuf_kb * 1024

        cnt = {"ps": 0, "s": 0, "pt": 0, "acc": 0, "pa": 0, "ob": 0, "attn": 0}

        def proj_bank():
            if cnt["attn"]:
                return 0
            k = cnt["ps"] % 2
            cnt["ps"] += 1
            return k

        def s_bank():
            k = (2, 3, 1)[cnt["s"] % 3]
            cnt["s"] += 1
            return k

        def bfv(k):
            return psb[k][:, :].bitcast(BF16)

        evac_rr = [0]

        def copy(out, in_, reads, writes, eng=None):
            if eng is None:
                eng = ("dve", "act")[evac_rr[0] % 2]
                evac_rr[0] += 1
            if eng == "act":
                P.op("act", lambda e: e.activation(out=out, in_=in_, func=AF.Copy), reads, writes)
            elif eng == "dve":
                P.op("dve", lambda e: e.tensor_copy(out=out, in_=in_), reads, writes)
            else:
                P.op("pool", lambda e: e.tensor_copy(out=out, in_=in_), reads, writes)

        def scaled(out, in_, sc, reads, writes, eng):
            if eng == "act":
                P.op("act", lambda e: e.activation(out=out, in_=in_, func=AF.Copy, scale=sc), reads, writes)
            else:
                P.op("dve", lambda e: e.tensor_scalar(out=out, in0=in_, scalar1=sc, scalar2=None, op0=ALU.mult), reads, writes)

        def dma(out, in_, reads, writes, key, q="sp"):
            P.op(q, lambda e: e.dma_start(out=out, in_=in_), reads, writes, dma=True, semkey=key)

        def transposes(srcs, n, bank, rd):
            bv = bfv(bank)
            for k, src in enumerate(srcs):
                m = src.shape[1]
                P.op("pe", lambda e, k=k, src=src, m=m: e.transpose(out=bv[0:m, k * 128:k * 128 + n], in_=src, identity=ident[0:n, 0:n]),
                     reads=list(rd) + ["ident"], writes=[("ps", bank)])

        def mm_group(bank, col0, ncols, n, lhs_list, rhs_list, rd):
            nk = len(lhs_list)
            for k in range(nk):
                P.op("pe", lambda e, k=k: e.matmul(psb[bank][0:n, col0:col0 + ncols], lhsT=lhs_list[k], rhs=rhs_list[k],
                                                    start=(k == 0), stop=(k == nk - 1)),
                     reads=rd, writes=[("ps", bank)])

        def rsqrt_chain(ssq, mul, out, nm, n, extra=None):
            P.op("dve", lambda e: e.tensor_scalar(out=out[0:n], in0=ssq[0:n], scalar1=mul, scalar2=EPS, op0=ALU.mult, op1=ALU.add),
                 reads=[nm + "_ssq"], writes=[nm])
            P.op("pool", lambda e: e.tensor_tensor(out=out[0:n], in0=out[0:n], in1=mhalf[0:n], op=ALU.pow),
                 reads=[nm, "consts"], writes=[nm])

        dma(ident, ident_d, [], ["ident"], "c0")
        dma(ropeT, rope_d.rearrange("a p j e -> p a j e"), [], ["rope"], "c1")
        dma(gkva, g_kva.partition_broadcast(128), [], ["gkva"], "c2")
        dma(gsub8, g_sub.partition_broadcast(128), [], ["gsub8"], "c3")
        dma(gfin, g_fin.partition_broadcast(128), [], ["gfin"], "c4")
        dma(ginT, g_inT, [], ["ginT"], "c5")
        dma(gqaT, g_qaT, [], ["gqaT"], "c6")
        P.op("pool", lambda e: e.memset(sc_c, -0.5), [], ["consts"])
        P.op("dve", lambda e: e.tensor_scalar(out=gsub8, in0=gsub8, scalar1=OSC, scalar2=None, op0=ALU.mult), ["gsub8"], ["gsub8"])
        P.op("pool", lambda e: e.memset(Vaug, 1.0), [], ["Vaug"])
        P.op("pool", lambda e: e.memset(ckvaug, 1.0), [], ["ckvaug"])
        P.op("pool", lambda e: e.memset(kpeT, 0.0), [], ["kpeT"])

        U = Carver(big, U0, ULIM)
        stage = [U.take(2880, F32) for _ in range(2)]
        wkv16 = U.take(1344, BF16)
        lamj = U.take(64, F32)
        lamt = U.take(4 * 64, F32).rearrange("p (a e) -> p a e", a=4)
        for a in range(4):
            dma(lamt[:, a, :], lam4[a:a + 1, :].partition_broadcast(128), [], ["lamt"], "c7")
        P.op("dve", lambda e: e.scalar_tensor_tensor(out=lamj, in0=lamt[:, 0, :], scalar=1.0, in1=lamt[:, 1, :], op0=ALU.mult, op1=ALU.mult, accum_out=ld1),
             ["lamt", "consts"], ["lamj", "ld1"])
        P.op("dve", lambda e: e.scalar_tensor_tensor(out=lamj, in0=lamt[:, 2, :], scalar=1.0, in1=lamt[:, 3, :], op0=ALU.mult, op1=ALU.mult, accum_out=ld2),
             ["lamt", "ld1"], ["lamj", "ld2"])
        P.op("act", lambda e: e.activation(out=ld1, in_=ld1, func=AF.Exp), ["ld1"], ["ld1"])
        P.op("act", lambda e: e.activation(out=ld2, in_=ld2, func=AF.Exp), ["ld2"], ["ld2"])
        P.op("dve", lambda e: e.scalar_tensor_tensor(out=nlam, in0=ld2, scalar=-LAM_INIT, in1=ld1, op0=ALU.add, op1=ALU.subtract),
             ["ld1", "ld2"], ["nlam"])
        segs0 = [(0, 512, 0), (2048, 2560, 512), (1536, 2048, 1024)]
        segs1 = [(2880, 3392, 1536), (3392, 5440, 2048)]
        for c in range(8):
            gc = ginT[:, c:c + 1]
            dma(stage[0], w_in[c * 128:(c + 1) * 128, 0:2880], [], [("stage", 0)], ("stg", 0))
            for si, (a, b, o) in enumerate(segs0):
                scaled(wB[:, c, o:o + (b - a)], stage[0][:, a:b], gc, [("stage", 0), "ginT"], ["wB"], ("act", "dve")[si % 2])
            scaled(wkv16[:, 0:1024], stage[0][:, 512:1536], gc, [("stage", 0), "ginT"], ["wkv16"], "dve")
            scaled(wkv16[:, 1024:1344], stage[0][:, 2560:2880], gc, [("stage", 0), "ginT"], ["wkv16"], "act")
            dma(wkv_scr[c], wkv16, ["wkv16"], ["wkv_scr"], "wkvo")
            dma(stage[1][:, 0:2560], w_in[c * 128:(c + 1) * 128, 2880:5440], [], [("stage", 1)], ("stg", 1))
            for si, (a, b, o) in enumerate(segs1):
                scaled(wB[:, c, o:o + (b - a)], stage[1][:, a - 2880:b - 2880], gc, [("stage", 1), "ginT"], ["wB"], ("dve", "act")[si % 2])
        P.barrier_all()

        U = Carver(big, U0, ULIM)
        st2 = [U.take(1024, F32) for _ in range(2)]
        wuqn16 = U.take(4 * 512, BF16).rearrange("p (c n) -> p c n", c=4)
        wuk16 = U.take(2 * 512, BF16).rearrange("p (c n) -> p c n", c=2)
        wukT = U.take(4 * 256, BF16).rearrange("p (h n) -> p h n", h=4)
        wuqT = U.take(4 * 512, BF16).rearrange("p (h n) -> p h n", h=4)
        k2 = [0]

        def stage2(src, cols):
            sb = k2[0] % 2
            k2[0] += 1
            dma(st2[sb][:, 0:cols], src, [], [("st2", sb)], ("st2", sb))
            return st2[sb], ("st2", sb)

        for c in range(4):
            st, r = stage2(w_uq[c * 128:(c + 1) * 128, :], 768)
            s3 = st[:, 0:768].rearrange("p (h e) -> p h e", h=4)
            scaled(wuqn16[:, c, :].rearrange("p (h e) -> p h e", h=4), s3[:, :, 0:128], gqaT[:, c:c + 1], [r, "gqaT"], ["wuqn16"], "dve")
            scaled(wpe[:, c, :].rearrange("p (h e) -> p h e", h=4), s3[:, :, 128:192], gqaT[:, c:c + 1], [r, "gqaT"], ["wpe"], "act")
        for c in range(2):
            st, r = stage2(w_uk[c * 128:(c + 1) * 128, :], 512)
            copy(wuk16[:, c, :], st[:, 0:512], [r], ["wuk16"])
            st, r = stage2(w_uv[c * 128:(c + 1) * 128, :], 512)
            copy(wuv[:, c, :], st[:, 0:512], [r], ["wuv"])
        for (wsrc, wdst, nm, nch) in ((w_oa, woa, "woa", 4), (w_ob, wob, "wob", 4), (w_out, wout, "wout", 8)):
            for c in range(nch):
                st, r = stage2(wsrc[c * 128:(c + 1) * 128, :], 1024)
                scaled(wdst[:, c, :], st[:, 0:1024], 0.5, [r], [nm], ("act", "dve")[c % 2])
        for h in range(4):
            b = proj_bank()
            transposes([wuk16[:, cc, h * 128:(h + 1) * 128] for cc in range(2)], 128, b, ["wuk16"])
            copy(wukT[:, h, :], bfv(b)[:, 0:256], [("ps", b)], ["wukT"])
        for h in range(4):
            b = proj_bank()
            transposes([wuqn16[:, cc, h * 128:(h + 1) * 128] for cc in range(4)], 128, b, ["wuqn16"])
            copy(wuqT[:, h, :], bfv(b)[:, 0:512], [("ps", b)], ["wuqT"])
        for cc in range(4):
            for hp in range(2):
                b = proj_bank()
                for hh in range(2):
                    h = hp * 2 + hh
                    mm_group(b, hh * 256, 256, 128, [wuqT[:, h, cc * 128:(cc + 1) * 128]], [wukT[:, h, :]], ["wuqT", "wukT"])
                copy(Wabs[:, cc, hp * 512:(hp + 1) * 512], psb[b][:, 0:512], [("ps", b)], ["Wabs"])
        P.barrier_all()

        U = Carver(big, U0, ULIM)
        x32 = [U.take(D, F32) for _ in range(2)]
        ropeA = U.take(512, F32); ropeB = U.take(512, F32)
        scal = U.take(64, F32)
        xbf = U.take(D, BF16); hT = U.take(D, BF16); junk = U.take(D, BF16)
        UA0 = U.off
        oh32 = U.take(128, F32)
        qa16 = U.take(512, BF16); qd16 = U.take(512, BF16); qdT = U.take(512, BF16)
        zz = U.take(1024, BF16); tgs = [U.take(2048, BF16) for _ in range(2)]
        qlat16 = U.take(1024, BF16); qpe16 = U.take(512, BF16)
        QlT = U.take(1024, BF16)
        PT = [U.take(384, BF16) for _ in range(3)]
        ob16 = U.take(512, BF16); obT = U.take(512, BF16)
        m16 = U.take(1024, BF16); mT = U.take(1024, BF16)
        Q12z = U.take(1024, BF16).rearrange("p (h a q) -> p h a q", h=4, a=2); qpeT = U.take(512, BF16)

        def zero_kpe():
            P.op("pool", lambda e: e.memset(kpe16, 0.0), [], ["kpe16"])

        def zero_q():
            P.op("pool", lambda e: e.memset(qpe16, 0.0), [], ["qpe16"])
            P.op("pool", lambda e: e.memset(qpeT, 0.0), [], ["qpeT"])
            P.op("pool", lambda e: e.memset(Q12z, 0.0), [], ["Q12z"])
        oa16, oaT, olat16, olatT = qd16, qdT, qlat16, QlT
        rmall = scal[:, 32:36]
        UA = Carver(big, UA0, ULIM)
        wkv = UA.take(8 * 1344, BF16).rearrange("p (c n) -> p c n", c=8)
        kvout = UA.take(1344, F32)
        ka16 = UA.take(512, BF16); kpe16 = UA.take(128, BF16); ckv16 = UA.take(256, BF16)
        xcnt = [0]


        def S(i, n=128):
            return scal[0:n, i:i + 1]

        def rope(src, sc, ti, G, n, dst, rd, wr, dst_is_f32):
            W = G * 64
            cosb = ropeT[0:n, 0, ti, :].unsqueeze(1).broadcast_to([n, 2 * G, 32])
            sinb = ropeT[0:n, 1, ti, :].unsqueeze(1).broadcast_to([n, G, 32])
            nsc = S(23, n)
            P.op("dve", lambda e: e.tensor_scalar(out=nsc, in0=sc, scalar1=-1.0, scalar2=None, op0=ALU.mult), [r for r in rd if not (isinstance(r, tuple) and r[0] == "ps")], ["nsc"])
            g64 = lambda ap: ap.rearrange("p (g j) -> p g j", j=64)
            t1 = dst if dst_is_f32 else g64(ropeA[0:n, 0:W])
            t1n = wr if dst_is_f32 else ["ropeA"]
            v4 = lambda ap: ap.rearrange("p (g t j) -> p g t j", t=2, j=32)
            P.op("dve", lambda e: e.scalar_tensor_tensor(out=t1.rearrange("p g (t j) -> p (g t) j", j=32), in0=src.rearrange("p (g j) -> p g j", j=32),
                                                        scalar=sc, in1=cosb, op0=ALU.mult, op1=ALU.mult),
                 rd + ["rope"], t1n)
            t2 = ropeB[0:n, 0:W]
            P.op("dve", lambda e: e.scalar_tensor_tensor(out=v4(t2)[:, :, 0, :], in0=v4(src)[:, :, 1, :], scalar=nsc, in1=sinb, op0=ALU.mult, op1=ALU.mult),
                 rd + ["rope", "nsc"], ["ropeB"])
            P.op("dve", lambda e: e.scalar_tensor_tensor(out=v4(t2)[:, :, 1, :], in0=v4(src)[:, :, 0, :], scalar=sc, in1=sinb, op0=ALU.mult, op1=ALU.mult),
                 rd + ["rope", "ropeB"], ["ropeB"])
            P.op("pool", lambda e: e.tensor_tensor(out=dst, in0=t1, in1=g64(t2), op=ALU.add), t1n + ["ropeB"], wr)

        def load_x(src_ap, n):
            xb = xcnt[0] % 2
            xcnt[0] += 1
            dma(x32[xb][0:n, :], src_ap, [], [("x32", xb)], ("x", xb))
            return xb

        def norm_and_T(xb, n, want_rstd, ti, bank=None, evac="dve"):
            xa = x32[xb]
            if want_rstd:
                P.op("act", lambda e: e.activation(out=junk[0:n, :], in_=xa[0:n, :], func=AF.Square, accum_out=S(0, n)),
                     [("x32", xb)], ["junk", "rx_ssq"])
                rsqrt_chain(S(0), 1.0 / D, S(1), "rx", n)
                P.op("dve", lambda e: e.tensor_copy(out=rstd_all[0:n, ti:ti + 1], in_=S(1, n)), ["rx"], [("rstd", ti)])
            copy(xbf[0:n, :], xa[0:n, :], [("x32", xb)], ["xbf"], eng="act")
            b = proj_bank() if bank is None else bank
            transposes([xbf[0:n, c * 128:(c + 1) * 128] for c in range(8)], n, b, ["xbf"])
            copy(hT[:, :].rearrange("p (c t) -> p c t", c=8)[:, :, 0:n], bfv(b)[:, :].rearrange("p (c t) -> p c t", c=8)[:, :, 0:n],
                 [("ps", b)], ["hT"], eng=evac)

        def hT_c(c, n):
            return hT[:, c * 128:c * 128 + n]

        def bankA():
            k = cnt["pa"] % 8
            cnt["pa"] += 1
            return k

        def passA_part1(xb, n, ti):
            norm_and_T(xb, n, True, ti, bank=6)
            hl = [hT_c(c, n) for c in range(8)]
            k0 = 3 * (bankA() % 2)
            bc, bk_, bv = k0, k0 + 1, k0 + 2
            mm_group(bc, 0, 320, n, hl, [wkv[:, c, 1024:1344] for c in range(8)], ["hT", "wkv"])
            mm_group(bk_, 0, 512, n, hl, [wkv[:, c, 0:512] for c in range(8)], ["hT", "wkv"])
            mm_group(bv, 0, 512, n, hl, [wkv[:, c, 512:1024] for c in range(8)], ["hT", "wkv"])
            return (bc, bk_, bv)

        def passA_part2(banks, n, ti, kt, outs):
            ko, vo, co, po = outs
            bc, bk_, bv = banks
            rs = rstd_all[0:n, ti:ti + 1]
            rrs = [("rstd", ti)]
            g64 = lambda ap: ap.rearrange("p (g j) -> p g j", j=64)
            P.op("act", lambda e: e.activation(out=junk[0:n, 0:256], in_=psb[bc][0:n, 0:256], func=AF.Square, accum_out=S(2, n)),
                 [("ps", bc)], ["junk", "rc_ssq"])
            P.op("dve", lambda e: e.tensor_scalar(out=S(22, n), in0=rs, scalar1=rs, scalar2=1.0 / 256, op0=ALU.mult, op1=ALU.mult), rrs, ["tmp22"])
            P.op("dve", lambda e: e.tensor_scalar(out=S(3, n), in0=S(2, n), scalar1=S(22, n), scalar2=EPS, op0=ALU.mult, op1=ALU.add),
                 ["rc_ssq", "tmp22"], ["rc"])
            P.op("pool", lambda e: e.tensor_tensor(out=S(3, n), in0=S(3, n), in1=mhalf[0:n], op=ALU.pow), ["rc", "consts"], ["rc"])
            rope(psb[bk_][0:n, 0:512], rs, ti, 8, n, g64(kvout[0:n, 0:512]), [("ps", bk_)] + rrs, ["kv_k"], True)
            scaled(kvout[0:n, 512:1024], psb[bv][0:n, 0:512], rs, [("ps", bv)] + rrs, ["kv_v"], "act")
            copy(ka16[0:n, :], kvout[0:n, 0:512], ["kv_k"], ["ka16"], eng="act")
            dma(ko, kvout[0:n, 0:512], ["kv_k"], [], "oA0")
            dma(vo, kvout[0:n, 512:1024], ["kv_v"], [], "oA1")
            P.op("dve", lambda e: e.tensor_tensor(out=S(4, n), in0=S(3, n), in1=rs, op=ALU.mult), ["rc"] + rrs, ["sc_c"])
            P.op("dve", lambda e: e.scalar_tensor_tensor(out=kvout[0:n, 1024:1280], in0=psb[bc][0:n, 0:256], scalar=S(4, n), in1=gkva[0:n, :],
                                                        op0=ALU.mult, op1=ALU.mult), [("ps", bc), "sc_c", "gkva"], ["kv_c"])
            copy(ckv16[0:n, :], kvout[0:n, 1024:1280], ["kv_c"], ["ckv16"], eng="act")
            dma(co, kvout[0:n, 1024:1280], ["kv_c"], [], "oA2")
            rope(psb[bc][0:n, 256:320], rs, ti, 1, n, g64(kvout[0:n, 1280:1344]), [("ps", bc)] + rrs, ["kv_p"], True)
            copy(kpe16[0:n, 0:64], kvout[0:n, 1280:1344], ["kv_p"], ["kpe16"], eng="act")
            dma(po, kvout[0:n, 1280:1344], ["kv_p"], [], "oA3")
            P.op("pool", lambda e: e.tensor_copy(out=Vaug[0:n, kt, :, 0:128], in_=kvout[0:n, 512:1024].rearrange("p (h e) -> p h e", h=4)),
                 ["kv_v"], ["Vaug"])
            P.op("pool", lambda e: e.tensor_copy(out=ckvaug[0:n, kt, 0:256], in_=ckv16[0:n, :]), ["ckv16"], ["ckvaug"])
            b = 7
            transposes([ka16[0:n, h * 128:(h + 1) * 128] for h in range(4)], n, b, ["ka16"])
            copy(KT[:, :, kt * 128:kt * 128 + n], bfv(b)[:, 0:512].rearrange("p (h t) -> p h t", h=4)[:, :, 0:n], [("ps", b)], ["KT"], eng="dve")
            b = 6
            transposes([ckv16[0:n, c * 128:(c + 1) * 128] for c in range(2)] + [kpe16[0:n, :]], n, b, ["ckv16", "kpe16"])
            copy(ckvT[:, :, kt * 128:kt * 128 + n], bfv(b)[:, 0:256].rearrange("p (c t) -> p c t", c=2)[:, :, 0:n], [("ps", b)], ["ckvT"], eng="dve")
            copy(kpeT[:, kt * 128:kt * 128 + n], bfv(b)[:, 256:256 + n], [("ps", b)], ["kpeT"], eng="dve")

        def passA_seq(items, pre=None):
            xbs = dict(pre or {})
            for j in range(min(2, len(items))):
                if j not in xbs:
                    xbs[j] = load_x(items[j][0], items[j][1])
            banks = passA_part1(xbs[0], items[0][1], items[0][2])
            for i, (xsrc, n, ti, kt, outs) in enumerate(items):
                if i + 2 < len(items):
                    xbs[i + 2] = load_x(items[i + 2][0], items[i + 2][1])
                nb = None
                if i + 1 < len(items):
                    nb = passA_part1(xbs[i + 1], items[i + 1][1], items[i + 1][2])
                passA_part2(banks, n, ti, kt, outs)
                banks = nb

        def cache_seq(s):
            UC = Carver(big, UA0, ULIM)
            stg = [UC.take(1344, F32) for _ in range(3)]
            k16 = [UC.take(512, BF16) for _ in range(2)]
            c16 = [UC.take(256, BF16) for _ in range(2)]
            p16 = [UC.take(128, BF16) for _ in range(2)]
            for k in range(2):
                P.op("pool", lambda e, k=k: e.memset(p16[k], 0.0), [], [("p16", k)])

            def loads(kt):
                c0 = stg[kt % 3]
                r = slice(kt * 128, (kt + 1) * 128)
                q = kt % 3
                dma(c0[:, 0:512], ck[s, r, :], [], [("cstg", q, 0)], ("cA0", q))
                dma(c0[:, 512:1024], cv[s, r, :], [], [("cstg", q, 1)], ("cA1", q))
                dma(c0[:, 1024:1280], cckv[s, r, :], [], [("cstg", q, 2)], ("cA2", q))
                dma(c0[:, 1280:1344], cpe[s, r, :], [], [("cstg", q, 3)], ("cA3", q))

            loads(0)
            loads(1)
            for kt in range(16):
                if kt + 2 < 16:
                    loads(kt + 2)
                q, w = kt % 3, kt % 2
                c0 = stg[q]
                copy(k16[w][:, :], c0[:, 0:512], [("cstg", q, 0)], [("k16", w)], eng="act")
                copy(c16[w][:, :], c0[:, 1024:1280], [("cstg", q, 2)], [("c16", w)], eng="dve")
                copy(p16[w][:, 0:64], c0[:, 1280:1344], [("cstg", q, 3)], [("p16", w)], eng="act")
                v3 = c0[:, 512:1024].rearrange("p (h e) -> p h e", h=4)
                P.op("dve", lambda e, kt=kt, v3=v3: e.tensor_copy(out=Vaug[:, kt, 0:2, 0:128], in_=v3[:, 0:2, :]), [("cstg", q, 1)], [("Vaug", kt, 0)])
                P.op("pool", lambda e, kt=kt, v3=v3: e.tensor_copy(out=Vaug[:, kt, 2:4, 0:128], in_=v3[:, 2:4, :]), [("cstg", q, 1)], [("Vaug", kt, 1)])
                P.op("pool", lambda e, kt=kt, w=w: e.tensor_copy(out=ckvaug[:, kt, 0:256], in_=c16[w][:, :]), [("c16", w)], ["ckvaug"])
                b = bankA()
                transposes([k16[w][:, h * 128:(h + 1) * 128] for h in range(4)], 128, b, [("k16", w)])
                copy(KT[:, :, kt * 128:(kt + 1) * 128], bfv(b)[:, 0:512].rearrange("p (h t) -> p h t", h=4), [("ps", b)], ["KT"], eng="dve")
                b = bankA()
                transposes([c16[w][:, c * 128:(c + 1) * 128] for c in range(2)] + [p16[w][:, :]], 128, b, [("c16", w), ("p16", w)])
                copy(ckvT[:, :, kt * 128:(kt + 1) * 128], bfv(b)[:, 0:256].rearrange("p (c t) -> p c t", c=2), [("ps", b)], ["ckvT"], eng="act")
                copy(kpeT[:, kt * 128:(kt + 1) * 128], bfv(b)[:, 256:384], [("ps", b)], ["kpeT"], eng="act")

        def wcols(a, b_):
            return [wB[:, c, a:b_] for c in range(8)]

        class Blk:
            pass

        def pB_load(bk):
            bk.xb = load_x(bk.xsrc, bk.n)

        def pB_prologueA(bk):
            n, ti = bk.n, bk.ti
            norm_and_T(bk.xb, n, False, ti, evac="act")
            bk.rs = rstd_all[0:n, ti:ti + 1]
            bk.rrs = [("rstd", ti)]
            rs, rrs = bk.rs, bk.rrs
            P.op("dve", lambda e: e.tensor_scalar(out=S(7, n), in0=rs, scalar1=DIFF_SCALE, scalar2=None, op0=ALU.mult), rrs, ["sq8"])
            P.op("dve", lambda e: e.tensor_scalar(out=S(8, n), in0=rs, scalar1=0.5, scalar2=None, op0=ALU.mult), rrs, ["hrs"])
            bk.hl = [hT_c(c, n) for c in range(8)]

        def pB_prologueB(bk):
            n, ti = bk.n, bk.ti
            b = proj_bank()
            mm_group(b, 0, 512, n, bk.hl, wcols(0, 512), ["hT", "wB"])
            rope(psb[b][0:n, 0:512], S(7, n), ti, 8, n, qa16[0:n, :].rearrange("p (g j) -> p g j", j=64), [("ps", b), "sq8"], ["qa16"], False)

        def pB_gate(bk, gi, b):
            n = bk.n
            tg = tgs[bk.par]
            mm_group(b, 0, 512, n, bk.hl, wcols(2048 + gi * 512, 2560 + gi * 512), ["hT", "wB"])
            P.op("act", lambda e: e.activation(out=tg[0:n, gi * 512:(gi + 1) * 512], in_=psb[b][0:n, 0:512], func=AF.Tanh, scale=S(8, n)),
                 [("ps", b), "hrs"], [("tg", bk.par, gi)])

        def pB_stageP(bk):
            n, ti, rs, rrs, hl = bk.n, bk.ti, bk.rs, bk.rrs, bk.hl
            b = proj_bank()
            mm_group(b, 0, 512, n, hl, wcols(512, 1024), ["hT", "wB"])
            P.op("act", lambda e, b=b: e.activation(out=xbf[0:n, 0:512], in_=psb[b][0:n, 0:512], func=AF.Square, accum_out=S(5, n)),
                 [("ps", b)], ["xbf", "rq_ssq"])
            copy(qd16[0:n, :], psb[b][0:n, 0:512], [("ps", b)], ["qd16"], eng="dve")
            P.op("dve", lambda e: e.tensor_scalar(out=S(22, n), in0=rs, scalar1=rs, scalar2=1.0 / 512, op0=ALU.mult, op1=ALU.mult), rrs, ["tmp22"])
            P.op("dve", lambda e: e.tensor_scalar(out=S(6, n), in0=S(5, n), scalar1=S(22, n), scalar2=EPS, op0=ALU.mult, op1=ALU.add),
                 ["rq_ssq", "tmp22"], ["sq"])
            P.op("pool", lambda e: e.tensor_tensor(out=S(6, n), in0=S(6, n), in1=mhalf[0:n], op=ALU.pow), ["sq", "consts"], ["sq"])
            P.op("dve", lambda e: e.tensor_scalar(out=S(6, n), in0=S(6, n), scalar1=rs, scalar2=MLA_SCALE, op0=ALU.mult, op1=ALU.mult),
                 ["sq"] + rrs, ["sq"])

            yield
            if not bk.gates_done:
                for gi in range(4):
                    pB_gate(bk, gi, proj_bank())
            yield
            b = proj_bank()
            transposes([qd16[0:n, c * 128:(c + 1) * 128] for c in range(4)], n, b, ["qd16"])
            copy(qdT[:, :].rearrange("p (c t) -> p c t", c=4)[:, :, 0:n], bfv(b)[:, 0:512].rearrange("p (c t) -> p c t", c=4)[:, :, 0:n],
                 [("ps", b)], ["qdT"], eng="dve")
            yield
            ql = [qdT[:, c * 128:c * 128 + n] for c in range(4)]
            for half in range(2):
                b = proj_bank()
                mm_group(b, 0, 512, n, ql, [Wabs[:, c, half * 512:(half + 1) * 512] for c in range(4)], ["qdT", "Wabs"])
                scaled(qlat16[0:n, half * 512:(half + 1) * 512], psb[b][0:n, 0:512], S(6, n), [("ps", b), "sq"], ["qlat16"], ("act", "dve")[half])
            yield
            b = proj_bank()
            mm_group(b, 0, 256, n, ql, [wpe[:, c, :] for c in range(4)], ["qdT", "wpe"])
            rope(psb[b][0:n, 0:256], S(6, n), ti, 4, n, qpe16[0:n, :].rearrange("p (h e) -> p h e", h=4)[:, :, 0:64], [("ps", b), "sq"], ["qpe16"], False)
            yield
            v3 = lambda ap: ap.rearrange("p (h t) -> p h t", h=4)[:, :, 0:n]
            b = proj_bank()
            transposes([qa16[0:n, h * 128:(h + 1) * 128] for h in range(4)], n, b, ["qa16"])
            copy(Q12z[0:64, :, 0, 0:n], v3(bfv(b)[0:64, 0:512]), [("ps", b)], ["Q12z"], eng="dve")
            copy(Q12z[64:128, :, 1, 0:n], v3(bfv(b)[64:128, 0:512]), [("ps", b)], ["Q12z"], eng="dve")
            yield
            for zi in range(2):
                b = proj_bank()
                mm_group(b, 0, 512, n, hl, wcols(1024 + zi * 512, 1536 + zi * 512), ["hT", "wB"])
                P.op("act", lambda e, b=b: e.activation(out=xbf[0:n, 512:1024], in_=psb[b][0:n, 0:512], func=AF.Tanh, scale=S(8, n)),
                     [("ps", b), "hrs"], ["xbf"])
                P.op("dve", lambda e, b=b, zi=zi: e.scalar_tensor_tensor(out=zz[0:n, zi * 512:(zi + 1) * 512], in0=xbf[0:n, 512:1024], scalar=1.0,
                                                                        in1=psb[b][0:n, 0:512], op0=ALU.add, op1=ALU.mult),
                     [("ps", b), "xbf"], [("zz", zi)])
                if zi == 0:
                    P.op("dve", lambda e: e.tensor_tensor(out=zz[0:n, 0:512].rearrange("p (h e) -> p h e", h=4), in0=zz[0:n, 0:512].rearrange("p (h e) -> p h e", h=4),
                                                         in1=gsub8[0:n, :].unsqueeze(1).broadcast_to([n, 4, 128]), op=ALU.mult), [("zz", 0), "gsub8"], [("zz", 0)])
            yield
            b = proj_bank()
            transposes([qlat16[0:n, c * 128:(c + 1) * 128] for c in range(8)], n, b, ["qlat16"])
            copy(QlT[:, :].rearrange("p (c t) -> p c t", c=8)[:, :, 0:n], bfv(b)[:, :].rearrange("p (c t) -> p c t", c=8)[:, :, 0:n],
                 [("ps", b)], [("QlT", h_) for h_ in range(4)], eng="dve")
            yield
            b = proj_bank()
            transposes([qpe16[0:n, h * 128:(h + 1) * 128] for h in range(4)], n, b, ["qpe16"])
            copy(v3(qpeT[:, :]), v3(bfv(b)[:, 0:512]), [("ps", b)], ["qpeT"], eng="act")
            yield

        def pB_attention(bk, nxt):
            n, ti, rs, rrs, ktiles, diag = bk.n, bk.ti, bk.rs, bk.rrs, bk.ktiles, bk.diag
            nj = len(ktiles)
            tiles = [(h, jj, kt, nk) for h in range(4) for jj, (kt, nk) in enumerate(ktiles)]
            st = {}
            pending = []

            def emit_S(t):
                h, jj, kt, nk = tiles[t]
                sb = s_bank()
                pb = cnt["pt"] % 3
                cnt["pt"] += 1
                pt = PT[pb]
                ks = slice(kt * 128, kt * 128 + nk)
                qs = slice(h * 128, h * 128 + n)
                if n == 128:
                    P.op("pe", lambda e: e.matmul(psb[sb][0:nk, 0:256], lhsT=KT[:, h, ks], rhs=Q12z[:, h, :, :].rearrange("p a q -> p (a q)"),
                                                  start=True, stop=True), ["KT", "Q12z"], [("ps", sb)])
                else:
                    mm_group(sb, 0, n, nk, [KT[:, h, ks]], [Q12z[:, h, 0, 0:n]], ["KT", "Q12z"])
                    mm_group(sb, 128, n, nk, [KT[:, h, ks]], [Q12z[:, h, 1, 0:n]], ["KT", "Q12z"])
                mm_group(sb, 256, n, nk, [ckvT[:, 0, ks], ckvT[:, 1, ks], kpeT[:, ks]],
                         [QlT[:, (2 * h) * 128:(2 * h) * 128 + n], QlT[:, (2 * h + 1) * 128:(2 * h + 1) * 128 + n], qpeT[:, qs]],
                         ["ckvT", "kpeT", ("QlT", h), "qpeT"])
                P.op("act", lambda e: e.activation(
                    out=pt[0:nk, :].rearrange("p (a q) -> p a q", a=3)[:, :, 0:n],
                    in_=psb[sb][0:nk, 0:384].rearrange("p (a q) -> p a q", a=3)[:, :, 0:n], func=AF.Exp),
                    [("ps", sb)], [("PT", pb)])
                if diag and jj == nj - 1:
                    P.op("pool", lambda e: e.memset(pt[64:128, :].rearrange("p (a q) -> p a q", a=3)[:, :, 0:64], 0.0),
                         [("PT", pb)], [("PT", pb)])
                st[t] = (pt, pb)

            def emit_PV(t):
                h, jj, kt, nk = tiles[t]
                pt, pb = st.pop(t)
                if jj == 0:
                    cnt["acc"] += 1
                aset = cnt["acc"] % 2
                b12, bm = 4 + 2 * aset, 5 + 2 * aset
                first, last = (jj == 0), (jj == nj - 1)
                va = Vaug[0:nk, kt, h, 0:129]
                P.op("pe", lambda e: e.matmul(psb[b12][0:n, 0:129], lhsT=pt[0:nk, 0:n], rhs=va, start=first, stop=last),
                     [("PT", pb), "Vaug"], [("ps", b12)])
                P.op("pe", lambda e: e.matmul(psb[b12][0:n, 129:258], lhsT=pt[0:nk, 128:128 + n], rhs=va, start=False, stop=last, skip_group_check=True),
                     [("PT", pb), "Vaug"], [("ps", b12)])
                P.op("pe", lambda e: e.matmul(psb[bm][0:n, 0:257], lhsT=pt[0:nk, 256:256 + n], rhs=ckvaug[0:nk, kt, 0:257], start=first, stop=last),
                     [("PT", pb), "ckvaug"], [("ps", bm)])
                if last:
                    epilogue(h, b12, bm)
                    pending.append((t + 7, lambda h=h: head_post(h)))

            def epilogue(h, b12, bm):
                A = psb[b12]
                P.op("dve", lambda e: e.reciprocal(out=S(17, n), in_=A[0:n, 128:129]), [("ps", b12)], ["r1"])
                P.op("dve", lambda e: e.reciprocal(out=S(18, n), in_=A[0:n, 257:258]), [("ps", b12)], ["r2"])
                P.op("dve", lambda e: e.tensor_tensor(out=S(19, n), in0=S(18, n), in1=nlam[0:n], op=ALU.mult), ["r2", "nlam"], ["nl2"])
                P.op("dve", lambda e: e.tensor_scalar(out=oh32[0:n, :], in0=A[0:n, 0:128], scalar1=S(17, n), scalar2=None, op0=ALU.mult),
                     [("ps", b12), "r1"], ["oh32"])
                P.op("dve", lambda e: e.scalar_tensor_tensor(out=oh32[0:n, :], in0=A[0:n, 129:257], scalar=S(19, n), in1=oh32[0:n, :],
                                                            op0=ALU.mult, op1=ALU.add), [("ps", b12), "nl2", "oh32"], ["oh32"])
                P.op("dve", lambda e: e.reciprocal(out=rmall[0:n, h:h + 1], in_=psb[bm][0:n, 256:257]), [("ps", bm)], ["rm"])
                copy(olat16[0:n, h * 256:(h + 1) * 256], psb[bm][0:n, 0:256], [("ps", bm)], [("olat16", h), "qlat16"], eng="dve")
                P.op("dve", lambda e: e.tensor_tensor(out=rmall[0:n, h:h + 1], in0=rmall[0:n, h:h + 1], in1=rs, op=ALU.mult), ["rm"] + rrs, ["rm"])
                P.op("dve", lambda e: e.scalar_tensor_tensor(out=junk[0:n, 0:128], in0=oh32[0:n, :], scalar=1.0, in1=oh32[0:n, :],
                                                            op0=ALU.mult, op1=ALU.mult, accum_out=S(9, n)), ["oh32"], ["junk", "ro_ssq"])
                rsqrt_chain(S(9), 1.0 / 128, S(13), "ro", n)
                P.op("dve", lambda e: e.tensor_tensor(out=S(13, n), in0=S(13, n), in1=rs, op=ALU.mult), ["ro"] + rrs, ["ro"])
                P.op("dve", lambda e: e.scalar_tensor_tensor(out=oa16[0:n, h * 128:(h + 1) * 128], in0=oh32[0:n, :], scalar=S(13, n), in1=zz[0:n, h * 128:(h + 1) * 128],
                                                            op0=ALU.mult, op1=ALU.mult), ["oh32", "ro", ("zz", 0)], [("oa16", h), "qd16"])

            def head_post(h):
                b = proj_bank()
                transposes([oa16[0:n, h * 128:(h + 1) * 128], olat16[0:n, (2 * h) * 128:(2 * h + 1) * 128], olat16[0:n, (2 * h + 1) * 128:(2 * h + 2) * 128]],
                           n, b, [("oa16", h), ("olat16", h), "qd16", "qlat16"])
                copy(oaT[:, h * 128:h * 128 + n], bfv(b)[:, 0:n], [("ps", b)], [("oaT", h), "qdT"], eng="dve")
                copy(olatT[:, (2 * h) * 128:(2 * h + 2) * 128].rearrange("p (c t) -> p c t", c=2)[:, :, 0:n],
                     bfv(b)[:, 128:384].rearrange("p (c t) -> p c t", c=2)[:, :, 0:n], [("ps", b)], [("olatT", h), ("QlT", h)], eng="dve")
                b = proj_bank()
                mm_group(b, 0, 128, n, [olatT[:, (2 * h + c) * 128:(2 * h + c) * 128 + n] for c in range(2)],
                         [wuv[:, c, h * 128:(h + 1) * 128] for c in range(2)], [("olatT", h), ("QlT", h), "wuv"])
                P.op("dve", lambda e, b=b: e.scalar_tensor_tensor(out=ob16[0:n, h * 128:(h + 1) * 128], in0=psb[b][0:n, 0:128],
                                                                 scalar=rmall[0:n, h:h + 1], in1=zz[0:n, 512 + h * 128:512 + (h + 1) * 128],
                                                                 op0=ALU.mult, op1=ALU.mult), [("ps", b), "rm", ("zz", 1)], [("ob16", h)])

            cnt["attn"] = 1
            emit_S(0)
            if len(tiles) > 1:
                emit_S(1)
            tA = (len(tiles) * 5) // 8
            for t in range(len(tiles)):
                if t + 2 < len(tiles):
                    emit_S(t + 2)
                emit_PV(t)
                if t == tA and nxt is not None:
                    pB_prologueA(nxt)
                for (at, fn) in [p for p in pending if p[0] <= t]:
                    fn()
                pending[:] = [p for p in pending if p[0] > t]
            bk.pending = [fn for (_, fn) in pending]
            cnt["attn"] = 0

        def obank():
            k = cnt["ob"] % 4
            cnt["ob"] += 1
            return k

        def pB_stageO(bk, nxt):
            n, ti = bk.n, bk.ti
            xb = bk.xb
            if nxt is not None:
                pB_prologueB(nxt)
            for fn in bk.pending:
                fn()
            oaS = [("oaT", h) for h in range(4)]
            junk32 = junk[:, :].bitcast(F32)
            tg = tgs[bk.par]
            tmps = ((ropeA, ["ropeA"], ropeB, ["ropeB"]), (junk32, ["junk", "junk2"], ropeA, ["ropeA"]))

            def ngate(gi):
                if nxt is not None:
                    pB_gate(nxt, gi, obank())
                    nxt.gates_done = True
            for half in range(2):
                cs = slice(half * 512, (half + 1) * 512)
                b = obank()
                mm_group(b, 0, 512, n, [oaT[:, c * 128:c * 128 + n] for c in range(4)], [woa[:, c, cs] for c in range(4)], oaS + ["qdT", "woa"])
                ta, tak, tb, tbk = tmps[half]
                P.op("dve", lambda e, b=b, cs=cs, ta=ta: e.scalar_tensor_tensor(out=ta[0:n, :], in0=tg[0:n, cs], scalar=1.0, in1=psb[b][0:n, 0:512],
                                                                               op0=ALU.add, op1=ALU.mult), [("ps", b), ("tg", bk.par, half)], tak)
                ngate(half)
                if half == 0:
                    b = obank()
                    transposes([ob16[0:n, c * 128:(c + 1) * 128] for c in range(4)], n, b, [("ob16", h) for h in range(4)])
                    copy(obT[:, :].rearrange("p (c t) -> p c t", c=4)[:, :, 0:n], bfv(b)[:, 0:512].rearrange("p (c t) -> p c t", c=4)[:, :, 0:n],
                         [("ps", b)], ["obT"], eng="act")
            yield
            for half in range(2):
                cs = slice(half * 512, (half + 1) * 512)
                ta, tak, tb, tbk = tmps[half]
                b = obank()
                mm_group(b, 0, 512, n, [obT[:, c * 128:c * 128 + n] for c in range(4)], [wob[:, c, cs] for c in range(4)], ["obT", "wob"])
                P.op("dve", lambda e, b=b, half=half, tb=tb: e.scalar_tensor_tensor(out=tb[0:n, :], in0=tg[0:n, 1024 + half * 512:1536 + half * 512], scalar=1.0,
                                                                                   in1=psb[b][0:n, 0:512], op0=ALU.add, op1=ALU.mult),
                     [("ps", b), ("tg", bk.par, 2 + half)], tbk)
                P.op("dve", lambda e, cs=cs, ta=ta, tb=tb: e.tensor_tensor(out=m16[0:n, cs], in0=ta[0:n, :], in1=tb[0:n, :], op=ALU.add), tak + tbk, [("m16", half)])
                ngate(2 + half)
                b = obank()
                transposes([m16[0:n, c * 128:(c + 1) * 128] for c in range(4 * half, 4 * half + 4)], n, b, [("m16", half)])
                copy(mT[:, half * 512:(half + 1) * 512].rearrange("p (c t) -> p c t", c=4)[:, :, 0:n], bfv(b)[:, 0:512].rearrange("p (c t) -> p c t", c=4)[:, :, 0:n],
                     [("ps", b)], [("mT", half)], eng="act")
                yield
            xa = x32[xb]
            for half in range(2):
                cs = slice(half * 512, (half + 1) * 512)
                b = obank()
                mm_group(b, 0, 512, n, [mT[:, c * 128:c * 128 + n] for c in range(8)], [wout[:, c, cs] for c in range(8)], [("mT", 0), ("mT", 1), "wout"])
                P.op("dve", lambda e, b=b, cs=cs: e.tensor_tensor(out=xa[0:n, cs], in0=psb[b][0:n, 0:512], in1=xa[0:n, cs], op=ALU.add),
                     [("ps", b), ("x32", xb)], [("x32", xb)])
                yield
            P.op("act", lambda e: e.activation(out=junk[0:n, :], in_=xa[0:n, :], func=AF.Square, accum_out=S(20, n)),
                 [("x32", xb)], ["junk", "junk2", "rf_ssq"])
            rsqrt_chain(S(20), 1.0 / D, S(21), "rf", n)
            P.op("dve", lambda e: e.scalar_tensor_tensor(out=xa[0:n, :], in0=xa[0:n, :], scalar=S(21, n), in1=gfin[0:n, :], op0=ALU.mult, op1=ALU.mult),
                 [("x32", xb), "rf", "gfin"], [("x32", xb)])
            dma(bk.ydst, xa[0:n, :], [("x32", xb)], [], ("yo", xb))
            yield

        def passB_seq(blocks, pre=None):
            bks = []
            for (xsrc, n, ti, ktiles, diag, ydst) in blocks:
                bk = Blk()
                bk.xsrc, bk.n, bk.ti, bk.ktiles, bk.diag, bk.ydst = xsrc, n, ti, ktiles, diag, ydst
                bk.par = len(bks) % 2
                bk.gates_done = False
                bks.append(bk)
            if pre is None:
                pB_load(bks[0])
            else:
                bks[0].xb = pre
            pB_prologueA(bks[0])
            pB_prologueB(bks[0])
            for _ in pB_stageP(bks[0]):
                pass
            for i, bk in enumerate(bks):
                nxt = bks[i + 1] if i + 1 < len(bks) else None
                if nxt is not None:
                    pB_load(nxt)
                pB_attention(bk, nxt)
                gO = pB_stageO(bk, nxt)
                next(gO)
                gP = pB_stageP(nxt) if nxt is not None else iter(())
                doneO = doneP = False
                while not (doneO and doneP):
                    if not doneP:
                        try:
                            next(gP)
                        except StopIteration:
                            doneP = True
                    if not doneO:
                        try:
                            next(gO)
                        except StopIteration:
                            doneO = True

        def load_wkv():
            for c in range(8):
                dma(wkv[:, c, :], wkv_scr[c], ["wkv_scr"], ["wkv"], "wkvl")

        if stop <= 1:
            dma(x32[0], xp[0, 0:128, :], [], [("x32", 0)], ("x", 0))
        for s in range(nseq if stop >= 2 else 0):
            preA = {j: load_x(xp[s, j * 128:(j + 1) * 128, :], 128) for j in range(min(2, nblk))}
            load_wkv()
            zero_kpe()
            passA_seq(pre=preA, items=[(xp[s, i * 128:(i + 1) * 128, :], 128, i, i,
                        (kp[s, i * 128:(i + 1) * 128, :], vp[s, i * 128:(i + 1) * 128, :], cpo[s, i * 128:(i + 1) * 128, :], ppo[s, i * 128:(i + 1) * 128, :]))
                       for i in range(nblk)])
            P.barrier_all()
            zero_q()
            if stop >= 3:
                passB_seq([(xp[s, i * 128:(i + 1) * 128, :], 128, i, [(j, 128) for j in range(i + 1)], True, yp[s, i * 128:(i + 1) * 128, :])
                           for i in range(nblk)])
            P.barrier_all()
        if with_sample:
            for s in range(nseq):
                xa_ = load_x(xs[s], DEC)
                xb_ = load_x(xs[s], DEC)
                load_wkv()
                zero_kpe()
                passA_seq([(xs[s], DEC, 16, 16, (kso[s], vso[s], cso[s], pso[s]))], pre={0: xa_})
                P.barrier_all()
                cache_seq(s)
                P.barrier_all()
                zero_q()
                passB_seq([(xs[s], DEC, 16, [(j, 128) for j in range(16)] + [(16, DEC)], False, ys[s])], pre=xb_)
                P.barrier_all()

        P.finalize()
        sems = {e: es.enter_context(nc.semaphore(f"s_{e}")) for e in ENGS}
        dsems = {k: es.enter_context(nc.semaphore("d_" + "".join(ch for ch in str(k) if ch.isalnum()))) for k in P.dma_keys}
        run = P.emit(sems, dsems)
        with nc.Block() as block:
            @block.tensor
            def _(e):
                run("pe", e)

            @block.scalar
            def _(e):
                run("act", e)

            @block.vector
            def _(e):
                run("dve", e)

            @block.gpsimd
            def _(e):
                run("pool", e)

            @block.sync
            def _(e):
                run("sp", e)
    nc._prog_stats = {e: len(P.eng_ops[e]) for e in ENGS}
    return nc


def _consts():
    import ml_dtypes
    return {"ident": np.eye(128, dtype=np.float32).astype(ml_dtypes.bfloat16), "rope": rope_tables()}


def _weight_map(w_in, w_uq, w_uk, w_uv, w_oa, w_ob, w_out, lambda_q1, lambda_k1, lambda_q2, lambda_k2,
                norm_in, norm_qa, norm_kva, norm_subln, norm_final):
    f = lambda a: np.ascontiguousarray(np.asarray(a, dtype=np.float32))
    m = {
        "w_in": f(w_in[0]), "w_uq": f(w_uq[0]).reshape(512, 768), "w_uk": f(w_uk[0]).reshape(256, 512),
        "w_uv": f(w_uv[0]).reshape(256, 512), "w_oa": f(w_oa[0]), "w_ob": f(w_ob[0]), "w_out": f(w_out[0]),
        "lam4": f(np.stack([lambda_q1[0], lambda_k1[0], lambda_q2[0], lambda_k2[0]], axis=0)),
        "g_inT": f(np.asarray(norm_in[0]).reshape(8, 128).T), "g_qaT": f(np.asarray(norm_qa[0]).reshape(4, 128).T),
        "g_kva": f(norm_kva[0]).reshape(1, 256), "g_sub": f(norm_subln[0]).reshape(1, 128),
        "g_fin": f(norm_final).reshape(1, D),
    }
    m.update(_consts())
    return m


_NC_CACHE = {}


def kernel(x_prompt, x_sample, cache_diff_k, cache_diff_v, cache_mla_ckv, cache_mla_kpe,
           w_in, w_uq, w_uk, w_uv, w_oa, w_ob, w_out,
           lambda_q1, lambda_k1, lambda_q2, lambda_k2,
           norm_in, norm_qa, norm_kva, norm_subln, norm_final):
    NCORE = 8
    B, T, _ = x_prompt.shape
    nseq = B // NCORE
    nblk = T // 128
    key = (nseq, nblk)
    if key not in _NC_CACHE:
        _NC_CACHE[key] = build_program(nseq=nseq, nblk=nblk, with_sample=True)
    nc = _NC_CACHE[key]
    wm = _weight_map(w_in, w_uq, w_uk, w_uv, w_oa, w_ob, w_out, lambda_q1, lambda_k1, lambda_q2, lambda_k2,
                     norm_in, norm_qa, norm_kva, norm_subln, norm_final)
    f = lambda a: np.ascontiguousarray(np.asarray(a, dtype=np.float32))
    in_maps = []
    for c in range(NCORE):
        sl = slice(c * nseq, (c + 1) * nseq)
        m = dict(wm)
        m["xp"] = f(x_prompt[sl])
        m["xs"] = f(x_sample[sl])
        m["ck"] = f(cache_diff_k[0, sl]).reshape(nseq, PAST, 512)
        m["cv"] = f(cache_diff_v[0, sl]).reshape(nseq, PAST, 512)
        m["cc"] = f(cache_mla_ckv[0, sl])
        m["cpe"] = f(cache_mla_kpe[0, sl])
        in_maps.append(m)
    res = run_bass_kernel_spmd(nc, in_maps, core_ids=list(range(NCORE)))
    R = res.results
    cat = lambda k: np.concatenate([np.asarray(r[k], dtype=np.float32) for r in R], axis=0)
    y_p = cat("yp"); y_s = cat("ys")
    k_p = cat("kp").reshape(1, B, T, 4, 128); v_p = cat("vp").reshape(1, B, T, 4, 128)
    c_p = cat("cpo").reshape(1, B, T, 256); p_p = cat("ppo").reshape(1, B, T, 64)
    k_s = cat("ks").reshape(1, B, DEC, 4, 128); v_s = cat("vs").reshape(1, B, DEC, 4, 128)
    c_s = cat("cs").reshape(1, B, DEC, 256); p_s = cat("ps").reshape(1, B, DEC, 64)
    return (y_p, y_s, k_p, v_p, c_p, p_p, k_s, v_s, c_s, p_s)
```

```python
import numpy as np
import concourse.bass as bass
import concourse.mybir as mybir
from concourse.bass_utils import run_bass_kernel_spmd

F32 = mybir.dt.float32
BF16 = mybir.dt.bfloat16
AF = mybir.ActivationFunctionType
ALU = mybir.AluOpType

ENGS = ("pe", "act", "dve", "pool", "sp")


class Op:
    __slots__ = ("idx", "eng", "fn", "deps", "dma", "semkey", "sig", "val", "eidx")

    def __init__(self, idx, eng, fn, dma, semkey):
        self.idx = idx
        self.eng = eng
        self.fn = fn
        self.deps = set()
        self.dma = dma
        self.semkey = semkey
        self.sig = False
        self.val = 0
        self.eidx = 0


class Prog:
    def __init__(self, nc):
        self.nc = nc
        self.ops = []
        self.eng_ops = {e: [] for e in ENGS}
        self.last_w = {}
        self.rd_eng = {}
        self.rd_dma = {}
        self.last_x = {}

    def op(self, eng, fn, reads=(), writes=(), dma=False, semkey=None):
        o = Op(len(self.ops), eng, fn, dma, semkey)
        xs = [r for r in list(reads) + list(writes) if isinstance(r, tuple) and r[0] == "ps"]
        if xs:
            reads = [r for r in reads if r not in xs]
            writes = [r for r in writes if r not in xs]
            for r in set(xs):
                la = self.last_x.get(r)
                if la is not None and self.ops[la].eng != eng:
                    o.deps.add(la)
                self.last_x[r] = o.idx
        for r in reads:
            w = self.last_w.get(r)
            if w is not None:
                o.deps.add(w)
        for w_ in writes:
            w = self.last_w.get(w_)
            rdrs = self.rd_eng.get(w_, {})
            covered = (w is not None and not dma and not self.ops[w].dma and self.ops[w].eng == eng
                       and any(e2 != eng for e2 in rdrs))
            if w is not None and not covered:
                o.deps.add(w)
            for i in rdrs.values():
                o.deps.add(i)
            for i in self.rd_dma.get(w_, ()):
                o.deps.add(i)
        for r in reads:
            if dma:
                self.rd_dma.setdefault(r, []).append(o.idx)
            else:
                self.rd_eng.setdefault(r, {})[eng] = o.idx
        for w_ in writes:
            self.last_w[w_] = o.idx
            self.rd_eng[w_] = {}
            self.rd_dma[w_] = []
        o.deps.discard(o.idx)
        o.eidx = len(self.eng_ops[eng])
        self.ops.append(o)
        self.eng_ops[eng].append(o)
        return o

    def barrier_all(self):
        lasts = []
        for e in ENGS:
            lst = [o for o in self.eng_ops[e] if not o.dma and o.fn is not None]
            if lst:
                lasts.append(lst[-1].idx)
        lastd = {}
        for o in self.ops:
            if o.dma:
                lastd[o.semkey] = o.idx
        self._pending_barrier = set(lasts) | set(lastd.values())
        for e in ENGS:
            eo = e
            o = self.op(eo, None, (), ())
            o.deps |= {d for d in self._pending_barrier if d != o.idx}

    def finalize(self):
        ops = self.ops
        for o in ops:
            for d in o.deps:
                od = ops[d]
                if od.dma:
                    od.sig = True
                elif od.eng == "pe" and o.eng == "pe" and not o.dma:
                    continue
                else:
                    od.sig = True
        cnt = {e: 0 for e in ENGS}
        dcnt = {}
        for o in ops:
            if o.dma:
                dcnt[o.semkey] = dcnt.get(o.semkey, 0) + 16
                o.val = dcnt[o.semkey]
            else:
                if o.sig:
                    cnt[o.eng] += 1
                o.val = cnt[o.eng]
        self.dma_keys = list(dcnt.keys())
        self.final_dma = dict(dcnt)

    def plan_waits(self):
        ops = self.ops
        prod = {}
        for o in ops:
            if o.dma:
                prod[(("d", o.semkey), o.val)] = o.idx
            elif o.sig:
                prod[(("e", o.eng), o.val)] = o.idx
        k_eng = {e: {} for e in ENGS}
        k_done = {}
        plan = {}
        for o in ops:
            e = o.eng
            need = {}
            for d in o.deps:
                od = ops[d]
                if od.dma:
                    s = ("d", od.semkey)
                else:
                    if od.eng == "pe" and e == "pe" and not o.dma:
                        continue
                    s = ("e", od.eng)
                if od.val > need.get(s, 0):
                    need[s] = od.val
            K = k_eng[e]
            todo = []
            for s, v in sorted(need.items(), key=lambda kv: -prod.get((kv[0], kv[1]), -1)):
                if K.get(s, 0) >= v:
                    continue
                todo.append((s, v))
                K[s] = v
                p = prod.get((s, v))
                if p is not None and p in k_done:
                    for s2, v2 in k_done[p].items():
                        if v2 > K.get(s2, 0):
                            K[s2] = v2
            plan[o.idx] = todo
            if o.dma:
                kd = dict(K)
                kd[("d", o.semkey)] = o.val
                k_done[o.idx] = kd
            elif o.sig:
                kd = dict(K)
                kd[("e", e)] = o.val
                k_done[o.idx] = kd
        return plan

    def emit(self, sems, dsems):
        plan = self.plan_waits()

        def run(e, eng):
            for o in self.eng_ops[e]:
                todo = [(dsems[s[1]] if s[0] == "d" else sems[s[1]], v) for (s, v) in plan[o.idx]]
                attach = None
                if todo and o.fn is not None and not o.dma:
                    attach = todo.pop(0)
                for sem, v in todo:
                    eng.wait_ge(sem, v)
                if o.fn is None:
                    continue
                ins = o.fn(eng)
                if attach is not None:
                    ins.wait_op(attach[0], attach[1], "sem-ge")
                if o.dma:
                    ins.then_inc(dsems[o.semkey], 16)
                elif o.sig:
                    ins.then_inc(sems[e], 1)
            if e == "sp":
                for k, v in self.final_dma.items():
                    eng.wait_ge(dsems[k], v)

        return run


D = 1024
DIN = 5440
NKT = 17
NKEY = NKT * 128
LAM_INIT = 0.2
OSC = 1.0 - LAM_INIT
EPS = 1e-6
DIFF_SCALE = 0.125
MLA_SCALE = 192.0 ** -0.5
PAST = 2048
DEC = 16


class Carver:
    def __init__(self, big, off, limit):
        self.big, self.off, self.limit, self.base = big, off, limit, off

    def take(self, cols, dt):
        nb = cols * (4 if dt == F32 else 2)
        nb = (nb + 63) // 64 * 64
        assert self.off + nb <= self.limit, ("SBUF carve overflow", self.off + nb, self.limit, self.base)
        a = self.big[:, self.off // 4:(self.off + nb) // 4]
        self.off += nb
        if dt != F32:
            a = a.bitcast(dt)
        return a[:, 0:cols]


def rope_tables():
    half = 32
    inv = (10000.0 ** (-np.arange(half, dtype=np.float32) * 2.0 / 64)).astype(np.float32)
    pos = np.arange(NKT * 128, dtype=np.float32)
    ang = (pos[:, None] * inv[None, :]).astype(np.float32)
    c = np.cos(ang).astype(np.float32).reshape(NKT, 128, half).transpose(1, 0, 2)
    s = np.sin(ang).astype(np.float32).reshape(NKT, 128, half).transpose(1, 0, 2)
    return np.ascontiguousarray(np.stack([c, s], axis=0))


def build_program(nseq=4, nblk=16, with_sample=True, sbuf_kb=223, stop=99):
    from contextlib import ExitStack
    T = nblk * 128
    nc = bass.Bass("TRN2", target_bir_lowering=False, dynamic_dma_scratch_size=256)

    def din(name, shape, dt=F32):
        return nc.dram_tensor(name, list(shape), dt, kind="ExternalInput").ap()

    def dout(name, shape, dt=F32):
        return nc.dram_tensor(name, list(shape), dt, kind="ExternalOutput").ap()

    xp = din("xp", [nseq, T, D])
    w_in = din("w_in", [D, DIN]); w_uq = din("w_uq", [512, 768]); w_uk = din("w_uk", [256, 512])
    w_uv = din("w_uv", [256, 512]); w_oa = din("w_oa", [512, D]); w_ob = din("w_ob", [512, D])
    w_out = din("w_out", [D, D]); lam4 = din("lam4", [4, 64])
    g_inT = din("g_inT", [128, 8]); g_qaT = din("g_qaT", [128, 4]); g_kva = din("g_kva", [1, 256])
    g_sub = din("g_sub", [1, 128]); g_fin = din("g_fin", [1, D])
    ident_d = din("ident", [128, 128], BF16); rope_d = din("rope", [2, 128, NKT, 32])
    yp = dout("yp", [nseq, T, D]); kp = dout("kp", [nseq, T, 512]); vp = dout("vp", [nseq, T, 512])
    cpo = dout("cpo", [nseq, T, 256]); ppo = dout("ppo", [nseq, T, 64])
    if with_sample:
        xs = din("xs", [nseq, DEC, D]); ck = din("ck", [nseq, PAST, 512]); cv = din("cv", [nseq, PAST, 512])
        cckv = din("cc", [nseq, PAST, 256]); cpe = din("cpe", [nseq, PAST, 64])
        ys = dout("ys", [nseq, DEC, D]); kso = dout("ks", [nseq, DEC, 512]); vso = dout("vs", [nseq, DEC, 512])
        cso = dout("cs", [nseq, DEC, 256]); pso = dout("ps", [nseq, DEC, 64])
    wkv_scr = nc.dram_tensor("wkv_scr", [8, 128, 1344], BF16, kind="Internal").ap()

    es = ExitStack()
    with es:
        big = es.enter_context(nc.sbuf_tensor("big", [128, sbuf_kb * 256], F32))
        psb = [es.enter_context(nc.psum_tensor(f"psb{i}", [128, 512], F32)) for i in range(8)]
        P = Prog(nc)
        R = Carver(big, 0, sbuf_kb * 1024)
        wB = R.take(8 * 4096, BF16).rearrange("p (c n) -> p c n", c=8)
        Wabs = R.take(4 * 1024, BF16).rearrange("p (c n) -> p c n", c=4)
        wpe = R.take(4 * 256, BF16).rearrange("p (c n) -> p c n", c=4)
        wuv = R.take(2 * 512, BF16).rearrange("p (c n) -> p c n", c=2)
        woa = R.take(4 * 1024, BF16).rearrange("p (c n) -> p c n", c=4)
        wob = R.take(4 * 1024, BF16).rearrange("p (c n) -> p c n", c=4)
        wout = R.take(8 * 1024, BF16).rearrange("p (c n) -> p c n", c=8)
        KT = R.take(4 * NKEY, BF16).rearrange("p (h k) -> p h k", h=4)
        Vaug = R.take(NKT * 4 * 130, BF16).rearrange("p (j h e) -> p j h e", j=NKT, h=4)
        ckvT = R.take(2 * NKEY, BF16).rearrange("p (c k) -> p c k", c=2)
        ckvaug = R.take(NKT * 258, BF16).rearrange("p (j e) -> p j e", j=NKT)
        kpeT = R.take(NKEY, BF16)
        ident = R.take(128, BF16)
        ropeT = R.take(2 * NKT * 32, F32).rearrange("p (a j e) -> p a j e", a=2, j=NKT)
        gkva = R.take(256, F32); gsub8 = R.take(128, F32); gfin = R.take(D, F32)
        ginT = R.take(8, F32); gqaT = R.take(4, F32)
        rstd_all = R.take(NKT, F32)
        sc_c = R.take(16, F32)
        mhalf = sc_c[:, 0:1]; nlam = sc_c[:, 1:2]; ld1 = sc_c[:, 2:3]; ld2 = sc_c[:, 3:4]
        U0 = R.off
        nc._u0 = U0
        ULIM = sbuf_kb * 1024

        cnt = {"ps": 0, "s": 0, "pt": 0, "acc": 0, "pa": 0, "ob": 0, "attn": 0}

        def proj_bank():
            if cnt["attn"]:
                return 0
            k = cnt["ps"] % 2
            cnt["ps"] += 1
            return k

        def s_bank():
            k = (2, 3, 1)[cnt["s"] % 3]
            cnt["s"] += 1
            return k

        def bfv(k):
            return psb[k][:, :].bitcast(BF16)

        evac_rr = [0]

        def copy(out, in_, reads, writes, eng=None):
            if eng is None:
                eng = ("dve", "act")[evac_rr[0] % 2]
                evac_rr[0] += 1
            if eng == "act":
                P.op("act", lambda e: e.activation(out=out, in_=in_, func=AF.Copy), reads, writes)
            elif eng == "dve":
                P.op("dve", lambda e: e.tensor_copy(out=out, in_=in_), reads, writes)
            else:
                P.op("pool", lambda e: e.tensor_copy(out=out, in_=in_), reads, writes)

        def scaled(out, in_, sc, reads, writes, eng):
            if eng == "act":
                P.op("act", lambda e: e.activation(out=out, in_=in_, func=AF.Copy, scale=sc), reads, writes)
            else:
                P.op("dve", lambda e: e.tensor_scalar(out=out, in0=in_, scalar1=sc, scalar2=None, op0=ALU.mult), reads, writes)

        def dma(out, in_, reads, writes, key, q="sp"):
            P.op(q, lambda e: e.dma_start(out=out, in_=in_), reads, writes, dma=True, semkey=key)

        def transposes(srcs, n, bank, rd):
            bv = bfv(bank)
            for k, src in enumerate(srcs):
                m = src.shape[1]
                P.op("pe", lambda e, k=k, src=src, m=m: e.transpose(out=bv[0:m, k * 128:k * 128 + n], in_=src, identity=ident[0:n, 0:n]),
                     reads=list(rd) + ["ident"], writes=[("ps", bank)])

        def mm_group(bank, col0, ncols, n, lhs_list, rhs_list, rd):
            nk = len(lhs_list)
            for k in range(nk):
                P.op("pe", lambda e, k=k: e.matmul(psb[bank][0:n, col0:col0 + ncols], lhsT=lhs_list[k], rhs=rhs_list[k],
                                                    start=(k == 0), stop=(k == nk - 1)),
                     reads=rd, writes=[("ps", bank)])

        def rsqrt_chain(ssq, mul, out, nm, n, extra=None):
            P.op("dve", lambda e: e.tensor_scalar(out=out[0:n], in0=ssq[0:n], scalar1=mul, scalar2=EPS, op0=ALU.mult, op1=ALU.add),
                 reads=[nm + "_ssq"], writes=[nm])
            P.op("pool", lambda e: e.tensor_tensor(out=out[0:n], in0=out[0:n], in1=mhalf[0:n], op=ALU.pow),
                 reads=[nm, "consts"], writes=[nm])

        dma(ident, ident_d, [], ["ident"], "c0")
        dma(ropeT, rope_d.rearrange("a p j e -> p a j e"), [], ["rope"], "c1")
        dma(gkva, g_kva.partition_broadcast(128), [], ["gkva"], "c2")
        dma(gsub8, g_sub.partition_broadcast(128), [], ["gsub8"], "c3")
        dma(gfin, g_fin.partition_broadcast(128), [], ["gfin"], "c4")
        dma(ginT, g_inT, [], ["ginT"], "c5")
        dma(gqaT, g_qaT, [], ["gqaT"], "c6")
        P.op("pool", lambda e: e.memset(sc_c, -0.5), [], ["consts"])
        P.op("dve", lambda e: e.tensor_scalar(out=gsub8, in0=gsub8, scalar1=OSC, scalar2=None, op0=ALU.mult), ["gsub8"], ["gsub8"])
        P.op("pool", lambda e: e.memset(Vaug, 1.0), [], ["Vaug"])
        P.op("pool", lambda e: e.memset(ckvaug, 1.0), [], ["ckvaug"])
        P.op("pool", lambda e: e.memset(kpeT, 0.0), [], ["kpeT"])

        U = Carver(big, U0, ULIM)
        stage = [U.take(2880, F32) for _ in range(2)]
        wkv16 = U.take(1344, BF16)
        lamj = U.take(64, F32)
        lamt = U.take(4 * 64, F32).rearrange("p (a e) -> p a e", a=4)
        for a in range(4):
            dma(lamt[:, a, :], lam4[a:a + 1, :].partition_broadcast(128), [], ["lamt"], "c7")
        P.op("dve", lambda e: e.scalar_tensor_tensor(out=lamj, in0=lamt[:, 0, :], scalar=1.0, in1=lamt[:, 1, :], op0=ALU.mult, op1=ALU.mult, accum_out=ld1),
             ["lamt", "consts"], ["lamj", "ld1"])
        P.op("dve", lambda e: e.scalar_tensor_tensor(out=lamj, in0=lamt[:, 2, :], scalar=1.0, in1=lamt[:, 3, :], op0=ALU.mult, op1=ALU.mult, accum_out=ld2),
             ["lamt", "ld1"], ["lamj", "ld2"])
        P.op("act", lambda e: e.activation(out=ld1, in_=ld1, func=AF.Exp), ["ld1"], ["ld1"])
        P.op("act", lambda e: e.activation(out=ld2, in_=ld2, func=AF.Exp), ["ld2"], ["ld2"])
        P.op("dve", lambda e: e.scalar_tensor_tensor(out=nlam, in0=ld2, scalar=-LAM_INIT, in1=ld1, op0=ALU.add, op1=ALU.subtract),
             ["ld1", "ld2"], ["nlam"])
        segs0 = [(0, 512, 0), (2048, 2560, 512), (1536, 2048, 1024)]
        segs1 = [(2880, 3392, 1536), (3392, 5440, 2048)]
        for c in range(8):
            gc = ginT[:, c:c + 1]
            dma(stage[0], w_in[c * 128:(c + 1) * 128, 0:2880], [], [("stage", 0)], ("stg", 0))
            for si, (a, b, o) in enumerate(segs0):
                scaled(wB[:, c, o:o + (b - a)], stage[0][:, a:b], gc, [("stage", 0), "ginT"], ["wB"], ("act", "dve")[si % 2])
            scaled(wkv16[:, 0:1024], stage[0][:, 512:1536], gc, [("stage", 0), "ginT"], ["wkv16"], "dve")
            scaled(wkv16[:, 1024:1344], stage[0][:, 2560:2880], gc, [("stage", 0), "ginT"], ["wkv16"], "act")
            dma(wkv_scr[c], wkv16, ["wkv16"], ["wkv_scr"], "wkvo")
            dma(stage[1][:, 0:2560], w_in[c * 128:(c + 1) * 128, 2880:5440], [], [("stage", 1)], ("stg", 1))
            for si, (a, b, o) in enumerate(segs1):
                scaled(wB[:, c, o:o + (b - a)], stage[1][:, a - 2880:b - 2880], gc, [("stage", 1), "ginT"], ["wB"], ("dve", "act")[si % 2])
        P.barrier_all()

        U = Carver(big, U0, ULIM)
        st2 = [U.take(1024, F32) for _ in range(2)]
        wuqn16 = U.take(4 * 512, BF16).rearrange("p (c n) -> p c n", c=4)
        wuk16 = U.take(2 * 512, BF16).rearrange("p (c n) -> p c n", c=2)
        wukT = U.take(4 * 256, BF16).rearrange("p (h n) -> p h n", h=4)
        wuqT = U.take(4 * 512, BF16).rearrange("p (h n) -> p h n", h=4)
        k2 = [0]

        def stage2(src, cols):
            sb = k2[0] % 2
            k2[0] += 1
            dma(st2[sb][:, 0:cols], src, [], [("st2", sb)], ("st2", sb))
            return st2[sb], ("st2", sb)

        for c in range(4):
            st, r = stage2(w_uq[c * 128:(c + 1) * 128, :], 768)
            s3 = st[:, 0:768].rearrange("p (h e) -> p h e", h=4)
            scaled(wuqn16[:, c, :].rearrange("p (h e) -> p h e", h=4), s3[:, :, 0:128], gqaT[:, c:c + 1], [r, "gqaT"], ["wuqn16"], "dve")
            scaled(wpe[:, c, :].rearrange("p (h e) -> p h e", h=4), s3[:, :, 128:192], gqaT[:, c:c + 1], [r, "gqaT"], ["wpe"], "act")
        for c in range(2):
            st, r = stage2(w_uk[c * 128:(c + 1) * 128, :], 512)
            copy(wuk16[:, c, :], st[:, 0:512], [r], ["wuk16"])
            st, r = stage2(w_uv[c * 128:(c + 1) * 128, :], 512)
            copy(wuv[:, c, :], st[:, 0:512], [r], ["wuv"])
        for (wsrc, wdst, nm, nch) in ((w_oa, woa, "woa", 4), (w_ob, wob, "wob", 4), (w_out, wout, "wout", 8)):
            for c in range(nch):
                st, r = stage2(wsrc[c * 128:(c + 1) * 128, :], 1024)
                scaled(wdst[:, c, :], st[:, 0:1024], 0.5, [r], [nm], ("act", "dve")[c % 2])
        for h in range(4):
            b = proj_bank()
            transposes([wuk16[:, cc, h * 128:(h + 1) * 128] for cc in range(2)], 128, b, ["wuk16"])
            copy(wukT[:, h, :], bfv(b)[:, 0:256], [("ps", b)], ["wukT"])
        for h in range(4):
            b = proj_bank()
            transposes([wuqn16[:, cc, h * 128:(h + 1) * 128] for cc in range(4)], 128, b, ["wuqn16"])
            copy(wuqT[:, h, :], bfv(b)[:, 0:512], [("ps", b)], ["wuqT"])
        for cc in range(4):
            for hp in range(2):
                b = proj_bank()
                for hh in range(2):
                    h = hp * 2 + hh
                    mm_group(b, hh * 256, 256, 128, [wuqT[:, h, cc * 128:(cc + 1) * 128]], [wukT[:, h, :]], ["wuqT", "wukT"])
                copy(Wabs[:, cc, hp * 512:(hp + 1) * 512], psb[b][:, 0:512], [("ps", b)], ["Wabs"])
        P.barrier_all()

        U = Carver(big, U0, ULIM)
        x32 = [U.take(D, F32) for _ in range(2)]
        ropeA = U.take(512, F32); ropeB = U.take(512, F32)
        scal = U.take(64, F32)
        xbf = U.take(D, BF16); hT = U.take(D, BF16); junk = U.take(D, BF16)
        UA0 = U.off
        oh32 = U.take(128, F32)
        qa16 = U.take(512, BF16); qd16 = U.take(512, BF16); qdT = U.take(512, BF16)
        zz = U.take(1024, BF16); tgs = [U.take(2048, BF16) for _ in range(2)]
        qlat16 = U.take(1024, BF16); qpe16 = U.take(512, BF16)
        QlT = U.take(1024, BF16)
        PT = [U.take(384, BF16) for _ in range(3)]
        ob16 = U.take(512, BF16); obT = U.take(512, BF16)
        m16 = U.take(1024, BF16); mT = U.take(1024, BF16)
        Q12z = U.take(1024, BF16).rearrange("p (h a q) -> p h a q", h=4, a=2); qpeT = U.take(512, BF16)

        def zero_kpe():
            P.op("pool", lambda e: e.memset(kpe16, 0.0), [], ["kpe16"])

        def zero_q():
            P.op("pool", lambda e: e.memset(qpe16, 0.0), [], ["qpe16"])
            P.op("pool", lambda e: e.memset(qpeT, 0.0), [], ["qpeT"])
            P.op("pool", lambda e: e.memset(Q12z, 0.0), [], ["Q12z"])
        oa16, oaT, olat16, olatT = qd16, qdT, qlat16, QlT
        rmall = scal[:, 32:36]
        UA = Carver(big, UA0, ULIM)
        wkv = UA.take(8 * 1344, BF16).rearrange("p (c n) -> p c n", c=8)
        kvout = UA.take(1344, F32)
        ka16 = UA.take(512, BF16); kpe16 = UA.take(128, BF16); ckv16 = UA.take(256, BF16)
        xcnt = [0]


        def S(i, n=128):
            return scal[0:n, i:i + 1]

        def rope(src, sc, ti, G, n, dst, rd, wr, dst_is_f32):
            W = G * 64
            cosb = ropeT[0:n, 0, ti, :].unsqueeze(1).broadcast_to([n, 2 * G, 32])
            sinb = ropeT[0:n, 1, ti, :].unsqueeze(1).broadcast_to([n, G, 32])
            nsc = S(23, n)
            P.op("dve", lambda e: e.tensor_scalar(out=nsc, in0=sc, scalar1=-1.0, scalar2=None, op0=ALU.mult), [r for r in rd if not (isinstance(r, tuple) and r[0] == "ps")], ["nsc"])
            g64 = lambda ap: ap.rearrange("p (g j) -> p g j", j=64)
            t1 = dst if dst_is_f32 else g64(ropeA[0:n, 0:W])
            t1n = wr if dst_is_f32 else ["ropeA"]
            v4 = lambda ap: ap.rearrange("p (g t j) -> p g t j", t=2, j=32)
            P.op("dve", lambda e: e.scalar_tensor_tensor(out=t1.rearrange("p g (t j) -> p (g t) j", j=32), in0=src.rearrange("p (g j) -> p g j", j=32),
                                                        scalar=sc, in1=cosb, op0=ALU.mult, op1=ALU.mult),
                 rd + ["rope"], t1n)
            t2 = ropeB[0:n, 0:W]
            P.op("dve", lambda e: e.scalar_tensor_tensor(out=v4(t2)[:, :, 0, :], in0=v4(src)[:, :, 1, :], scalar=nsc, in1=sinb, op0=ALU.mult, op1=ALU.mult),
                 rd + ["rope", "nsc"], ["ropeB"])
            P.op("dve", lambda e: e.scalar_tensor_tensor(out=v4(t2)[:, :, 1, :], in0=v4(src)[:, :, 0, :], scalar=sc, in1=sinb, op0=ALU.mult, op1=ALU.mult),
                 rd + ["rope", "ropeB"], ["ropeB"])
            P.op("pool", lambda e: e.tensor_tensor(out=dst, in0=t1, in1=g64(t2), op=ALU.add), t1n + ["ropeB"], wr)

        def load_x(src_ap, n):
            xb = xcnt[0] % 2
            xcnt[0] += 1
            dma(x32[xb][0:n, :], src_ap, [], [("x32", xb)], ("x", xb))
            return xb

        def norm_and_T(xb, n, want_rstd, ti, bank=None, evac="dve"):
            xa = x32[xb]
            if want_rstd:
                P.op("act", lambda e: e.activation(out=junk[0:n, :], in_=xa[0:n, :], func=AF.Square, accum_out=S(0, n)),
                     [("x32", xb)], ["junk", "rx_ssq"])
                rsqrt_chain(S(0), 1.0 / D, S(1), "rx", n)
                P.op("dve", lambda e: e.tensor_copy(out=rstd_all[0:n, ti:ti + 1], in_=S(1, n)), ["rx"], [("rstd", ti)])
            copy(xbf[0:n, :], xa[0:n, :], [("x32", xb)], ["xbf"], eng="act")
            b = proj_bank() if bank is None else bank
            transposes([xbf[0:n, c * 128:(c + 1) * 128] for c in range(8)], n, b, ["xbf"])
            copy(hT[:, :].rearrange("p (c t) -> p c t", c=8)[:, :, 0:n], bfv(b)[:, :].rearrange("p (c t) -> p c t", c=8)[:, :, 0:n],
                 [("ps", b)], ["hT"], eng=evac)

        def hT_c(c, n):
            return hT[:, c * 128:c * 128 + n]

        def bankA():
            k = cnt["pa"] % 8
            cnt["pa"] += 1
            return k

        def passA_part1(xb, n, ti):
            norm_and_T(xb, n, True, ti, bank=6)
            hl = [hT_c(c, n) for c in range(8)]
            k0 = 3 * (bankA() % 2)
            bc, bk_, bv = k0, k0 + 1, k0 + 2
            mm_group(bc, 0, 320, n, hl, [wkv[:, c, 1024:1344] for c in range(8)], ["hT", "wkv"])
            mm_group(bk_, 0, 512, n, hl, [wkv[:, c, 0:512] for c in range(8)], ["hT", "wkv"])
            mm_group(bv, 0, 512, n, hl, [wkv[:, c, 512:1024] for c in range(8)], ["hT", "wkv"])
            return (bc, bk_, bv)

        def passA_part2(banks, n, ti, kt, outs):
            ko, vo, co, po = outs
            bc, bk_, bv = banks
            rs = rstd_all[0:n, ti:ti + 1]
            rrs = [("rstd", ti)]
            g64 = lambda ap: ap.rearrange("p (g j) -> p g j", j=64)
            P.op("act", lambda e: e.activation(out=junk[0:n, 0:256], in_=psb[bc][0:n, 0:256], func=AF.Square, accum_out=S(2, n)),
                 [("ps", bc)], ["junk", "rc_ssq"])
            P.op("dve", lambda e: e.tensor_scalar(out=S(22, n), in0=rs, scalar1=rs, scalar2=1.0 / 256, op0=ALU.mult, op1=ALU.mult), rrs, ["tmp22"])
            P.op("dve", lambda e: e.tensor_scalar(out=S(3, n), in0=S(2, n), scalar1=S(22, n), scalar2=EPS, op0=ALU.mult, op1=ALU.add),
                 ["rc_ssq", "tmp22"], ["rc"])
            P.op("pool", lambda e: e.tensor_tensor(out=S(3, n), in0=S(3, n), in1=mhalf[0:n], op=ALU.pow), ["rc", "consts"], ["rc"])
            rope(psb[bk_][0:n, 0:512], rs, ti, 8, n, g64(kvout[0:n, 0:512]), [("ps", bk_)] + rrs, ["kv_k"], True)
            scaled(kvout[0:n, 512:1024], psb[bv][0:n, 0:512], rs, [("ps", bv)] + rrs, ["kv_v"], "act")
            copy(ka16[0:n, :], kvout[0:n, 0:512], ["kv_k"], ["ka16"], eng="act")
            dma(ko, kvout[0:n, 0:512], ["kv_k"], [], "oA0")
            dma(vo, kvout[0:n, 512:1024], ["kv_v"], [], "oA1")
            P.op("dve", lambda e: e.tensor_tensor(out=S(4, n), in0=S(3, n), in1=rs, op=ALU.mult), ["rc"] + rrs, ["sc_c"])
            P.op("dve", lambda e: e.scalar_tensor_tensor(out=kvout[0:n, 1024:1280], in0=psb[bc][0:n, 0:256], scalar=S(4, n), in1=gkva[0:n, :],
                                                        op0=ALU.mult, op1=ALU.mult), [("ps", bc), "sc_c", "gkva"], ["kv_c"])
            copy(ckv16[0:n, :], kvout[0:n, 1024:1280], ["kv_c"], ["ckv16"], eng="act")
            dma(co, kvout[0:n, 1024:1280], ["kv_c"], [], "oA2")
            rope(psb[bc][0:n, 256:320], rs, ti, 1, n, g64(kvout[0:n, 1280:1344]), [("ps", bc)] + rrs, ["kv_p"], True)
            copy(kpe16[0:n, 0:64], kvout[0:n, 1280:1344], ["kv_p"], ["kpe16"], eng="act")
            dma(po, kvout[0:n, 1280:1344], ["kv_p"], [], "oA3")
            P.op("pool", lambda e: e.tensor_copy(out=Vaug[0:n, kt, :, 0:128], in_=kvout[0:n, 512:1024].rearrange("p (h e) -> p h e", h=4)),
                 ["kv_v"], ["Vaug"])
            P.op("pool", lambda e: e.tensor_copy(out=ckvaug[0:n, kt, 0:256], in_=ckv16[0:n, :]), ["ckv16"], ["ckvaug"])
            b = 7
            transposes([ka16[0:n, h * 128:(h + 1) * 128] for h in range(4)], n, b, ["ka16"])
            copy(KT[:, :, kt * 128:kt * 128 + n], bfv(b)[:, 0:512].rearrange("p (h t) -> p h t", h=4)[:, :, 0:n], [("ps", b)], ["KT"], eng="dve")
            b = 6
            transposes([ckv16[0:n, c * 128:(c + 1) * 128] for c in range(2)] + [kpe16[0:n, :]], n, b, ["ckv16", "kpe16"])
            copy(ckvT[:, :, kt * 128:kt * 128 + n], bfv(b)[:, 0:256].rearrange("p (c t) -> p c t", c=2)[:, :, 0:n], [("ps", b)], ["ckvT"], eng="dve")
            copy(kpeT[:, kt * 128:kt * 128 + n], bfv(b)[:, 256:256 + n], [("ps", b)], ["kpeT"], eng="dve")

        def passA_seq(items, pre=None):
            xbs = dict(pre or {})
            for j in range(min(2, len(items))):
                if j not in xbs:
                    xbs[j] = load_x(items[j][0], items[j][1])
            banks = passA_part1(xbs[0], items[0][1], items[0][2])
            for i, (xsrc, n, ti, kt, outs) in enumerate(items):
                if i + 2 < len(items):
                    xbs[i + 2] = load_x(items[i + 2][0], items[i + 2][1])
                nb = None
                if i + 1 < len(items):
                    nb = passA_part1(xbs[i + 1], items[i + 1][1], items[i + 1][2])
                passA_part2(banks, n, ti, kt, outs)
                banks = nb

        def cache_seq(s):
            UC = Carver(big, UA0, ULIM)
            stg = [UC.take(1344, F32) for _ in range(3)]
            k16 = [UC.take(512, BF16) for _ in range(2)]
            c16 = [UC.take(256, BF16) for _ in range(2)]
            p16 = [UC.take(128, BF16) for _ in range(2)]
            for k in range(2):
                P.op("pool", lambda e, k=k: e.memset(p16[k], 0.0), [], [("p16", k)])

            def loads(kt):
                c0 = stg[kt % 3]
                r = slice(kt * 128, (kt + 1) * 128)
                q = kt % 3
                dma(c0[:, 0:512], ck[s, r, :], [], [("cstg", q, 0)], ("cA0", q))
                dma(c0[:, 512:1024], cv[s, r, :], [], [("cstg", q, 1)], ("cA1", q))
                dma(c0[:, 1024:1280], cckv[s, r, :], [], [("cstg", q, 2)], ("cA2", q))
                dma(c0[:, 1280:1344], cpe[s, r, :], [], [("cstg", q, 3)], ("cA3", q))

            loads(0)
            loads(1)
            for kt in range(16):
                if kt + 2 < 16:
                    loads(kt + 2)
                q, w = kt % 3, kt % 2
                c0 = stg[q]
                copy(k16[w][:, :], c0[:, 0:512], [("cstg", q, 0)], [("k16", w)], eng="act")
                copy(c16[w][:, :], c0[:, 1024:1280], [("cstg", q, 2)], [("c16", w)], eng="dve")
                copy(p16[w][:, 0:64], c0[:, 1280:1344], [("cstg", q, 3)], [("p16", w)], eng="act")
                v3 = c0[:, 512:1024].rearrange("p (h e) -> p h e", h=4)
                P.op("dve", lambda e, kt=kt, v3=v3: e.tensor_copy(out=Vaug[:, kt, 0:2, 0:128], in_=v3[:, 0:2, :]), [("cstg", q, 1)], [("Vaug", kt, 0)])
                P.op("pool", lambda e, kt=kt, v3=v3: e.tensor_copy(out=Vaug[:, kt, 2:4, 0:128], in_=v3[:, 2:4, :]), [("cstg", q, 1)], [("Vaug", kt, 1)])
                P.op("pool", lambda e, kt=kt, w=w: e.tensor_copy(out=ckvaug[:, kt, 0:256], in_=c16[w][:, :]), [("c16", w)], ["ckvaug"])
                b = bankA()
                transposes([k16[w][:, h * 128:(h + 1) * 128] for h in range(4)], 128, b, [("k16", w)])
                copy(KT[:, :, kt * 128:(kt + 1) * 128], bfv(b)[:, 0:512].rearrange("p (h t) -> p h t", h=4), [("ps", b)], ["KT"], eng="dve")
                b = bankA()
                transposes([c16[w][:, c * 128:(c + 1) * 128] for c in range(2)] + [p16[w][:, :]], 128, b, [("c16", w), ("p16", w)])
                copy(ckvT[:, :, kt * 128:(kt + 1) * 128], bfv(b)[:, 0:256].rearrange("p (c t) -> p c t", c=2), [("ps", b)], ["ckvT"], eng="act")
                copy(kpeT[:, kt * 128:(kt + 1) * 128], bfv(b)[:, 256:384], [("ps", b)], ["kpeT"], eng="act")

        def wcols(a, b_):
            return [wB[:, c, a:b_] for c in range(8)]

        class Blk:
            pass

        def pB_load(bk):
            bk.xb = load_x(bk.xsrc, bk.n)

        def pB_prologueA(bk):
            n, ti = bk.n, bk.ti
            norm_and_T(bk.xb, n, False, ti, evac="act")
            bk.rs = rstd_all[0:n, ti:ti + 1]
            bk.rrs = [("rstd", ti)]
            rs, rrs = bk.rs, bk.rrs
            P.op("dve", lambda e: e.tensor_scalar(out=S(7, n), in0=rs, scalar1=DIFF_SCALE, scalar2=None, op0=ALU.mult), rrs, ["sq8"])
            P.op("dve", lambda e: e.tensor_scalar(out=S(8, n), in0=rs, scalar1=0.5, scalar2=None, op0=ALU.mult), rrs, ["hrs"])
            bk.hl = [hT_c(c, n) for c in range(8)]

        def pB_prologueB(bk):
            n, ti = bk.n, bk.ti
            b = proj_bank()
            mm_group(b, 0, 512, n, bk.hl, wcols(0, 512), ["hT", "wB"])
            rope(psb[b][0:n, 0:512], S(7, n), ti, 8, n, qa16[0:n, :].rearrange("p (g j) -> p g j", j=64), [("ps", b), "sq8"], ["qa16"], False)

        def pB_gate(bk, gi, b):
            n = bk.n
            tg = tgs[bk.par]
            mm_group(b, 0, 512, n, bk.hl, wcols(2048 + gi * 512, 2560 + gi * 512), ["hT", "wB"])
            P.op("act", lambda e: e.activation(out=tg[0:n, gi * 512:(gi + 1) * 512], in_=psb[b][0:n, 0:512], func=AF.Tanh, scale=S(8, n)),
                 [("ps", b), "hrs"], [("tg", bk.par, gi)])

        def pB_stageP(bk):
            n, ti, rs, rrs, hl = bk.n, bk.ti, bk.rs, bk.rrs, bk.hl
            b = proj_bank()
            mm_group(b, 0, 512, n, hl, wcols(512, 1024), ["hT", "wB"])
            P.op("act", lambda e, b=b: e.activation(out=xbf[0:n, 0:512], in_=psb[b][0:n, 0:512], func=AF.Square, accum_out=S(5, n)),
                 [("ps", b)], ["xbf", "rq_ssq"])
            copy(qd16[0:n, :], psb[b][0:n, 0:512], [("ps", b)], ["qd16"], eng="dve")
            P.op("dve", lambda e: e.tensor_scalar(out=S(22, n), in0=rs, scalar1=rs, scalar2=1.0 / 512, op0=ALU.mult, op1=ALU.mult), rrs, ["tmp22"])
            P.op("dve", lambda e: e.tensor_scalar(out=S(6, n), in0=S(5, n), scalar1=S(22, n), scalar2=EPS, op0=ALU.mult, op1=ALU.add),
                 ["rq_ssq", "tmp22"], ["sq"])
            P.op("pool", lambda e: e.tensor_tensor(out=S(6, n), in0=S(6, n), in1=mhalf[0:n], op=ALU.pow), ["sq", "consts"], ["sq"])
            P.op("dve", lambda e: e.tensor_scalar(out=S(6, n), in0=S(6, n), scalar1=rs, scalar2=MLA_SCALE, op0=ALU.mult, op1=ALU.mult),
                 ["sq"] + rrs, ["sq"])

            yield
            if not bk.gates_done:
                for gi in range(4):
                    pB_gate(bk, gi, proj_bank())
            yield
            b = proj_bank()
            transposes([qd16[0:n, c * 128:(c + 1) * 128] for c in range(4)], n, b, ["qd16"])
            copy(qdT[:, :].rearrange("p (c t) -> p c t", c=4)[:, :, 0:n], bfv(b)[:, 0:512].rearrange("p (c t) -> p c t", c=4)[:, :, 0:n],
                 [("ps", b)], ["qdT"], eng="dve")
            yield
            ql = [qdT[:, c * 128:c * 128 + n] for c in range(4)]
            for half in range(2):
                b = proj_bank()
                mm_group(b, 0, 512, n, ql, [Wabs[:, c, half * 512:(half + 1) * 512] for c in range(4)], ["qdT", "Wabs"])
                scaled(qlat16[0:n, half * 512:(half + 1) * 512], psb[b][0:n, 0:512], S(6, n), [("ps", b), "sq"], ["qlat16"], ("act", "dve")[half])
            yield
            b = proj_bank()
            mm_group(b, 0, 256, n, ql, [wpe[:, c, :] for c in range(4)], ["qdT", "wpe"])
            rope(psb[b][0:n, 0:256], S(6, n), ti, 4, n, qpe16[0:n, :].rearrange("p (h e) -> p h e", h=4)[:, :, 0:64], [("ps", b), "sq"], ["qpe16"], False)
            yield
            v3 = lambda ap: ap.rearrange("p (h t) -> p h t", h=4)[:, :, 0:n]
            b = proj_bank()
            transposes([qa16[0:n, h * 128:(h + 1) * 128] for h in range(4)], n, b, ["qa16"])
            copy(Q12z[0:64, :, 0, 0:n], v3(bfv(b)[0:64, 0:512]), [("ps", b)], ["Q12z"], eng="dve")
            copy(Q12z[64:128, :, 1, 0:n], v3(bfv(b)[64:128, 0:512]), [("ps", b)], ["Q12z"], eng="dve")
            yield
            for zi in range(2):
                b = proj_bank()
                mm_group(b, 0, 512, n, hl, wcols(1024 + zi * 512, 1536 + zi * 512), ["hT", "wB"])
                P.op("act", lambda e, b=b: e.activation(out=xbf[0:n, 512:1024], in_=psb[b][0:n, 0:512], func=AF.Tanh, scale=S(8, n)),
                     [("ps", b), "hrs"], ["xbf"])
                P.op("dve", lambda e, b=b, zi=zi: e.scalar_tensor_tensor(out=zz[0:n, zi * 512:(zi + 1) * 512], in0=xbf[0:n, 512:1024], scalar=1.0,
                                                                        in1=psb[b][0:n, 0:512], op0=ALU.add, op1=ALU.mult),
                     [("ps", b), "xbf"], [("zz", zi)])
                if zi == 0:
                    P.op("dve", lambda e: e.tensor_tensor(out=zz[0:n, 0:512].rearrange("p (h e) -> p h e", h=4), in0=zz[0:n, 0:512].rearrange("p (h e) -> p h e", h=4),
                                                         in1=gsub8[0:n, :].unsqueeze(1).broadcast_to([n, 4, 128]), op=ALU.mult), [("zz", 0), "gsub8"], [("zz", 0)])
            yield
            b = proj_bank()
            transposes([qlat16[0:n, c * 128:(c + 1) * 128] for c in range(8)], n, b, ["qlat16"])
            copy(QlT[:, :].rearrange("p (c t) -> p c t", c=8)[:, :, 0:n], bfv(b)[:, :].rearrange("p (c t) -> p c t", c=8)[:, :, 0:n],
                 [("ps", b)], [("QlT", h_) for h_ in range(4)], eng="dve")
            yield
            b = proj_bank()
            transposes([qpe16[0:n, h * 128:(h + 1) * 128] for h in range(4)], n, b, ["qpe16"])
            copy(v3(qpeT[:, :]), v3(bfv(b)[:, 0:512]), [("ps", b)], ["qpeT"], eng="act")
            yield

        def pB_attention(bk, nxt):
            n, ti, rs, rrs, ktiles, diag = bk.n, bk.ti, bk.rs, bk.rrs, bk.ktiles, bk.diag
            nj = len(ktiles)
            tiles = [(h, jj, kt, nk) for h in range(4) for jj, (kt, nk) in enumerate(ktiles)]
            st = {}
            pending = []

            def emit_S(t):
                h, jj, kt, nk = tiles[t]
                sb = s_bank()
                pb = cnt["pt"] % 3
                cnt["pt"] += 1
                pt = PT[pb]
                ks = slice(kt * 128, kt * 128 + nk)
                qs = slice(h * 128, h * 128 + n)
                if n == 128:
                    P.op("pe", lambda e: e.matmul(psb[sb][0:nk, 0:256], lhsT=KT[:, h, ks], rhs=Q12z[:, h, :, :].rearrange("p a q -> p (a q)"),
                                                  start=True, stop=True), ["KT", "Q12z"], [("ps", sb)])
                else:
                    mm_group(sb, 0, n, nk, [KT[:, h, ks]], [Q12z[:, h, 0, 0:n]], ["KT", "Q12z"])
                    mm_group(sb, 128, n, nk, [KT[:, h, ks]], [Q12z[:, h, 1, 0:n]], ["KT", "Q12z"])
                mm_group(sb, 256, n, nk, [ckvT[:, 0, ks], ckvT[:, 1, ks], kpeT[:, ks]],
                         [QlT[:, (2 * h) * 128:(2 * h) * 128 + n], QlT[:, (2 * h + 1) * 128:(2 * h + 1) * 128 + n], qpeT[:, qs]],
                         ["ckvT", "kpeT", ("QlT", h), "qpeT"])
                P.op("act", lambda e: e.activation(
                    out=pt[0:nk, :].rearrange("p (a q) -> p a q", a=3)[:, :, 0:n],
                    in_=psb[sb][0:nk, 0:384].rearrange("p (a q) -> p a q", a=3)[:, :, 0:n], func=AF.Exp),
                    [("ps", sb)], [("PT", pb)])
                if diag and jj == nj - 1:
                    P.op("pool", lambda e: e.memset(pt[64:128, :].rearrange("p (a q) -> p a q", a=3)[:, :, 0:64], 0.0),
                         [("PT", pb)], [("PT", pb)])
                st[t] = (pt, pb)

            def emit_PV(t):
                h, jj, kt, nk = tiles[t]
                pt, pb = st.pop(t)
                if jj == 0:
                    cnt["acc"] += 1
                aset = cnt["acc"] % 2
                b12, bm = 4 + 2 * aset, 5 + 2 * aset
                first, last = (jj == 0), (jj == nj - 1)
                va = Vaug[0:nk, kt, h, 0:129]
                P.op("pe", lambda e: e.matmul(psb[b12][0:n, 0:129], lhsT=pt[0:nk, 0:n], rhs=va, start=first, stop=last),
                     [("PT", pb), "Vaug"], [("ps", b12)])
                P.op("pe", lambda e: e.matmul(psb[b12][0:n, 129:258], lhsT=pt[0:nk, 128:128 + n], rhs=va, start=False, stop=last, skip_group_check=True),
                     [("PT", pb), "Vaug"], [("ps", b12)])
                P.op("pe", lambda e: e.matmul(psb[bm][0:n, 0:257], lhsT=pt[0:nk, 256:256 + n], rhs=ckvaug[0:nk, kt, 0:257], start=first, stop=last),
                     [("PT", pb), "ckvaug"], [("ps", bm)])
                if last:
                    epilogue(h, b12, bm)
                    pending.append((t + 7, lambda h=h: head_post(h)))

            def epilogue(h, b12, bm):
                A = psb[b12]
                P.op("dve", lambda e: e.reciprocal(out=S(17, n), in_=A[0:n, 128:129]), [("ps", b12)], ["r1"])
                P.op("dve", lambda e: e.reciprocal(out=S(18, n), in_=A[0:n, 257:258]), [("ps", b12)], ["r2"])
                P.op("dve", lambda e: e.tensor_tensor(out=S(19, n), in0=S(18, n), in1=nlam[0:n], op=ALU.mult), ["r2", "nlam"], ["nl2"])
                P.op("dve", lambda e: e.tensor_scalar(out=oh32[0:n, :], in0=A[0:n, 0:128], scalar1=S(17, n), scalar2=None, op0=ALU.mult),
                     [("ps", b12), "r1"], ["oh32"])
                P.op("dve", lambda e: e.scalar_tensor_tensor(out=oh32[0:n, :], in0=A[0:n, 129:257], scalar=S(19, n), in1=oh32[0:n, :],
                                                            op0=ALU.mult, op1=ALU.add), [("ps", b12), "nl2", "oh32"], ["oh32"])
                P.op("dve", lambda e: e.reciprocal(out=rmall[0:n, h:h + 1], in_=psb[bm][0:n, 256:257]), [("ps", bm)], ["rm"])
                copy(olat16[0:n, h * 256:(h + 1) * 256], psb[bm][0:n, 0:256], [("ps", bm)], [("olat16", h), "qlat16"], eng="dve")
                P.op("dve", lambda e: e.tensor_tensor(out=rmall[0:n, h:h + 1], in0=rmall[0:n, h:h + 1], in1=rs, op=ALU.mult), ["rm"] + rrs, ["rm"])
                P.op("dve", lambda e: e.scalar_tensor_tensor(out=junk[0:n, 0:128], in0=oh32[0:n, :], scalar=1.0, in1=oh32[0:n, :],
                                                            op0=ALU.mult, op1=ALU.mult, accum_out=S(9, n)), ["oh32"], ["junk", "ro_ssq"])
                rsqrt_chain(S(9), 1.0 / 128, S(13), "ro", n)
                P.op("dve", lambda e: e.tensor_tensor(out=S(13, n), in0=S(13, n), in1=rs, op=ALU.mult), ["ro"] + rrs, ["ro"])
                P.op("dve", lambda e: e.scalar_tensor_tensor(out=oa16[0:n, h * 128:(h + 1) * 128], in0=oh32[0:n, :], scalar=S(13, n), in1=zz[0:n, h * 128:(h + 1) * 128],
                                                            op0=ALU.mult, op1=ALU.mult), ["oh32", "ro", ("zz", 0)], [("oa16", h), "qd16"])

            def head_post(h):
                b = proj_bank()
                transposes([oa16[0:n, h * 128:(h + 1) * 128], olat16[0:n, (2 * h) * 128:(2 * h + 1) * 128], olat16[0:n, (2 * h + 1) * 128:(2 * h + 2) * 128]],
                           n, b, [("oa16", h), ("olat16", h), "qd16", "qlat16"])
                copy(oaT[:, h * 128:h * 128 + n], bfv(b)[:, 0:n], [("ps", b)], [("oaT", h), "qdT"], eng="dve")
                copy(olatT[:, (2 * h) * 128:(2 * h + 2) * 128].rearrange("p (c t) -> p c t", c=2)[:, :, 0:n],
                     bfv(b)[:, 128:384].rearrange("p (c t) -> p c t", c=2)[:, :, 0:n], [("ps", b)], [("olatT", h), ("QlT", h)], eng="dve")
                b = proj_bank()
                mm_group(b, 0, 128, n, [olatT[:, (2 * h + c) * 128:(2 * h + c) * 128 + n] for c in range(2)],
                         [wuv[:, c, h * 128:(h + 1) * 128] for c in range(2)], [("olatT", h), ("QlT", h), "wuv"])
                P.op("dve", lambda e, b=b: e.scalar_tensor_tensor(out=ob16[0:n, h * 128:(h + 1) * 128], in0=psb[b][0:n, 0:128],
                                                                 scalar=rmall[0:n, h:h + 1], in1=zz[0:n, 512 + h * 128:512 + (h + 1) * 128],
                                                                 op0=ALU.mult, op1=ALU.mult), [("ps", b), "rm", ("zz", 1)], [("ob16", h)])

            cnt["attn"] = 1
            emit_S(0)
            if len(tiles) > 1:
                emit_S(1)
            tA = (len(tiles) * 5) // 8
            for t in range(len(tiles)):
                if t + 2 < len(tiles):
                    emit_S(t + 2)
                emit_PV(t)
                if t == tA and nxt is not None:
                    pB_prologueA(nxt)
                for (at, fn) in [p for p in pending if p[0] <= t]:
                    fn()
                pending[:] = [p for p in pending if p[0] > t]
            bk.pending = [fn for (_, fn) in pending]
            cnt["attn"] = 0

        def obank():
            k = cnt["ob"] % 4
            cnt["ob"] += 1
            return k

        def pB_stageO(bk, nxt):
            n, ti = bk.n, bk.ti
            xb = bk.xb
            if nxt is not None:
                pB_prologueB(nxt)
            for fn in bk.pending:
                fn()
            oaS = [("oaT", h) for h in range(4)]
            junk32 = junk[:, :].bitcast(F32)
            tg = tgs[bk.par]
            tmps = ((ropeA, ["ropeA"], ropeB, ["ropeB"]), (junk32, ["junk", "junk2"], ropeA, ["ropeA"]))

            def ngate(gi):
                if nxt is not None:
                    pB_gate(nxt, gi, obank())
                    nxt.gates_done = True
            for half in range(2):
                cs = slice(half * 512, (half + 1) * 512)
                b = obank()
                mm_group(b, 0, 512, n, [oaT[:, c * 128:c * 128 + n] for c in range(4)], [woa[:, c, cs] for c in range(4)], oaS + ["qdT", "woa"])
                ta, tak, tb, tbk = tmps[half]
                P.op("dve", lambda e, b=b, cs=cs, ta=ta: e.scalar_tensor_tensor(out=ta[0:n, :], in0=tg[0:n, cs], scalar=1.0, in1=psb[b][0:n, 0:512],
                                                                               op0=ALU.add, op1=ALU.mult), [("ps", b), ("tg", bk.par, half)], tak)
                ngate(half)
                if half == 0:
                    b = obank()
                    transposes([ob16[0:n, c * 128:(c + 1) * 128] for c in range(4)], n, b, [("ob16", h) for h in range(4)])
                    copy(obT[:, :].rearrange("p (c t) -> p c t", c=4)[:, :, 0:n], bfv(b)[:, 0:512].rearrange("p (c t) -> p c t", c=4)[:, :, 0:n],
                         [("ps", b)], ["obT"], eng="act")
            yield
            for half in range(2):
                cs = slice(half * 512, (half + 1) * 512)
                ta, tak, tb, tbk = tmps[half]
                b = obank()
                mm_group(b, 0, 512, n, [obT[:, c * 128:c * 128 + n] for c in range(4)], [wob[:, c, cs] for c in range(4)], ["obT", "wob"])
                P.op("dve", lambda e, b=b, half=half, tb=tb: e.scalar_tensor_tensor(out=tb[0:n, :], in0=tg[0:n, 1024 + half * 512:1536 + half * 512], scalar=1.0,
                                                                                   in1=psb[b][0:n, 0:512], op0=ALU.add, op1=ALU.mult),
                     [("ps", b), ("tg", bk.par, 2 + half)], tbk)
                P.op("dve", lambda e, cs=cs, ta=ta, tb=tb: e.tensor_tensor(out=m16[0:n, cs], in0=ta[0:n, :], in1=tb[0:n, :], op=ALU.add), tak + tbk, [("m16", half)])
                ngate(2 + half)
                b = obank()
                transposes([m16[0:n, c * 128:(c + 1) * 128] for c in range(4 * half, 4 * half + 4)], n, b, [("m16", half)])
                copy(mT[:, half * 512:(half + 1) * 512].rearrange("p (c t) -> p c t", c=4)[:, :, 0:n], bfv(b)[:, 0:512].rearrange("p (c t) -> p c t", c=4)[:, :, 0:n],
                     [("ps", b)], [("mT", half)], eng="act")
                yield
            xa = x32[xb]
            for half in range(2):
                cs = slice(half * 512, (half + 1) * 512)
                b = obank()
                mm_group(b, 0, 512, n, [mT[:, c * 128:c * 128 + n] for c in range(8)], [wout[:, c, cs] for c in range(8)], [("mT", 0), ("mT", 1), "wout"])
                P.op("dve", lambda e, b=b, cs=cs: e.tensor_tensor(out=xa[0:n, cs], in0=psb[b][0:n, 0:512], in1=xa[0:n, cs], op=ALU.add),
                     [("ps", b), ("x32", xb)], [("x32", xb)])
                yield
            P.op("act", lambda e: e.activation(out=junk[0:n, :], in_=xa[0:n, :], func=AF.Square, accum_out=S(20, n)),
                 [("x32", xb)], ["junk", "junk2", "rf_ssq"])
            rsqrt_chain(S(20), 1.0 / D, S(21), "rf", n)
            P.op("dve", lambda e: e.scalar_tensor_tensor(out=xa[0:n, :], in0=xa[0:n, :], scalar=S(21, n), in1=gfin[0:n, :], op0=ALU.mult, op1=ALU.mult),
                 [("x32", xb), "rf", "gfin"], [("x32", xb)])
            dma(bk.ydst, xa[0:n, :], [("x32", xb)], [], ("yo", xb))
            yield

        def passB_seq(blocks, pre=None):
            bks = []
            for (xsrc, n, ti, ktiles, diag, ydst) in blocks:
                bk = Blk()
                bk.xsrc, bk.n, bk.ti, bk.ktiles, bk.diag, bk.ydst = xsrc, n, ti, ktiles, diag, ydst
                bk.par = len(bks) % 2
                bk.gates_done = False
                bks.append(bk)
            if pre is None:
                pB_load(bks[0])
            else:
                bks[0].xb = pre
            pB_prologueA(bks[0])
            pB_prologueB(bks[0])
            for _ in pB_stageP(bks[0]):
                pass
            for i, bk in enumerate(bks):
                nxt = bks[i + 1] if i + 1 < len(bks) else None
                if nxt is not None:
                    pB_load(nxt)
                pB_attention(bk, nxt)
                gO = pB_stageO(bk, nxt)
                next(gO)
                gP = pB_stageP(nxt) if nxt is not None else iter(())
                doneO = doneP = False
                while not (doneO and doneP):
                    if not doneP:
                        try:
                            next(gP)
                        except StopIteration:
                            doneP = True
                    if not doneO:
                        try:
                            next(gO)
                        except StopIteration:
                            doneO = True

        def load_wkv():
            for c in range(8):
                dma(wkv[:, c, :], wkv_scr[c], ["wkv_scr"], ["wkv"], "wkvl")

        if stop <= 1:
            dma(x32[0], xp[0, 0:128, :], [], [("x32", 0)], ("x", 0))
        for s in range(nseq if stop >= 2 else 0):
            preA = {j: load_x(xp[s, j * 128:(j + 1) * 128, :], 128) for j in range(min(2, nblk))}
            load_wkv()
            zero_kpe()
            passA_seq(pre=preA, items=[(xp[s, i * 128:(i + 1) * 128, :], 128, i, i,
                        (kp[s, i * 128:(i + 1) * 128, :], vp[s, i * 128:(i + 1) * 128, :], cpo[s, i * 128:(i + 1) * 128, :], ppo[s, i * 128:(i + 1) * 128, :]))
                       for i in range(nblk)])
            P.barrier_all()
            zero_q()
            if stop >= 3:
                passB_seq([(xp[s, i * 128:(i + 1) * 128, :], 128, i, [(j, 128) for j in range(i + 1)], True, yp[s, i * 128:(i + 1) * 128, :])
                           for i in range(nblk)])
            P.barrier_all()
        if with_sample:
            for s in range(nseq):
                xa_ = load_x(xs[s], DEC)
                xb_ = load_x(xs[s], DEC)
                load_wkv()
                zero_kpe()
                passA_seq([(xs[s], DEC, 16, 16, (kso[s], vso[s], cso[s], pso[s]))], pre={0: xa_})
                P.barrier_all()
                cache_seq(s)
                P.barrier_all()
                zero_q()
                passB_seq([(xs[s], DEC, 16, [(j, 128) for j in range(16)] + [(16, DEC)], False, ys[s])], pre=xb_)
                P.barrier_all()

        P.finalize()
        sems = {e: es.enter_context(nc.semaphore(f"s_{e}")) for e in ENGS}
        dsems = {k: es.enter_context(nc.semaphore("d_" + "".join(ch for ch in str(k) if ch.isalnum()))) for k in P.dma_keys}
        run = P.emit(sems, dsems)
        with nc.Block() as block:
            @block.tensor
            def _(e):
                run("pe", e)

            @block.scalar
            def _(e):
                run("act", e)

            @block.vector
            def _(e):
                run("dve", e)

            @block.gpsimd
            def _(e):
                run("pool", e)

            @block.sync
            def _(e):
                run("sp", e)
    nc._prog_stats = {e: len(P.eng_ops[e]) for e in ENGS}
    return nc


def _consts():
    import ml_dtypes
    return {"ident": np.eye(128, dtype=np.float32).astype(ml_dtypes.bfloat16), "rope": rope_tables()}


def _weight_map(w_in, w_uq, w_uk, w_uv, w_oa, w_ob, w_out, lambda_q1, lambda_k1, lambda_q2, lambda_k2,
                norm_in, norm_qa, norm_kva, norm_subln, norm_final):
    f = lambda a: np.ascontiguousarray(np.asarray(a, dtype=np.float32))
    m = {
        "w_in": f(w_in[0]), "w_uq": f(w_uq[0]).reshape(512, 768), "w_uk": f(w_uk[0]).reshape(256, 512),
        "w_uv": f(w_uv[0]).reshape(256, 512), "w_oa": f(w_oa[0]), "w_ob": f(w_ob[0]), "w_out": f(w_out[0]),
        "lam4": f(np.stack([lambda_q1[0], lambda_k1[0], lambda_q2[0], lambda_k2[0]], axis=0)),
        "g_inT": f(np.asarray(norm_in[0]).reshape(8, 128).T), "g_qaT": f(np.asarray(norm_qa[0]).reshape(4, 128).T),
        "g_kva": f(norm_kva[0]).reshape(1, 256), "g_sub": f(norm_subln[0]).reshape(1, 128),
        "g_fin": f(norm_final).reshape(1, D),
    }
    m.update(_consts())
    return m


_NC_CACHE = {}


def kernel(x_prompt, x_sample, cache_diff_k, cache_diff_v, cache_mla_ckv, cache_mla_kpe,
           w_in, w_uq, w_uk, w_uv, w_oa, w_ob, w_out,
           lambda_q1, lambda_k1, lambda_q2, lambda_k2,
           norm_in, norm_qa, norm_kva, norm_subln, norm_final):
    NCORE = 8
    B, T, _ = x_prompt.shape
    nseq = B // NCORE
    nblk = T // 128
    key = (nseq, nblk)
    if key not in _NC_CACHE:
        _NC_CACHE[key] = build_program(nseq=nseq, nblk=nblk, with_sample=True)
    nc = _NC_CACHE[key]
    wm = _weight_map(w_in, w_uq, w_uk, w_uv, w_oa, w_ob, w_out, lambda_q1, lambda_k1, lambda_q2, lambda_k2,
                     norm_in, norm_qa, norm_kva, norm_subln, norm_final)
    f = lambda a: np.ascontiguousarray(np.asarray(a, dtype=np.float32))
    in_maps = []
    for c in range(NCORE):
        sl = slice(c * nseq, (c + 1) * nseq)
        m = dict(wm)
        m["xp"] = f(x_prompt[sl])
        m["xs"] = f(x_sample[sl])
        m["ck"] = f(cache_diff_k[0, sl]).reshape(nseq, PAST, 512)
        m["cv"] = f(cache_diff_v[0, sl]).reshape(nseq, PAST, 512)
        m["cc"] = f(cache_mla_ckv[0, sl])
        m["cpe"] = f(cache_mla_kpe[0, sl])
        in_maps.append(m)
    res = run_bass_kernel_spmd(nc, in_maps, core_ids=list(range(NCORE)))
    R = res.results
    cat = lambda k: np.concatenate([np.asarray(r[k], dtype=np.float32) for r in R], axis=0)
    y_p = cat("yp"); y_s = cat("ys")
    k_p = cat("kp").reshape(1, B, T, 4, 128); v_p = cat("vp").reshape(1, B, T, 4, 128)
    c_p = cat("cpo").reshape(1, B, T, 256); p_p = cat("ppo").reshape(1, B, T, 64)
    k_s = cat("ks").reshape(1, B, DEC, 4, 128); v_s = cat("vs").reshape(1, B, DEC, 4, 128)
    c_s = cat("cs").reshape(1, B, DEC, 256); p_s = cat("ps").reshape(1, B, DEC, 64)
    return (y_p, y_s, k_p, v_p, c_p, p_p, k_s, v_s, c_s, p_s)
```
